# Optimizing a Trainium2 kernel written in Bass

```python
import math
import jax, jax.numpy as jnp
from jax import lax
import numpy as np

D_MODEL = 1024
BATCH = 32
SEQ = 256
DEPTH = 4
DEC_BATCH = 8
DEC_SEQ = 4096
PAST_LEN = 512

GRID_W = 64
N_EVEN = (DEPTH + 1) // 2
N_ODD = DEPTH // 2
CONV_CH = D_MODEL // 2
FOURIER_CH = D_MODEL - CONV_CH
FOURIER_GROUPS = 4
FOURIER_GROUP_CH = FOURIER_CH // FOURIER_GROUPS
SSM_GROUP_CH = 16
SSM_GROUPS = D_MODEL // SSM_GROUP_CH
SSM_STATE = 64
D_FF = ((8 * D_MODEL + 3 * 256 - 1) // (3 * 256)) * 256
N_MOD = 6
RMS_EPS = 1e-6
LAMBDA_RE_MAX = -1e-4

kernel_name = 'bidir_conv_fourier_s5_prefix_dit'


def rmsnorm(x, g):
    xf = x.astype(jnp.float32)
    y = xf * lax.rsqrt(jnp.mean(xf * xf, axis=-1, keepdims=True) + RMS_EPS)
    return (y * g.astype(jnp.float32)).astype(x.dtype)


def modulate(h, shift, scale):
    return h * (1 + scale) + shift


def short_conv(h, w, rows):
    b, L, ch = h.shape
    hr = h.reshape(b, rows, L // rows, ch)
    p = jnp.pad(hr, ((0, 0), (0, 0), (1, 1), (0, 0)))
    out = w[0] * p[:, :, :-2] + w[1] * p[:, :, 1:-1] + w[2] * p[:, :, 2:]
    return out.reshape(b, L, ch)


def conv_fourier_mixer(h, w_in, w_conv, w_out, rows):
    b, L, _ = h.shape
    proj = h @ w_in
    g_pre, g_post, v, f = jnp.split(proj, [CONV_CH, 2 * CONV_CH, 3 * CONV_CH], axis=-1)
    y_conv = g_post * short_conv(g_pre * v, w_conv, rows)
    fg = f.astype(jnp.float32).reshape(b, L, FOURIER_GROUPS, FOURIER_GROUP_CH)
    y_four = jnp.fft.fft2(fg, axes=(1, 3), norm='ortho').real.reshape(b, L, FOURIER_CH).astype(h.dtype)
    return jnp.concatenate([y_conv, y_four], axis=-1) @ w_out


def _ssm_combine(e_i, e_j):
    a_i, b_i = e_i
    a_j, b_j = e_j
    return a_j * a_i, a_j * b_i + b_j


def ssm_scan(u, lam_bar, b_bar, c_mat, h0, reverse):
    L = u.shape[1]
    bu = jnp.einsum('blgh,gph->lbgp', u, b_bar)
    first = L - 1 if reverse else 0
    bu = bu.at[first].add(lam_bar * h0)
    a = jnp.broadcast_to(lam_bar, (L, 1) + lam_bar.shape)
    _, xs = lax.associative_scan(_ssm_combine, (a, bu), reverse=reverse, axis=0)
    final = xs[0] if reverse else xs[L - 1]
    y = jnp.einsum('lbgp,ghp->blgh', xs, c_mat).real
    return y, final


def s5_mixer(h, lam_re, lam_im, log_step, b_re, b_im, c_re, c_im, d_skip, w_glu, h0):
    b, L, _ = h.shape
    f32 = jnp.float32
    u = h.astype(f32).reshape(b, L, SSM_GROUPS, SSM_GROUP_CH)
    lam = jnp.minimum(lam_re.astype(f32), LAMBDA_RE_MAX) + 1j * lam_im.astype(f32)
    step = jnp.exp(log_step.astype(f32))[..., None]
    lam_bar = jnp.exp(lam * step)
    b_bar = ((lam_bar - 1.0) / lam)[..., None] * (b_re.astype(f32) + 1j * b_im.astype(f32))
    c_mat = c_re.astype(f32) + 1j * c_im.astype(f32)
    h0f = h0.astype(f32)
    h0c = h0f[..., 0] + 1j * h0f[..., 1]
    y_fwd, fin_fwd = ssm_scan(u, lam_bar[0], b_bar[0], c_mat[0], h0c[:, 0], False)
    y_bwd, fin_bwd = ssm_scan(u, lam_bar[1], b_bar[1], c_mat[1], h0c[:, 1], True)
    y = y_fwd + y_bwd + d_skip.astype(f32).reshape(SSM_GROUPS, SSM_GROUP_CH) * u
    z = jax.nn.gelu(y.reshape(b, L, D_MODEL)).astype(h.dtype)
    a, g = jnp.split(z @ w_glu, 2, axis=-1)
    fin = jnp.stack([fin_fwd, fin_bwd], axis=1)
    state = jnp.stack([fin.real, fin.imag], axis=-1)
    return a * jax.nn.sigmoid(g), state


def swiglu(h, w_in, w_out):
    gate, up = jnp.split(h @ w_in, 2, axis=-1)
    return (jax.nn.silu(gate) * up) @ w_out


def run_trunk(x, c_in, rows, h0_all, p):
    finals = []
    for l in range(DEPTH):
        mod = jax.nn.silu(c_in) @ p['w_ada'][l] + p['b_ada'][l]
        sh1, sc1, g1, sh2, sc2, g2 = jnp.split(mod[:, None, :], N_MOD, axis=-1)
        h = modulate(rmsnorm(x, p['norm1_g'][l]), sh1, sc1)
        i = l // 2
        if l % 2 == 0:
            m = conv_fourier_mixer(h, p['w_in_mix'][i], p['w_conv'][i], p['w_out_mix'][i], rows)
        else:
            m, fin = s5_mixer(h, p['ssm_lambda_re'][i], p['ssm_lambda_im'][i], p['ssm_log_step'][i],
                              p['ssm_b_re'][i], p['ssm_b_im'][i], p['ssm_c_re'][i], p['ssm_c_im'][i],
                              p['ssm_d'][i], p['w_glu'][i], h0_all[:, i])
            finals.append(fin)
        x = x + g1 * m
        h = modulate(rmsnorm(x, p['norm2_g'][l]), sh2, sc2)
        x = x + g2 * swiglu(h, p['w_ffn_in'][l], p['w_ffn_out'][l])
    return rmsnorm(x, p['final_g']), jnp.stack(finals, axis=1)


def setup_inputs(seed: int = 0) -> dict:
    key = jax.random.key(seed)
    ks = jax.random.split(key, 24)
    f32 = jnp.float32

    def nrm(k, shape, s):
        return s * jax.random.normal(k, shape, f32)

    G, P, Hg = SSM_GROUPS, SSM_STATE, SSM_GROUP_CH
    lam_shape = (N_ODD, 2, G, P)
    return {
        'x_prompt': nrm(ks[0], (BATCH, SEQ, D_MODEL), 1.0),
        'x_sample': nrm(ks[1], (DEC_BATCH, DEC_SEQ, D_MODEL), 1.0),
        'state_ssm': nrm(ks[2], (DEC_BATCH, N_ODD, 2, G, P, 2), 0.5),
        'c': nrm(ks[3], (DEC_BATCH, D_MODEL), 1.0),
        'c_ctx': nrm(ks[4], (D_MODEL,), 1.0),
        'w_ada': nrm(ks[5], (DEPTH, D_MODEL, N_MOD * D_MODEL), 0.5 * D_MODEL ** -0.5),
        'b_ada': nrm(ks[6], (DEPTH, N_MOD * D_MODEL), 0.02),
        'norm1_g': 1.0 + nrm(ks[7], (DEPTH, D_MODEL), 0.02),
        'norm2_g': 1.0 + nrm(ks[8], (DEPTH, D_MODEL), 0.02),
        'final_g': 1.0 + nrm(ks[9], (D_MODEL,), 0.02),
        'w_in_mix': nrm(ks[10], (N_EVEN, D_MODEL, 3 * CONV_CH + FOURIER_CH), D_MODEL ** -0.5),
        'w_conv': nrm(ks[11], (N_EVEN, 3, CONV_CH), 3 ** -0.5),
        'w_out_mix': nrm(ks[12], (N_EVEN, CONV_CH + FOURIER_CH, D_MODEL), (CONV_CH + FOURIER_CH) ** -0.5),
        'ssm_lambda_re': -0.5 + nrm(ks[13], lam_shape, 0.01),
        'ssm_lambda_im': math.pi * jnp.arange(P, dtype=f32) + nrm(ks[14], lam_shape, 0.01),
        'ssm_log_step': jax.random.uniform(ks[15], (N_ODD, 2, G), f32, math.log(1e-3), math.log(1e-1)),
        'ssm_b_re': nrm(ks[16], (N_ODD, 2, G, P, Hg), (2 * Hg) ** -0.5),
        'ssm_b_im': nrm(ks[17], (N_ODD, 2, G, P, Hg), (2 * Hg) ** -0.5),
        'ssm_c_re': nrm(ks[18], (N_ODD, 2, G, Hg, P), (2 * P) ** -0.5),
        'ssm_c_im': nrm(ks[19], (N_ODD, 2, G, Hg, P), (2 * P) ** -0.5),
        'ssm_d': nrm(ks[20], (N_ODD, D_MODEL), 1.0),
        'w_glu': nrm(ks[21], (N_ODD, D_MODEL, 2 * D_MODEL), D_MODEL ** -0.5),
        'w_ffn_in': nrm(ks[22], (DEPTH, D_MODEL, 2 * D_FF), D_MODEL ** -0.5),
        'w_ffn_out': nrm(ks[23], (DEPTH, D_FF, D_MODEL), D_FF ** -0.5),
    }


def reference(x_prompt, x_sample, state_ssm, c, c_ctx, w_ada, b_ada, norm1_g, norm2_g, final_g,
              w_in_mix, w_conv, w_out_mix, ssm_lambda_re, ssm_lambda_im, ssm_log_step,
              ssm_b_re, ssm_b_im, ssm_c_re, ssm_c_im, ssm_d, w_glu, w_ffn_in, w_ffn_out):
    p = dict(w_ada=w_ada, b_ada=b_ada, norm1_g=norm1_g, norm2_g=norm2_g, final_g=final_g,
             w_in_mix=w_in_mix, w_conv=w_conv, w_out_mix=w_out_mix,
             ssm_lambda_re=ssm_lambda_re, ssm_lambda_im=ssm_lambda_im, ssm_log_step=ssm_log_step,
             ssm_b_re=ssm_b_re, ssm_b_im=ssm_b_im, ssm_c_re=ssm_c_re, ssm_c_im=ssm_c_im,
             ssm_d=ssm_d, w_glu=w_glu, w_ffn_in=w_ffn_in, w_ffn_out=w_ffn_out)
    h0_ctx = jnp.zeros((x_prompt.shape[0], N_ODD, 2, SSM_GROUPS, SSM_STATE, 2), jnp.float32)
    y_prompt, new_state_ssm = run_trunk(x_prompt, c_ctx[None, :], 1, h0_ctx, p)
    rows = x_sample.shape[1] // GRID_W
    y_sample, _ = run_trunk(x_sample, c, rows, state_ssm, p)
    return (y_prompt, y_sample, new_state_ssm)
```

```python
import math
from contextlib import ExitStack
import numpy as np
import ml_dtypes
import concourse.bass as bass
import concourse.mybir as mybir
from concourse.bass_utils import run_bass_kernel_spmd

F32 = mybir.dt.float32
BF16 = mybir.dt.bfloat16
AF = mybir.ActivationFunctionType
ALU = mybir.AluOpType

D = 1024
NT = 5120
TT = 512
NTT = NT // TT
NPT = 2
DFF = 2816
KT = 8
EPS = 1e-6


class Buf:
    __slots__ = ("name", "w", "r", "dsem", "dcnt")

    def __init__(self, name):
        self.name = name
        self.w = None
        self.r = {}
        self.dsem = None
        self.dcnt = 0


class KB:
    def __init__(self, nc, es):
        self.nc = nc
        self.es = es
        self.eng = {"pe": nc.tensor, "act": nc.scalar, "dve": nc.vector,
                    "pool": nc.gpsimd, "sp": nc.sync}
        self.sem = {e: es.enter_context(nc.semaphore("s_" + e)) for e in self.eng}
        self.cnt = {e: 0 for e in self.eng}
        self.seen = {e: {} for e in self.eng}
        self.bar = es.enter_context(nc.semaphore("s_bar"))
        self.barcnt = 0
        self.dsems = []
        self.free_ds = []
        self.bufs = []

    def buf(self, name):
        b = Buf("%s_%d" % (name, len(self.bufs)))
        self.bufs.append(b)
        return b

    def _deps(self, reads, writes):
        deps = {}

        def add(t):
            k, sem, v = t
            if k not in deps or deps[k][1] < v:
                deps[k] = (sem, v)
        for b in reads:
            if b.w is not None:
                add(b.w)
        for b in writes:
            if b.w is not None:
                add(b.w)
            for k, (sem, v) in b.r.items():
                add((k, sem, v))
        return deps

    def _waits(self, e, reads, writes):
        eng = self.eng[e]
        for k, (sem, v) in self._deps(reads, writes).items():
            if e == "pe" and k == "pe":
                continue
            if self.seen[e].get(k, 0) < v:
                eng.wait_ge(sem, v)
                self.seen[e][k] = v

    def _commit(self, tok, reads, writes):
        k, sem, v = tok
        for b in writes:
            b.w = tok
            b.r = {}
        for b in reads:
            if b.r.get(k, (None, 0))[1] < v:
                b.r[k] = (sem, v)

    def op(self, e, fn, reads=(), writes=()):
        self._waits(e, reads, writes)
        ins = fn(self.eng[e])
        self.cnt[e] += 1
        ins.then_inc(self.sem[e], 1)
        self._commit((e, self.sem[e], self.cnt[e]), reads, writes)

    def mm(self, out_ap, pairs, reads, writes):
        self._waits("pe", reads, writes)
        n = len(pairs)
        ins = None
        for i, (l, r) in enumerate(pairs):
            ins = self.nc.tensor.matmul(out_ap, lhsT=l, rhs=r, start=(i == 0), stop=(i == n - 1))
        self.cnt["pe"] += 1
        ins.then_inc(self.sem["pe"], 1)
        self._commit(("pe", self.sem["pe"], self.cnt["pe"]), reads, writes)

    def transpose(self, out_ap, in_ap, ident_ap, reads, writes):
        self._waits("pe", reads, writes)
        ins = self.nc.tensor.transpose(out_ap, in_ap, ident_ap)
        self.cnt["pe"] += 1
        ins.then_inc(self.sem["pe"], 1)
        self._commit(("pe", self.sem["pe"], self.cnt["pe"]), reads, writes)

    def dma(self, q, out_ap, in_ap, reads, writes, anchor, **kw):
        self._waits(q, reads, writes)
        if anchor.dsem is None:
            if self.free_ds:
                anchor.dsem = self.free_ds.pop()
            else:
                ds = [self.es.enter_context(self.nc.semaphore("dsem%d" % len(self.dsems))), 0, len(self.dsems)]
                self.dsems.append(ds)
                anchor.dsem = ds
        ds = anchor.dsem
        ins = self.eng[q].dma_start(out=out_ap, in_=in_ap, **kw)
        ds[1] += 16
        ins.then_inc(ds[0], 16)
        self._commit(("d%d" % ds[2], ds[0], ds[1]), reads, writes)

    def barrier(self):
        sp = self.nc.sync
        for e in self.eng:
            if e != "sp" and self.cnt[e] > 0:
                sp.wait_ge(self.sem[e], self.cnt[e])
        for ds in self.dsems:
            if ds[1] > 0:
                sp.wait_ge(ds[0], ds[1])
        self.barcnt += 1
        sp.sem_inc(self.bar, 1)
        for e in self.eng:
            if e != "sp":
                self.eng[e].wait_ge(self.bar, self.barcnt)
        for b in self.bufs:
            b.w = None
            b.r = {}
            b.dsem = None
        self.free_ds = list(self.dsems)
        for e in self.eng:
            for e2 in self.eng:
                self.seen[e][e2] = self.cnt[e2]
            for ds in self.dsems:
                self.seen[e]["d%d" % ds[2]] = ds[1]


def build(depth=4, with_s5=True, mode=None):
    nc = bass.Bass("TRN2", target_bir_lowering=False)
    dt = nc.dram_tensor

    def din(name, shape, dtype=F32):
        return dt(name, list(shape), dtype, kind="ExternalInput").ap()

    xp = din("xp", [1024, D])
    xs = din("xs", [4096, D])
    st_in = din("st", [2, 2, 64, 64, 2])
    cond = din("cond", [2, D])
    w_ada = din("w_ada", [4, D, 6 * D])
    b_ada = din("b_ada", [4, 6 * D])
    norm1_g = din("norm1_g", [4, D])
    norm2_g = din("norm2_g", [4, D])
    final_g = din("final_g", [D])
    w_in_mix = din("w_in_mix", [2, 8, 128, 8 * 384])
    w_conv = din("w_conv", [2, 3, 512])
    w_out_mix = din("w_out_mix", [2, D, D])
    lam_re = din("ssm_lambda_re", [2, 2, 64, 64])
    lam_im = din("ssm_lambda_im", [2, 2, 64, 64])
    log_step = din("ssm_log_step", [2, 2, 64])
    b_re = din("ssm_b_re", [2, 2, 64, 64, 16])
    b_im = din("ssm_b_im", [2, 2, 64, 64, 16])
    c_re = din("ssm_c_re", [2, 2, 64, 16, 64])
    c_im = din("ssm_c_im", [2, 2, 64, 16, 64])
    ssm_d = din("ssm_d", [2, D])
    w_glu = din("w_glu", [2, D, 2 * D])
    w_ffn_in = din("w_ffn_in", [4, 22, 128, 8 * 256])
    w_ffn_out = din("w_ffn_out", [4, DFF, D])
    ident_d = din("c_ident", [128, 128])
    ones_d = din("c_ones", [128, 128], BF16)
    cs128_d = din("c_cs128", [128, 256], BF16)
    dftp_d = din("c_dftp", [2, 256, 256], BF16)
    dfts_d = din("c_dfts", [16, 128, 2 * 32 * 256], BF16)
    sel_d = din("c_sel", [8, 8, 128, 128], BF16)
    selT_d = din("c_selT", [8, 8, 128, 128], BF16)
    mask_d = din("c_mask", [2, 128, 128])
    TBL = dt("TBL", [2, 64, 7, 128, 128], BF16, kind="Internal").ap()

    yp = dt("yp", [1024, D], F32, kind="ExternalOutput").ap()
    ys = dt("ys", [4096, D], F32, kind="ExternalOutput").ap()
    ns = dt("ns", [4, 2, 2, 64, 64, 2], F32, kind="ExternalOutput").ap()
    if mode is not None:
        dbg_tbl = dt("dbg_tbl", [64, 7, 128, 128], BF16, kind="ExternalOutput").ap()
        dbg_small = dt("dbg_small", [128, 5, 2, 64], F32, kind="ExternalOutput").ap()

    XT = dt("XT", [D, NT], F32, kind="Internal").ap()
    HT = dt("HT", [D, NT], BF16, kind="Internal").ap()
    YC = dt("YC", [D, NT], BF16, kind="Internal").ap()
    HID = dt("HID", [DFF, NT], BF16, kind="Internal").ap()

    XTv = XT.rearrange("(k p) n -> p k n", p=128)
    HTv = HT.rearrange("(k p) n -> p k n", p=128)
    YCv = YC.rearrange("(k p) n -> p k n", p=128)
    HIDv = HID.rearrange("(k p) n -> p k n", p=128)

    _uq = [0]

    def uq(name):
        _uq[0] += 1
        return "%s_%d" % (name, _uq[0])

    with ExitStack() as es:
        kb = KB(nc, es)
        sb = lambda name, shape, dtype: es.enter_context(nc.sbuf_tensor(uq(name), list(shape), dtype))

        ident = sb("ident", [128, 128], F32)
        ones = sb("ones", [128, 128], BF16)
        modT = sb("modT", [128, 4, 48, 2], F32)
        g1t = sb("g1t", [128, 4, 8], F32)
        g2t = sb("g2t", [128, 4, 8], F32)
        gft = sb("gft", [128, 8], F32)
        acoef = sb("acoef", [128, 4, 2, 8, 2], F32)
        epsb = sb("epsb", [128, 1], F32)
        b_const = kb.buf("const")
        b_mod = kb.buf("mod")
        psb = [kb.buf("ps%d" % i) for i in range(8)]
        big = [es.enter_context(nc.psum_tensor("psbig%d" % i, [128, 1024], F32)) for i in range(4)]

        class _V:
            def __init__(self, t, lo):
                self.t, self.lo = t, lo

            def __getitem__(self, idx):
                if not isinstance(idx, tuple):
                    idx = (idx, slice(None))
                p, c = idx[0], idx[1]
                c0 = 0 if c.start is None else c.start
                c1 = 512 if c.stop is None else c.stop
                assert c.step is None
                return self.t[p, self.lo + c0:self.lo + c1]
        pst = [_V(big[i // 2], 512 * (i % 2)) for i in range(8)]

        kb.dma("sp", ident[:], ident_d, [], [b_const], b_const)
        kb.dma("sp", ones[:], ones_d, [], [b_const], b_const)
        kb.op("dve", lambda e: e.memset(epsb[:], EPS), [], [b_const])

        _ADA_PH = []
        if True:
            ph = es.enter_context(ExitStack())
            psb_ = lambda name, shape, dtype: ph.enter_context(nc.sbuf_tensor(uq(name), list(shape), dtype))
            cT = psb_("cT", [128, 2, 8], F32)
            cTb = psb_("cTb", [128, 2, 8], BF16)
            bT = psb_("bT", [128, 4, 48], F32)
            gT = psb_("gT", [128, 2, 4, 8], F32)
            wsl = [psb_("wada%d" % i, [128, 6144], BF16) for i in range(3)]
            b_wsl = [kb.buf("wada%d" % i) for i in range(3)]
            b_c = kb.buf("cT")
            stage = psb_("vstage", [128, 128], F32)
            b_stage = kb.buf("vstage")

            def load_T(dst_ap, src_rows, n):
                kb.dma("sp", stage[0:n, :], src_rows, [], [b_stage], b_stage)
                kb.transpose(pst[1][:, 0:n], stage[0:n, :], ident[0:n, 0:n], [b_stage, b_const], [psb[1]])
                kb.op("dve", lambda e: e.tensor_copy(out=dst_ap, in_=pst[1][:, 0:n]), [psb[1]], [b_c])
            load_T(cT[:].rearrange("p j k -> p (j k)"), cond.rearrange("j (k p) -> (j k) p", p=128), 16)
            b2 = b_ada.rearrange("l (m p) -> (l m) p", p=128)
            load_T(bT[:, 0:2, :].rearrange("p l m -> p (l m)"), b2[0:96, :], 96)
            load_T(bT[:, 2:4, :].rearrange("p l m -> p (l m)"), b2[96:192, :], 96)
            load_T(gT[:, 0].rearrange("p l k -> p (l k)"), norm1_g.rearrange("l (k p) -> (l k) p", p=128), 32)
            load_T(gT[:, 1].rearrange("p l k -> p (l k)"), norm2_g.rearrange("l (k p) -> (l k) p", p=128), 32)
            load_T(gft[:], final_g.rearrange("(k p) -> k p", p=128), 8)
            kb.op("act", lambda e: e.activation(out=cTb[:], in_=cT[:], func=AF.Silu), [b_c], [b_c])
            idx_ = [0]

            def ada_step(l, kt):
                s = idx_[0] % 3
                idx_[0] += 1
                kb.dma("pool", wsl[s][:], w_ada[l, kt * 128:(kt + 1) * 128, :], [], [b_wsl[s]], b_wsl[s], max_dma_last_dim=4096)
                kb._waits("pe", [b_wsl[s], b_c], [psb[0]])
                ins = None
                for m in range(48):
                    ins = nc.tensor.matmul(pst[0][:, 2 * m:2 * m + 2], lhsT=wsl[s][:, m * 128:(m + 1) * 128],
                                           rhs=cTb[:, :, kt], start=(kt == 0 and m == 0), stop=(kt == KT - 1 and m == 47),
                                           skip_group_check=True)
                kb.cnt["pe"] += 1
                ins.then_inc(kb.sem["pe"], 1)
                kb._commit(("pe", kb.sem["pe"], kb.cnt["pe"]), [b_wsl[s], b_c], [psb[0]])

            def ada_epi(l):
                for j in range(2):
                    kb.op("dve", lambda e, l=l, j=j: e.tensor_tensor(
                        out=modT[:, l, :, j], in0=pst[0][:, 0:96].rearrange("p (m j) -> p m j", j=2)[:, :, j],
                        in1=bT[:, l, :], op=ALU.add), [psb[0], b_c], [b_mod])
                for sub in range(2):
                    base = 24 * sub
                    for j in range(2):
                        kb.op("dve", lambda e, l=l, sub=sub, j=j, base=base: e.scalar_tensor_tensor(
                            out=acoef[:, l, sub, :, j], in0=modT[:, l, base + 8:base + 16, j], scalar=1.0,
                            in1=gT[:, sub, l, :], op0=ALU.add, op1=ALU.mult), [b_mod, b_c], [b_mod])

            def ada_layer(l):
                for kt in range(KT):
                    ada_step(l, kt)
                ada_epi(l)
            _ada_q = []
            for l_ in range(1, depth):
                for kt_ in range(KT):
                    _ada_q.append(lambda l_=l_, kt_=kt_: ada_step(l_, kt_))
                _ada_q.append(lambda l_=l_: ada_epi(l_))
            ada_layer(0)
            _ADA_PH.append((ph, _ada_q))

        def mod_ap(l, chunk, ft, j):
            return modT[:, l, chunk * 8 + ft, j:j + 1]

        class Fin:
            pass

        def make_fin(ph, final=False):
            f = Fin()
            psb_ = lambda name, shape, dtype: ph.enter_context(nc.sbuf_tensor(uq(name), list(shape), dtype))
            f.sq = psb_("f_sq", [128, 8, TT], BF16)
            f.rstd = psb_("f_rstd", [128, TT], F32)
            if not final:
                f.tmp = [psb_("f_tmp%d" % i, [128, TT], F32) for i in range(2)]
                f.h = [psb_("f_h%d" % i, [128, 8, TT], BF16) for i in range(2)]
            f.b_sq = kb.buf("f_sq")
            f.b_rstd = kb.buf("f_rstd")
            f.b_tmp = [kb.buf("f_tmp%d" % i) for i in range(2)]
            f.b_h = [kb.buf("f_h%d" % i) for i in range(2)]
            f.n = 0
            return f

        b_XT = kb.buf("XT")
        b_HT = kb.buf("HT")

        def finish(f, tt, xn, b_xn, l_next, sub_next, ps_i, store_x=True, q="sp"):
            j = 0 if tt < NPT else 1
            cols = slice(tt * TT, (tt + 1) * TT)
            if store_x and l_next is not None:
                kb.dma(q, XTv[:, :, cols], xn[:], [b_xn], [], b_xn)
            kb.op("act", lambda e: e.activation(out=f.sq[:], in_=xn[:], func=AF.Square), [b_xn], [f.b_sq])
            kb.mm(pst[ps_i][:], [(ones[:], f.sq[:, k, :]) for k in range(KT)], [f.b_sq, b_const], [psb[ps_i]])
            kb.op("act", lambda e: e.activation(out=f.rstd[:], in_=pst[ps_i][:], func=AF.Sqrt,
                                                bias=epsb[:], scale=1.0 / D), [psb[ps_i], b_const], [f.b_rstd])
            kb.op("dve", lambda e: e.reciprocal(out=f.rstd[:], in_=f.rstd[:]), [f.b_rstd], [f.b_rstd])
            if l_next is None:
                return
            hi = f.n % 2
            f.n += 1
            for k in range(KT):
                ti = k % 2
                kb.op("dve", lambda e, k=k, ti=ti: e.scalar_tensor_tensor(
                    out=f.tmp[ti][:], in0=xn[:, k, :], scalar=acoef[:, l_next, sub_next, k, j:j + 1],
                    in1=f.rstd[:], op0=ALU.mult, op1=ALU.mult), [b_xn, f.b_rstd, b_mod], [f.b_tmp[ti]])
                kb.op("act", lambda e, k=k, ti=ti, hi=hi: e.activation(
                    out=f.h[hi][:, k, :], in_=f.tmp[ti][:], func=AF.Identity,
                    bias=mod_ap(l_next, 3 * sub_next, k, j), scale=1.0), [f.b_tmp[ti], b_mod], [f.b_h[hi]])
            kb.dma(q, HTv[:, :, cols], f.h[hi][:], [f.b_h[hi]], [], f.b_h[hi])

        with ExitStack() as ph:
            psb_ = lambda name, shape, dtype: ph.enter_context(nc.sbuf_tensor(uq(name), list(shape), dtype))
            fin = make_fin(ph)
            xtok = [psb_("xtok%d" % i, [128, 4, D], F32) for i in range(2)]
            b_xtok = [kb.buf("xtok%d" % i) for i in range(2)]
            xn = [psb_("xn%d" % i, [128, 8, TT], F32) for i in range(2)]
            b_xn = [kb.buf("xn%d" % i) for i in range(2)]
            for tt in range(NTT):
                i = tt % 2
                src = xp if tt < NPT else xs
                r0 = tt * TT if tt < NPT else (tt - NPT) * TT
                kb.dma("sp", xtok[i][:], src[r0:r0 + TT, :].rearrange("(a p) d -> p a d", p=128),
                       [], [b_xtok[i]], b_xtok[i])
                for k in range(KT):
                    pi = 1 + (k % 3)
                    for a in range(4):
                        kb.transpose(pst[pi][:, a * 128:(a + 1) * 128], xtok[i][:, a, k * 128:(k + 1) * 128],
                                     ident[:], [b_xtok[i], b_const], [psb[pi]])
                    if k % 2 == 0:
                        kb.op("act", lambda e, k=k, pi=pi, i=i: e.activation(out=xn[i][:, k, :], in_=pst[pi][:], func=AF.Copy),
                              [psb[pi]], [b_xn[i]])
                    else:
                        kb.op("dve", lambda e, k=k, pi=pi, i=i: e.tensor_copy(out=xn[i][:, k, :], in_=pst[pi][:]),
                              [psb[pi]], [b_xn[i]])
                finish(fin, tt, xn[i], b_xn[i], 0, 0, 4 + (tt % 2))
                for _ in range(3):
                    if _ADA_PH[0][1]:
                        _ADA_PH[0][1].pop(0)()
            while _ADA_PH[0][1]:
                _ADA_PH[0][1].pop(0)()
            kb.barrier()
        _ADA_PH[0][0].close()

        def load_h_all(ph, name="hall"):
            hall = ph.enter_context(nc.sbuf_tensor(uq(name), [128, 8, NT], BF16))
            b_hall = [kb.buf("%s%d" % (name, t)) for t in range(NTT)]
            for tt in range(NTT):
                cols = slice(tt * TT, (tt + 1) * TT)
                kb.dma("sp", hall[:, :, cols], HTv[:, :, cols], [], [b_hall[tt]], b_hall[tt])
            return hall, b_hall

        def ffn(l, l_next, sub_next):
            with ExitStack() as pho:
                wo = pho.enter_context(nc.sbuf_tensor(uq("wo"), [128, 22, D], BF16))
                b_wo = kb.buf("wo")
                wov = w_ffn_out[l].rearrange("(k p) n -> p k n", p=128)
                with ExitStack() as ph:
                    psb_ = lambda name, shape, dtype: ph.enter_context(nc.sbuf_tensor(uq(name), list(shape), dtype))
                    hall, b_hall = load_h_all(ph)
                    NS = 3
                    wr = [psb_("wf%d" % i, [128, 8, 256], BF16) for i in range(NS)]
                    b_wr = [kb.buf("wf%d" % i) for i in range(NS)]
                    gs = [psb_("gs%d" % i, [128, TT], F32) for i in range(2)]
                    b_gs = [kb.buf("gs%d" % i) for i in range(2)]
                    ho = [psb_("ho%d" % i, [128, NT], BF16) for i in range(2)]
                    b_ho = [kb.buf("ho%d" % i) for i in range(2)]
                    for m in range(22):
                        s = m % NS
                        kb.dma("pool", wr[s][:].rearrange("p k n -> p (k n)"), w_ffn_in[l, m], [], [b_wr[s]], b_wr[s], max_dma_last_dim=4096)
                        if 3 <= m < 14:
                            k0 = 2 * (m - 3)
                            kb.dma("pool", wo[:, k0:k0 + 2, :], wov[:, k0:k0 + 2, :], [], [b_wo], b_wo)
                        oi = m % 2
                        for tt in range(NTT):
                            cols = slice(tt * TT, (tt + 1) * TT)
                            pg = (2 * tt) % 8
                            pu = pg + 1
                            kb.mm(pst[pg][:], [(wr[s][:, k, 0:128], hall[:, k, cols]) for k in range(KT)],
                                  [b_wr[s], b_hall[tt]], [psb[pg]])
                            kb.mm(pst[pu][:], [(wr[s][:, k, 128:256], hall[:, k, cols]) for k in range(KT)],
                                  [b_wr[s], b_hall[tt]], [psb[pu]])
                            gi = tt % 2
                            kb.op("act", lambda e, pg=pg, gi=gi: e.activation(out=gs[gi][:], in_=pst[pg][:], func=AF.Silu),
                                  [psb[pg]], [b_gs[gi]])
                            kb.op("dve", lambda e, pu=pu, gi=gi, oi=oi, cols=cols: e.tensor_tensor(
                                out=ho[oi][:, cols], in0=pst[pu][:], in1=gs[gi][:], op=ALU.mult),
                                [psb[pu], b_gs[gi]], [b_ho[oi]])
                        kb.dma("sp", HIDv[:, m, :], ho[oi][:], [b_ho[oi]], [], b_ho[oi])
                    kb.barrier()
                with ExitStack() as ph:
                    psb_ = lambda name, shape, dtype: ph.enter_context(nc.sbuf_tensor(uq(name), list(shape), dtype))
                    fin = make_fin(ph, final=(l_next is None))
                    hid = [psb_("hid%d" % i, [128, 22, TT], BF16) for i in range(2)]
                    b_hid = [kb.buf("hid%d" % i) for i in range(2)]
                    NBX = 3
                    xt_ = [psb_("xt%d" % i, [128, 8, TT], F32) for i in range(NBX)]
                    b_xt = [kb.buf("xt%d" % i) for i in range(NBX)]

                    def load(tt):
                        cols = slice(tt * TT, (tt + 1) * TT)
                        kb.dma("sp", hid[tt % 2][:], HIDv[:, :, cols], [], [b_hid[tt % 2]], b_hid[tt % 2])
                        kb.dma("sp", xt_[tt % NBX][:], XTv[:, :, cols], [], [b_xt[tt % NBX]], b_xt[tt % NBX])
                    load(0)
                    for tt in range(NTT):
                        i = tt % 2
                        ix = tt % NBX
                        j = 0 if tt < NPT else 1
                        if tt + 1 < NTT:
                            load(tt + 1)
                        for m in range(KT):
                            pi = m % 6
                            kb.mm(pst[pi][:], [(wo[:, k, m * 128:(m + 1) * 128], hid[i][:, k, :]) for k in range(22)],
                                  [b_wo, b_hid[i]], [psb[pi]])
                            kb.op("dve", lambda e, m=m, pi=pi, ix=ix, j=j: e.scalar_tensor_tensor(
                                out=xt_[ix][:, m, :], in0=pst[pi][:], scalar=mod_ap(l, 5, m, j), in1=xt_[ix][:, m, :],
                                op0=ALU.mult, op1=ALU.add), [psb[pi], b_xt[ix], b_mod], [b_xt[ix]])
                        if l_next is None:
                            final_out(fin, tt, xt_[ix], b_xt[ix], ph)
                        else:
                            finish(fin, tt, xt_[ix], b_xt[ix], l_next, sub_next, 6 + (tt % 2))
                    kb.barrier()

        fo = {}

        def final_out(fin, tt, xn, b_xn, ph):
            if "y" not in fo:
                fo["y"] = [ph.enter_context(nc.sbuf_tensor(uq("fo_y%d" % i), [128, 8, TT], F32)) for i in range(1)]
                fo["b_y"] = [kb.buf("fo_y%d" % i) for i in range(1)]
                fo["o"] = [ph.enter_context(nc.sbuf_tensor(uq("fo_o%d" % i), [128, 4, D], F32)) for i in range(1)]
                fo["b_o"] = [kb.buf("fo_o%d" % i) for i in range(1)]
            finish(fin, tt, xn, b_xn, None, None, 6 + (tt % 2))
            y = fo["y"][0]
            b_y = fo["b_y"][0]
            o = fo["o"][0]
            b_o = fo["b_o"][0]
            for k in range(KT):
                kb.op("dve", lambda e, k=k: e.scalar_tensor_tensor(
                    out=y[:, k, :], in0=xn[:, k, :], scalar=gft[:, k:k + 1], in1=fin.rstd[:],
                    op0=ALU.mult, op1=ALU.mult), [b_xn, fin.b_rstd, b_const], [b_y])
            for a in range(4):
                for k0 in range(0, KT, 4):
                    pi = (a * 2 + k0 // 4) % 6
                    for kk in range(4):
                        k = k0 + kk
                        kb.transpose(pst[pi][:, kk * 128:(kk + 1) * 128], y[:, k, a * 128:(a + 1) * 128], ident[:],
                                     [b_y, b_const], [psb[pi]])
                    if (a + k0 // 4) % 2 == 0:
                        kb.op("act", lambda e, a=a, k0=k0, pi=pi: e.activation(
                            out=o[:, a, k0 * 128:(k0 + 4) * 128], in_=pst[pi][:], func=AF.Copy), [psb[pi]], [b_o])
                    else:
                        kb.op("dve", lambda e, a=a, k0=k0, pi=pi: e.tensor_copy(
                            out=o[:, a, k0 * 128:(k0 + 4) * 128], in_=pst[pi][:]), [psb[pi]], [b_o])
            dst = yp if tt < NPT else ys
            r0 = tt * TT if tt < NPT else (tt - NPT) * TT
            kb.dma("sp", dst[r0:r0 + TT, :].rearrange("(a p) d -> p a d", p=128), o[:], [b_o], [], b_o)

        def even_mixer(l):
            i2 = l // 2
            with ExitStack() as ph:
                psb_ = lambda name, shape, dtype: ph.enter_context(nc.sbuf_tensor(uq(name), list(shape), dtype))
                fall = psb_("fall", [128, 4, NT], BF16)
                b_fall = [kb.buf("fall%d" % g) for g in range(4)]
                with ExitStack() as ph1:
                    p1 = lambda name, shape, dtype: ph1.enter_context(nc.sbuf_tensor(uq(name), list(shape), dtype))
                    hall, b_hall = load_h_all(ph1)
                    NS = 3
                    wr = [p1("wm%d" % i, [128, 8, 384], BF16) for i in range(NS)]
                    b_wr = [kb.buf("wm%d" % i) for i in range(NS)]
                    wc = p1("wc", [128, 4, 3], F32)
                    b_wc = kb.buf("wc")
                    with nc.allow_non_contiguous_dma(reason="tiny conv weight load"):
                        for t_ in range(3):
                            for c_ in range(4):
                                kb.dma("sp", wc[:, c_, t_:t_ + 1], w_conv[i2, t_, c_ * 128:(c_ + 1) * 128].rearrange("(p o) -> p o", o=1),
                                       [], [b_wc], b_wc)
                    vS = [p1("vS%d" % i, [128, TT], F32) for i in range(2)]
                    b_vS = [kb.buf("vS%d" % i) for i in range(2)]
                    tS = [p1("tS%d" % i, [128, TT], F32) for i in range(2)]
                    b_tS = [kb.buf("tS%d" % i) for i in range(2)]
                    cS = [p1("cS%d" % i, [128, TT], F32) for i in range(2)]
                    b_cS = [kb.buf("cS%d" % i) for i in range(2)]
                    yo = [p1("yo%d" % i, [128, NT], BF16) for i in range(2)]
                    b_yo = [kb.buf("yo%d" % i) for i in range(2)]
                    cnt = 0
                    for c in range(4):
                        s = c % NS
                        kb.dma("pool", wr[s][:].rearrange("p k n -> p (k n)"), w_in_mix[i2, c], [], [b_wr[s]], b_wr[s], max_dma_last_dim=4096)
                        oi = c % 2
                        for tt in range(NTT):
                            cols = slice(tt * TT, (tt + 1) * TT)
                            rl = 256 if tt < NPT else 64
                            nr = TT // rl
                            pa = (3 * cnt) % 6
                            cnt += 1
                            for part in range(3):
                                kb.mm(pst[pa + part][:], [(wr[s][:, k, part * 128:(part + 1) * 128], hall[:, k, cols]) for k in range(KT)],
                                      [b_wr[s], b_hall[tt]], [psb[pa + part]])
                            bi = tt % 2
                            kb.op("act", lambda e, pa=pa, bi=bi: e.activation(out=vS[bi][:], in_=pst[pa + 1][:], func=AF.Copy),
                                  [psb[pa + 1]], [b_vS[bi]])
                            kb.op("dve", lambda e, pa=pa, bi=bi: e.tensor_tensor(out=tS[bi][:], in0=pst[pa][:], in1=vS[bi][:], op=ALU.mult),
                                  [psb[pa], b_vS[bi]], [b_tS[bi]])
                            kb.op("act", lambda e, bi=bi, c=c: e.activation(out=cS[bi][:], in_=tS[bi][:], func=AF.Copy,
                                                                          scale=wc[:, c, 1:2]), [b_tS[bi], b_wc], [b_cS[bi]])
                            t3 = tS[bi][:].rearrange("p (r w) -> p r w", w=rl)
                            c3 = cS[bi][:].rearrange("p (r w) -> p r w", w=rl)
                            kb.op("dve", lambda e, t3=t3, c3=c3, c=c, rl=rl: e.scalar_tensor_tensor(
                                out=c3[:, :, 1:rl], in0=t3[:, :, 0:rl - 1], scalar=wc[:, c, 0:1], in1=c3[:, :, 1:rl],
                                op0=ALU.mult, op1=ALU.add), [b_tS[bi], b_cS[bi], b_wc], [b_cS[bi]])
                            kb.op("dve", lambda e, t3=t3, c3=c3, c=c, rl=rl: e.scalar_tensor_tensor(
                                out=c3[:, :, 0:rl - 1], in0=t3[:, :, 1:rl], scalar=wc[:, c, 2:3], in1=c3[:, :, 0:rl - 1],
                                op0=ALU.mult, op1=ALU.add), [b_tS[bi], b_cS[bi], b_wc], [b_cS[bi]])
                            kb.op("dve", lambda e, pa=pa, bi=bi, oi=oi, cols=cols: e.tensor_tensor(
                                out=yo[oi][:, cols], in0=pst[pa + 2][:], in1=cS[bi][:], op=ALU.mult),
                                [psb[pa + 2], b_cS[bi]], [b_yo[oi]])
                        kb.dma("sp", YCv[:, c, :], yo[oi][:], [b_yo[oi]], [], b_yo[oi])
                    for g in range(4):
                        s = (4 + g) % NS
                        kb.dma("pool", wr[s][:].rearrange("p k n -> p (k n)"), w_in_mix[i2, 4 + g], [], [b_wr[s]], b_wr[s], max_dma_last_dim=4096)
                        for tt in range(NTT):
                            cols = slice(tt * TT, (tt + 1) * TT)
                            pa = 6 + (tt % 2)
                            kb.mm(pst[pa][:], [(wr[s][:, k, 0:128], hall[:, k, cols]) for k in range(KT)],
                                  [b_wr[s], b_hall[tt]], [psb[pa]])
                            kb.op("act", lambda e, pa=pa, g=g, cols=cols: e.activation(out=fall[:, g, cols], in_=pst[pa][:], func=AF.Copy),
                                  [psb[pa]], [b_fall[g]])
                    kb.barrier()
                with ExitStack() as ph2:
                    p2 = lambda name, shape, dtype: ph2.enter_context(nc.sbuf_tensor(uq(name), list(shape), dtype))
                    cs128 = p2("cs128", [128, 256], BF16)
                    dftp = p2("dftp", [128, 2, 2, 256], BF16)
                    b_tab = kb.buf("ftab")
                    kb.dma("sp", cs128[:], cs128_d, [], [b_tab], b_tab)
                    for cs_ in range(2):
                        kb.dma("sp", dftp[:, cs_], dftp_d[cs_].rearrange("(a p) k -> p a k", p=128), [], [b_tab], b_tab)
                    ftok = p2("ftok", [128, 32, 4, 256], BF16)
                    b_ftok = [kb.buf("ftok%d" % t) for t in range(32)]
                    def chan_dft(lt0, n):
                        for sl in range(n):
                            lt = lt0 + sl
                            for gp in range(2):
                                pa = (lt * 2 + gp) % 4
                                for gg in range(2):
                                    g = gp * 2 + gg
                                    kb.mm(pst[pa][:, gg * 256:(gg + 1) * 256], [(fall[:, g, lt * 128:(lt + 1) * 128], cs128[:])],
                                          [b_fall[g], b_tab], [psb[pa]])
                                if gp == 0:
                                    kb.op("act", lambda e, pa=pa, sl=sl, gp=gp: e.activation(
                                        out=ftok[:, sl, 2 * gp:2 * gp + 2, :].rearrange("p g c -> p (g c)"), in_=pst[pa][:], func=AF.Copy),
                                        [psb[pa]], [b_ftok[sl]])
                                else:
                                    kb.op("dve", lambda e, pa=pa, sl=sl, gp=gp: e.tensor_copy(
                                        out=ftok[:, sl, 2 * gp:2 * gp + 2, :].rearrange("p g c -> p (g c)"), in_=pst[pa][:]),
                                        [psb[pa]], [b_ftok[sl]])
                    yf = [p2("yf%d" % i, [128, 4, 256], BF16) for i in range(2)]
                    b_yf = [kb.buf("yf%d" % i) for i in range(2)]
                    nyf = 0
                    sc_p = 1.0 / math.sqrt(256 * 128)
                    chan_dft(0, 8)
                    for sq in range(4):
                        yi = nyf % 2
                        nyf += 1
                        for g in range(4):
                            pa = 4 + (g % 4)
                            pairs = []
                            for a in range(2):
                                lt = sq * 2 + a
                                pairs.append((ftok[:, lt, g, 0:128], dftp[:, 0, a, :]))
                                pairs.append((ftok[:, lt, g, 128:256], dftp[:, 1, a, :]))
                            kb.mm(pst[pa][:, 0:256], pairs, [b_ftok[sq * 2], b_ftok[sq * 2 + 1], b_tab], [psb[pa]])
                            kb.op("act", lambda e, pa=pa, g=g, yi=yi: e.activation(out=yf[yi][:, g, :], in_=pst[pa][:, 0:256],
                                                                                func=AF.Copy, scale=sc_p), [psb[pa]], [b_yf[yi]])
                        kb.dma("sp", YCv[:, 4:8, sq * 256:(sq + 1) * 256], yf[yi][:], [b_yf[yi]], [], b_yf[yi])
                    sc_s = 1.0 / math.sqrt(4096 * 128)
                    chan_dft(8, 32)
                    tb = [p2("dft%d" % i, [128, 2, 32, 256], BF16) for i in range(2)]
                    tbfl = [t_[:].rearrange("p c a k -> p (c a k)") for t_ in tb]
                    tbfl.append(fall[:].rearrange("p g n -> p (g n)")[:, 0:16384])
                    tbv = [t_[:] for t_ in tb] + [tbfl[2].rearrange("p (c a k) -> p c a k", c=2, a=32)]
                    b_tb = [kb.buf("dft%d" % i) for i in range(3)]
                    for kbk in range(16):
                        ti = kbk % 3
                        extra = b_fall if kbk == 2 else []
                        for q_ in range(4):
                            kb.dma("sp", tbfl[ti][:, q_ * 4096:(q_ + 1) * 4096], dfts_d[kbk, :, q_ * 4096:(q_ + 1) * 4096], [], [b_tb[ti]] + extra, b_tb[ti])
                        yi = nyf % 2
                        nyf += 1
                        for g in range(4):
                            pa = 4 + (g % 4)
                            pairs = []
                            for a in range(32):
                                pairs.append((ftok[:, a, g, 0:128], tbv[ti][:, 0, a, :]))
                                pairs.append((ftok[:, a, g, 128:256], tbv[ti][:, 1, a, :]))
                            kb.mm(pst[pa][:, 0:256], pairs, b_ftok[0:32] + [b_tb[ti]], [psb[pa]])
                            kb.op("act", lambda e, pa=pa, g=g, yi=yi: e.activation(out=yf[yi][:, g, :], in_=pst[pa][:, 0:256],
                                                                                func=AF.Copy, scale=sc_s), [psb[pa]], [b_yf[yi]])
                        kb.dma("sp", YCv[:, 4:8, 1024 + kbk * 256:1024 + (kbk + 1) * 256], yf[yi][:], [b_yf[yi]], [], b_yf[yi])
                    kb.barrier()
            out_proj(l, w_out_mix[i2], glu=False)

        def out_proj(l, w_ap, glu):
            ncol = 2 * D if glu else D
            with ExitStack() as ph:
                psb_ = lambda name, shape, dtype: ph.enter_context(nc.sbuf_tensor(uq(name), list(shape), dtype))
                fin = make_fin(ph)
                wo = psb_("wom", [128, 8, ncol], BF16)
                b_wo = kb.buf("wom")
                wov = w_ap.rearrange("(k p) n -> p k n", p=128)
                for k0 in range(0, 8, 2):
                    kb.dma("pool", wo[:, k0:k0 + 2, :], wov[:, k0:k0 + 2, :], [], [b_wo], b_wo)
                NB = 3
                yc = [psb_("yc%d" % i, [128, 8, TT], BF16) for i in range(NB)]
                b_yc = [kb.buf("yc%d" % i) for i in range(NB)]
                xt_ = [psb_("xt%d" % i, [128, 8, TT], F32) for i in range(NB)]
                b_xt = [kb.buf("xt%d" % i) for i in range(NB)]
                sg = [psb_("sg%d" % i, [128, TT], F32) for i in range(2)]
                b_sg = [kb.buf("sg%d" % i) for i in range(2)]

                def load(tt):
                    i = tt % NB
                    cols = slice(tt * TT, (tt + 1) * TT)
                    kb.dma("sp", yc[i][:], YCv[:, :, cols], [], [b_yc[i]], b_yc[i])
                    kb.dma("sp", xt_[i][:], XTv[:, :, cols], [], [b_xt[i]], b_xt[i])
                load(0)
                for tt in range(NTT):
                    i = tt % NB
                    cols = slice(tt * TT, (tt + 1) * TT)
                    j = 0 if tt < NPT else 1
                    if tt + 1 < NTT:
                        load(tt + 1)
                    for m in range(KT):
                        pi = (2 * m) % 6
                        kb.mm(pst[pi][:], [(wo[:, k, m * 128:(m + 1) * 128], yc[i][:, k, :]) for k in range(KT)],
                              [b_wo, b_yc[i]], [psb[pi]])
                        if glu:
                            kb.mm(pst[pi + 1][:], [(wo[:, k, D + m * 128:D + (m + 1) * 128], yc[i][:, k, :]) for k in range(KT)],
                                  [b_wo, b_yc[i]], [psb[pi + 1]])
                            si = m % 2
                            kb.op("act", lambda e, pi=pi, si=si: e.activation(out=sg[si][:], in_=pst[pi + 1][:], func=AF.Sigmoid),
                                  [psb[pi + 1]], [b_sg[si]])
                            kb.op("dve", lambda e, pi=pi, si=si: e.tensor_tensor(out=sg[si][:], in0=pst[pi][:], in1=sg[si][:], op=ALU.mult),
                                  [psb[pi], b_sg[si]], [b_sg[si]])
                            kb.op("dve", lambda e, m=m, si=si, i=i, j=j: e.scalar_tensor_tensor(
                                out=xt_[i][:, m, :], in0=sg[si][:], scalar=mod_ap(l, 2, m, j), in1=xt_[i][:, m, :],
                                op0=ALU.mult, op1=ALU.add), [b_sg[si], b_xt[i], b_mod], [b_xt[i]])
                        else:
                            kb.op("dve", lambda e, m=m, pi=pi, i=i, j=j: e.scalar_tensor_tensor(
                                out=xt_[i][:, m, :], in0=pst[pi][:], scalar=mod_ap(l, 2, m, j), in1=xt_[i][:, m, :],
                                op0=ALU.mult, op1=ALU.add), [psb[pi], b_xt[i], b_mod], [b_xt[i]])
                    finish(fin, tt, xt_[i], b_xt[i], l, 1, 6 + (tt % 2))
                kb.barrier()

        I32 = mybir.dt.int32
        r8t = sb("r8t", [128, 2, 64], F32)
        e1r = sb("e1r", [128, 2, 64], F32)
        e1i = sb("e1i", [128, 2, 64], F32)
        h0r = sb("h0r", [128, 2, 64], F32)
        h0i = sb("h0i", [128, 2, 64], F32)
        identb = sb("identb", [128, 128], BF16)
        b_s5p = kb.buf("s5p")
        kb.op("dve", lambda e: e.tensor_copy(out=identb[:], in_=ident[:]), [b_const], [b_s5p])

        def AP3(a, dims):
            return bass.AP(tensor=a.tensor, offset=a.offset, ap=[list(a.ap[0])] + [list(d_) for d_ in dims])

        def bc_last(a, n):
            return AP3(a, [a.ap[1], [0, n]])

        def bc_mid(a, n):
            return AP3(a, [[0, n], a.ap[1]])

        PI = float(np.pi)
        import os as _os
        _STOP = int(_os.environ.get('S5STOP', '0'))

        def s5_setup(li):
            with ExitStack() as ph:
                def T(name, shape, dtype=F32):
                    return ph.enter_context(nc.sbuf_tensor(uq(name), list(shape), dtype))
                B = kb.buf("s5set")
                R, W = [B, b_const, b_s5p], [B]

                def dve(fn):
                    kb.op("dve", fn, R, W)

                def act(fn):
                    kb.op("act", fn, R, W)
                ps1 = pst[1]

                Bpr = T("Bpr", [128, 64, 16]); Bpi = T("Bpi", [128, 64, 16])
                b_in = kb.buf("s5in")
                stgA = T("stgA", [128, 64]); stgB = T("stgB", [128, 64]); stgH = T("stgH", [128, 128])
                cstA = [T("cst%d" % i_, [128, 8, 64]) for i_ in range(4)]
                stD = T("stD", [64, 16])
                msk = T("msk", [128, 2, 128])
                LS = T("LS", [128, 64])
                kb.dma("sp", stgA[:], lam_re[li].rearrange("d g p -> (d g) p"), [], [b_in], b_in)
                kb.dma("sp", stgB[:], lam_im[li].rearrange("d g p -> (d g) p"), [], [b_in], b_in)
                for d_ in range(2):
                    kb.dma("sp", LS[d_ * 64:(d_ + 1) * 64, :], log_step[li, d_, :].partition_broadcast(64), [], [b_in], b_in)
                kb.dma("sp", stgH[:], st_in[li].rearrange("d g p c -> (d g) (p c)"), [], [b_in], b_in)
                kb.dma("sp", stD[:], ssm_d[li].rearrange("(g h) -> g h", h=16), [], [b_in], b_in)
                for m_ in range(2):
                    kb.dma("sp", msk[:, m_, :], mask_d[m_], [], [b_in], b_in)
                for d_ in range(2):
                    sl = slice(d_ * 64, (d_ + 1) * 64)
                    for dst, src in ((Bpr, b_re), (Bpi, b_im)):
                        for g0 in range(0, 64, 16):
                            kb.dma("sp", dst[sl, g0:g0 + 16, :], src[li, d_, g0:g0 + 16].rearrange("g p h -> p g h"), [], [b_in], b_in)
                for ci_, (src, d_) in enumerate(((c_re, 0), (c_re, 1), (c_im, 0), (c_im, 1))):
                    kb.dma("sp", cstA[ci_][:], src[li, d_].rearrange("(t g) h p -> (g h) t p", g=8), [], [b_in], b_in)
                R = R + [b_in]

                def tr_halves(dst, stg_):
                    for d_ in range(2):
                        sl = slice(d_ * 64, (d_ + 1) * 64)
                        kb.mm(ps1[sl, 0:64], [(stg_[sl, 0:64], ident[sl, sl])], R, [psb[1]])
                    kb.op("dve", lambda e: e.tensor_copy(out=dst, in_=ps1[:, 0:64]), R + [psb[1]], W)
                LR = T("LR", [128, 64]); LI = T("LI", [128, 64])
                tr_halves(LR[:], stgA)
                tr_halves(LI[:], stgB)
                if _STOP == 1:
                    kb.barrier()
                    return

                stp = T("stp", [128, 64]); lr = T("lr", [128, 64]); Are = T("Are", [128, 64]); Aim = T("Aim", [128, 64])
                mag = T("mag", [128, 64]); kf = T("kf", [128, 64]); ki = T("ki", [128, 64], I32); red = T("red", [128, 64])
                sn = T("sn", [128, 64]); cs = T("cs", [128, 64]); P1r = T("P1r", [128, 64]); P1i = T("P1i", [128, 64])
                u1 = T("u1", [128, 64]); u2 = T("u2", [128, 64])
                act(lambda e: e.activation(out=stp[:], in_=LS[:], func=AF.Exp))
                dve(lambda e: e.tensor_scalar(out=lr[:], in0=LR[:], scalar1=-1e-4, scalar2=None, op0=ALU.min))
                dve(lambda e: e.tensor_tensor(out=Are[:], in0=lr[:], in1=stp[:], op=ALU.mult))
                dve(lambda e: e.tensor_tensor(out=Aim[:], in0=LI[:], in1=stp[:], op=ALU.mult))
                act(lambda e: e.activation(out=mag[:], in_=Are[:], func=AF.Exp))

                def wrap(x):
                    dve(lambda e: e.tensor_scalar(out=kf[:], in0=x, scalar1=PI, scalar2=None, op0=ALU.is_gt))
                    dve(lambda e: e.scalar_tensor_tensor(out=x, in0=kf[:], scalar=-2 * PI, in1=x, op0=ALU.mult, op1=ALU.add))
                    dve(lambda e: e.tensor_scalar(out=kf[:], in0=x, scalar1=-PI, scalar2=None, op0=ALU.is_lt))
                    dve(lambda e: e.scalar_tensor_tensor(out=x, in0=kf[:], scalar=2 * PI, in1=x, op0=ALU.mult, op1=ALU.add))
                dve(lambda e: e.tensor_scalar(out=kf[:], in0=Aim[:], scalar1=1.0 / (2 * PI), scalar2=None, op0=ALU.mult))
                dve(lambda e: e.tensor_copy(out=ki[:], in_=kf[:]))
                dve(lambda e: e.tensor_copy(out=u1[:], in_=ki[:]))
                dve(lambda e: e.scalar_tensor_tensor(out=red[:], in0=u1[:], scalar=-2 * PI, in1=Aim[:], op0=ALU.mult, op1=ALU.add))
                wrap(red[:])
                wrap(red[:])
                act(lambda e: e.activation(out=sn[:], in_=red[:], func=AF.Sin))
                dve(lambda e: e.tensor_scalar(out=red[:], in0=red[:], scalar1=PI / 2, scalar2=None, op0=ALU.add))
                wrap(red[:])
                act(lambda e: e.activation(out=cs[:], in_=red[:], func=AF.Sin))
                dve(lambda e: e.tensor_tensor(out=P1r[:], in0=mag[:], in1=cs[:], op=ALU.mult))
                dve(lambda e: e.tensor_tensor(out=P1i[:], in0=mag[:], in1=sn[:], op=ALU.mult))
                if _STOP == 2:
                    kb.barrier()
                    return


                def cmul(eng, outr, outi, ar, ai, br, bi, t1, t2, neg_i=False, R_=None, W_=None):
                    o = lambda fn: kb.op(eng, fn, R if R_ is None else R_, W if W_ is None else W_)
                    o(lambda e: e.tensor_tensor(out=t1, in0=ar, in1=br, op=ALU.mult))
                    o(lambda e: e.tensor_tensor(out=t2, in0=ai, in1=bi, op=ALU.mult))
                    o(lambda e: e.tensor_tensor(out=outr, in0=t1, in1=t2, op=ALU.subtract))
                    o(lambda e: e.tensor_tensor(out=t1, in0=ar, in1=bi, op=ALU.mult))
                    o(lambda e: e.tensor_tensor(out=t2, in0=ai, in1=br, op=ALU.mult))
                    if neg_i:
                        o(lambda e: e.scalar_tensor_tensor(out=outi, in0=t1, scalar=-1.0, in1=t2, op0=ALU.mult, op1=ALU.subtract))
                    else:
                        o(lambda e: e.tensor_tensor(out=outi, in0=t1, in1=t2, op=ALU.add))
                den = T("den", [128, 64]); cr = T("cr", [128, 64]); ci = T("ci", [128, 64]); pm1 = T("pm1", [128, 64])
                dve(lambda e: e.tensor_tensor(out=u1[:], in0=lr[:], in1=lr[:], op=ALU.mult))
                dve(lambda e: e.tensor_tensor(out=u2[:], in0=LI[:], in1=LI[:], op=ALU.mult))
                dve(lambda e: e.tensor_tensor(out=den[:], in0=u1[:], in1=u2[:], op=ALU.add))
                dve(lambda e: e.reciprocal(out=den[:], in_=den[:]))
                dve(lambda e: e.tensor_scalar(out=pm1[:], in0=P1r[:], scalar1=-1.0, scalar2=None, op0=ALU.add))
                dve(lambda e: e.tensor_tensor(out=u1[:], in0=pm1[:], in1=lr[:], op=ALU.mult))
                dve(lambda e: e.tensor_tensor(out=u2[:], in0=P1i[:], in1=LI[:], op=ALU.mult))
                dve(lambda e: e.tensor_tensor(out=cr[:], in0=u1[:], in1=u2[:], op=ALU.add))
                dve(lambda e: e.tensor_tensor(out=cr[:], in0=cr[:], in1=den[:], op=ALU.mult))
                dve(lambda e: e.tensor_tensor(out=u1[:], in0=P1i[:], in1=lr[:], op=ALU.mult))
                dve(lambda e: e.tensor_tensor(out=u2[:], in0=pm1[:], in1=LI[:], op=ALU.mult))
                dve(lambda e: e.tensor_tensor(out=ci[:], in0=u1[:], in1=u2[:], op=ALU.subtract))
                dve(lambda e: e.tensor_tensor(out=ci[:], in0=ci[:], in1=den[:], op=ALU.mult))
                ivr = T("ivr", [128, 64]); ivi = T("ivi", [128, 64]); im2 = T("im2", [128, 64])
                act(lambda e: e.activation(out=im2[:], in_=Are[:], func=AF.Exp, scale=-2.0))
                dve(lambda e: e.tensor_tensor(out=ivr[:], in0=P1r[:], in1=im2[:], op=ALU.mult))
                dve(lambda e: e.scalar_tensor_tensor(out=ivi[:], in0=P1i[:], scalar=-1.0, in1=im2[:], op0=ALU.mult, op1=ALU.mult))
                M1r = T("M1r", [128, 64]); M1i = T("M1i", [128, 64]); Mvr = T("Mvr", [128, 64]); Mvi = T("Mvi", [128, 64])
                fh, bh = slice(0, 64), slice(64, 128)
                for dst, sf, sb_ in ((M1r, ivr, P1r), (M1i, ivi, P1i), (Mvr, P1r, ivr), (Mvi, P1i, ivi)):
                    dve(lambda e, dst=dst, sf=sf: e.tensor_copy(out=dst[fh, :], in_=sf[fh, :]))
                    dve(lambda e, dst=dst, sb_=sb_: e.tensor_copy(out=dst[bh, :], in_=sb_[bh, :]))
                Rr = T("Rr", [128, 9, 64]); Ri = T("Ri", [128, 9, 64]); Tr = T("Tr", [128, 9, 64]); Ti = T("Ti", [128, 9, 64])
                for X, v in ((Rr, 1.0), (Ri, 0.0), (Tr, 1.0), (Ti, 0.0)):
                    dve(lambda e, X=X, v=v: e.memset(X[:, 0, :], v))
                for j in range(8):
                    cmul("dve", Rr[:, j + 1, :], Ri[:, j + 1, :], Rr[:, j, :], Ri[:, j, :], M1r[:], M1i[:], u1[:], u2[:])
                    cmul("dve", Tr[:, j + 1, :], Ti[:, j + 1, :], Tr[:, j, :], Ti[:, j, :], Mvr[:], Mvi[:], u1[:], u2[:])
                A7r = T("A7r", [128, 64]); A7i = T("A7i", [128, 64]); C7r = T("C7r", [128, 64]); C7i = T("C7i", [128, 64])
                Cpr = T("Cpr", [128, 64]); Cpi = T("Cpi", [128, 64]); L8r = T("L8r", [128, 64]); L8i = T("L8i", [128, 64])
                for X, v in ((A7r, 1.0), (A7i, 0.0), (C7r, 1.0), (C7i, 0.0)):
                    dve(lambda e, X=X, v=v: e.memset(X[:], v))
                for dst, src, hs in ((A7r, Tr[:, 7, :], fh), (A7i, Ti[:, 7, :], fh), (C7r, Rr[:, 7, :], fh), (C7i, Ri[:, 7, :], fh),
                                     (Cpr, Tr[:, 1, :], fh), (Cpi, Ti[:, 1, :], fh), (Cpr, Rr[:, 8, :], bh), (Cpi, Ri[:, 8, :], bh),
                                     (L8r, Tr[:, 8, :], fh), (L8i, Ti[:, 8, :], fh), (L8r, Rr[:, 8, :], bh), (L8i, Ri[:, 8, :], bh)):
                    dve(lambda e, dst=dst, src=src, hs=hs: e.tensor_copy(out=dst[hs, :], in_=src[hs, :]))
                act(lambda e: e.activation(out=r8t[:, li, :], in_=Are[:], func=AF.Exp, scale=8.0))
                act(lambda e: e.activation(out=im2[:], in_=Are[:], func=AF.Exp, scale=-8.0))
                dve(lambda e: e.tensor_tensor(out=e1r[:, li, :], in0=L8r[:], in1=im2[:], op=ALU.mult))
                dve(lambda e: e.scalar_tensor_tensor(out=e1i[:, li, :], in0=L8i[:], scalar=-1.0, in1=im2[:], op0=ALU.mult, op1=ALU.mult))
                for d_ in range(2):
                    sl = slice(d_ * 64, (d_ + 1) * 64)
                    for c_ in range(2):
                        kb.mm(ps1[sl, c_ * 64:(c_ + 1) * 64], [(stgH[sl, c_:128:2], ident[sl, sl])], R, [psb[1]])
                dve(lambda e: e.tensor_copy(out=h0r[:, li, :], in_=ps1[:, 0:64]))
                dve(lambda e: e.tensor_copy(out=h0i[:, li, :], in_=ps1[:, 64:128]))
                if _STOP == 3:
                    kb.barrier()
                    return

                stD2 = T("stD2", [64, 8, 16]); D8 = T("D8", [128, 64])
                dve(lambda e: e.tensor_copy(out=stD2[:], in_=bc_mid(stD[:], 8)))
                kb.mm(ps1[:, 0:64], [(stD2[:].rearrange("p j h -> p (j h)"), ident[0:64, 0:64])], R, [psb[1]])
                dve(lambda e: e.tensor_copy(out=D8[:], in_=ps1[:, 0:64]))
                if _STOP == 4:
                    kb.barrier()
                    return

                Cpr_ = T("Cpr_", [128, 64, 16]); Cpi_ = T("Cpi_", [128, 64, 16])
                w1 = T("w1", [128, 64, 16]); w2 = T("w2", [128, 64, 16]); Xr = T("Xr", [128, 64, 16]); Xi = T("Xi", [128, 64, 16])
                Yr = T("Yr", [128, 64, 16]); Yi = T("Yi", [128, 64, 16])
                for ri_, dst in enumerate((Cpr_, Cpi_)):
                    for d_ in range(2):
                        sl = slice(d_ * 64, (d_ + 1) * 64)
                        cst = cstA[ri_ * 2 + d_]
                        for t0 in range(0, 8, 4):
                            for tt_ in range(4):
                                kb.mm(big[1][sl, tt_ * 128:(tt_ + 1) * 128], [(cst[:, t0 + tt_, :], ident[:])], R, [psb[2]])
                            dve(lambda e, dst=dst, sl=sl, t0=t0: e.tensor_copy(
                                out=dst[sl, t0 * 8:(t0 + 4) * 8, :].rearrange("p g h -> p (g h)"), in_=big[1][sl, 0:512]))
                cmul("dve", Xr[:], Xi[:], bc_last(cr[:], 16), bc_last(ci[:], 16), Bpr[:], Bpi[:], w1[:], w2[:])
                cmul("dve", Bpr[:], Bpi[:], bc_last(A7r[:], 16), bc_last(A7i[:], 16), Xr[:], Xi[:], w1[:], w2[:])
                cmul("dve", Xr[:], Xi[:], bc_last(C7r[:], 16), bc_last(C7i[:], 16), Cpr_[:], Cpi_[:], w1[:], w2[:])
                cmul("dve", Yr[:], Yi[:], bc_last(Cpr[:], 16), bc_last(Cpi[:], 16), Cpr_[:], Cpi_[:], w1[:], w2[:])
                if _STOP == 5:
                    kb.barrier()
                    return

                BFr = T("BFr", [128, 64, 8, 16], BF16); BFi = T("BFi", [128, 64, 8, 16], BF16)
                CMr = T("CMr", [128, 64, 8, 16], BF16); CMn = T("CMn", [128, 64, 8, 16], BF16)
                CFr = T("CFr", [128, 64, 8, 16], BF16); CFn = T("CFn", [128, 64, 8, 16], BF16)
                w3 = T("w3", [128, 64, 16]); w4 = T("w4", [128, 64, 16])
                b_bf = kb.buf("s5bf")
                b_cm = kb.buf("s5cm")
                for j in range(8):
                    cmul("pool", BFr[:, :, j, :], BFi[:, :, j, :], bc_last(Rr[:, j, :], 16), bc_last(Ri[:, j, :], 16), Bpr[:], Bpi[:], w3[:], w4[:],
                         R_=R + [b_bf], W_=[b_bf])
                    cmul("dve", CMr[:, :, j, :], CMn[:, :, j, :], bc_last(Tr[:, j, :], 16), bc_last(Ti[:, j, :], 16), Xr[:], Xi[:], w1[:], w2[:], neg_i=True,
                         R_=R + [b_cm], W_=[b_cm])
                    cmul("dve", CFr[:, :, j, :], CFn[:, :, j, :], bc_last(Tr[:, j, :], 16), bc_last(Ti[:, j, :], 16), Yr[:], Yi[:], w1[:], w2[:], neg_i=True,
                         R_=R + [b_cm], W_=[b_cm])
                R = R + [b_bf, b_cm]
                tblv = TBL[li]
                zt = T("zt", [128, 16, 128], BF16)
                dve(lambda e: e.memset(zt[:], 0.0))
                for slot, X, hs, ho_ in ((3, CFr, slice(0, 64), slice(64, 128)), (4, CFn, slice(0, 64), slice(64, 128)),
                                         (5, CFr, slice(64, 128), slice(0, 64)), (6, CFn, slice(64, 128), slice(0, 64))):
                    tv = tblv[:, slot].rearrange("g p n -> p g n")
                    kb.dma("sp", tv[hs], X[hs].rearrange("p g j h -> p g (j h)"), R, [], B)
                    for g0 in range(0, 64, 16):
                        kb.dma("sp", tv[ho_, g0:g0 + 16, :], zt[ho_], R, [], B)
                if _STOP == 6:
                    kb.barrier()
                    return

                st3 = [T("st3_%d" % i_, [128, 3, 8, 128], BF16) for i_ in range(2)]
                b_st3 = [kb.buf("st3_%d" % i_) for i_ in range(2)]
                k1 = [T("k1_%d" % i_, [128, 128]) for i_ in range(2)]
                k2 = [T("k2_%d" % i_, [128, 128]) for i_ in range(2)]
                b_k = [kb.buf("k_%d" % i_) for i_ in range(2)]
                cmz = [T("cmz%d" % i_, [128, 4, 128], BF16) for i_ in range(2)]
                b_cmz = [kb.buf("cmz%d" % i_) for i_ in range(2)]
                for i_ in range(2):
                    kb.op("dve", lambda e, i_=i_: e.memset(cmz[i_][:], 0.0), [], [b_cmz[i_]])
                RO = [B, b_const, b_s5p, b_in, b_bf, b_cm]
                for g in range(64):
                    s_ = g % 2
                    x_ = (g // 8) % 2
                    sti = st3[x_]
                    gi = g % 8
                    fl = lambda X: X[:, g].rearrange("p j h -> p (j h)")
                    pT, pK = psb[4 + s_], psb[6 + s_]
                    cT, cK = 512 * s_, 512 * s_
                    kb.mm(big[2][:, cT:cT + 128], [(fl(BFr), identb[:])], RO, [pT])
                    kb.mm(big[2][:, cT + 128:cT + 256], [(fl(BFi), identb[:])], RO, [pT])
                    kb.op("act", lambda e, sti=sti, gi=gi, cT=cT: e.activation(
                        out=sti[:, 0:2, gi, :], in_=big[2][:, cT:cT + 256].rearrange("p (t n) -> p t n", t=2), func=AF.Copy), [pT], [b_st3[x_]])
                    for d_ in range(2):
                        sl = slice(d_ * 64, (d_ + 1) * 64)
                        kb.op("act", lambda e, sl=sl, d_=d_, g=g, s_=s_: e.activation(
                            out=cmz[s_][sl, 2 * d_, :], in_=CMr[sl, g].rearrange("p j h -> p (j h)"), func=AF.Copy), RO + [b_cmz[s_]], [b_cmz[s_]])
                        kb.op("act", lambda e, sl=sl, d_=d_, g=g, s_=s_: e.activation(
                            out=cmz[s_][sl, 2 * d_ + 1, :], in_=CMn[sl, g].rearrange("p j h -> p (j h)"), func=AF.Copy), RO + [b_cmz[s_]], [b_cmz[s_]])
                    for d_ in range(2):
                        kb.mm(big[3][:, cK + d_ * 128:cK + (d_ + 1) * 128],
                              [(fl(BFr), cmz[s_][:, 2 * d_, :]), (fl(BFi), cmz[s_][:, 2 * d_ + 1, :])], RO + [b_cmz[s_]], [pK])
                    kb.op("dve", lambda e, s_=s_, cK=cK: e.tensor_tensor(out=k1[s_][:], in0=big[3][:, cK:cK + 128], in1=msk[:, 0, :], op=ALU.mult),
                          RO + [pK, b_k[s_]], [b_k[s_]])
                    kb.op("dve", lambda e, s_=s_, cK=cK: e.tensor_tensor(out=k2[s_][:], in0=big[3][:, cK + 128:cK + 256], in1=msk[:, 1, :], op=ALU.mult),
                          RO + [pK, b_k[s_]], [b_k[s_]])
                    kb.op("dve", lambda e, s_=s_: e.tensor_tensor(out=k1[s_][:], in0=k1[s_][:], in1=k2[s_][:], op=ALU.add), [b_k[s_]], [b_k[s_]])
                    kb.op("dve", lambda e, g=g, sti=sti, gi=gi, s_=s_: e.scalar_tensor_tensor(
                        out=sti[:, 2, gi, :], in0=ident[:], scalar=D8[:, g:g + 1], in1=k1[s_][:], op0=ALU.mult, op1=ALU.add),
                        RO + [b_k[s_], b_st3[x_]], [b_st3[x_]])
                    if gi == 7:
                        g0 = g - 7
                        for t_ in range(3):
                            kb.dma("sp", tblv[g0:g0 + 8, t_].rearrange("g p n -> p g n"), sti[:, t_], [b_st3[x_]], [], b_st3[x_])
                kb.barrier()

        def odd_mixer(l):
            li = l // 2
            with ExitStack() as phO:
                u8 = phO.enter_context(nc.sbuf_tensor(uq("u8all"), [128, 64, 640], BF16))
                b_u8 = [kb.buf("u8_%d" % g) for g in range(64)]
                finS = phO.enter_context(nc.sbuf_tensor(uq("finS"), [128, 64, 4, 2], F32))
                b_fin = kb.buf("finS")
                with ExitStack() as ph:
                    hall, b_hall = load_h_all(ph)
                    selS = ph.enter_context(nc.sbuf_tensor(uq("selS"), [128, 64, 128], BF16))
                    b_sel = kb.buf("selS")
                    kb.dma("sp", selS[:], sel_d.rearrange("a j r c -> r (a j) c"), [], [b_sel], b_sel)
                    for g in range(64):
                        ft, a = g // 8, g % 8
                        bk = big[g % 4]
                        pbs = [psb[2 * (g % 4)], psb[2 * (g % 4) + 1]]
                        kb.mm(bk[:, 0:512], [(selS[:, a * 8 + j, :], hall[:, ft, 1024 + j:5120:8]) for j in range(8)],
                              [b_sel] + b_hall[2:], pbs)
                        kb.mm(bk[:, 512:640], [(selS[:, a * 8 + j, :], hall[:, ft, j:1024:8]) for j in range(8)],
                              [b_sel] + b_hall[0:2], pbs)
                        if g % 2 == 0:
                            kb.op("act", lambda e, g=g, bk=bk: e.activation(out=u8[:, g, :], in_=bk[:, 0:640], func=AF.Copy), pbs, [b_u8[g]])
                        else:
                            kb.op("dve", lambda e, g=g, bk=bk: e.tensor_copy(out=u8[:, g, :], in_=bk[:, 0:640]), pbs, [b_u8[g]])
                    kb.barrier()
                with ExitStack() as ph:
                    def T(name, shape, dtype=F32):
                        return ph.enter_context(nc.sbuf_tensor(uq(name), list(shape), dtype))
                    GB = 2
                    SB = 16
                    TWr = [T("TWr%d" % i_, [128, GB, 640]) for i_ in range(2)]
                    TWi = [T("TWi%d" % i_, [128, GB, 640]) for i_ in range(2)]
                    dec = [T("dec%d" % i_, [128, GB, 640]) for i_ in range(2)]
                    b_TW = [kb.buf("TW%d" % i_) for i_ in range(2)]
                    b_dec = [kb.buf("dec%d" % i_) for i_ in range(2)]
                    F0r = T("F0r", [128, SB, 32]); F0i = T("F0i", [128, SB, 32])
                    F1r = T("F1r", [128, SB, 16]); F1i = T("F1i", [128, SB, 16])
                    b_F = kb.buf("F")
                    fq1 = T("fq1", [128, SB, 16]); fq2 = T("fq2", [128, SB, 16])
                    q1 = T("q1", [128, GB, 512]); q2 = T("q2", [128, GB, 512])
                    b_q = kb.buf("q")
                    mrow = T("mrow", [128, 640])
                    b_mrow = kb.buf("mrow")
                    tb = [T("tb%d" % i_, [128, 7, 128], BF16) for i_ in range(2)]
                    b_tb = [kb.buf("tb%d" % i_) for i_ in range(2)]
                    zin_r = T("zinr", [128, 640]); zin_i = T("zini", [128, 640])
                    b_zin = kb.buf("zin")
                    _zr = T("zr", [128, 640]); _zi = T("zi", [128, 640])
                    z_r = [_zr, _zr]
                    z_i = [_zi, _zi]
                    _bz = kb.buf("z")
                    b_z = [_bz, _bz]
                    S_r = [T("Sr%d" % i_, [128, 640]) for i_ in range(2)]
                    S_i = [T("Si%d" % i_, [128, 640]) for i_ in range(2)]
                    b_S = [kb.buf("S%d" % i_) for i_ in range(2)]
                    car_r = [T("carr%d" % i_, [128, 640], BF16) for i_ in range(2)]
                    car_i = [T("cari%d" % i_, [128, 640], BF16) for i_ in range(2)]
                    b_car = [kb.buf("car%d" % i_) for i_ in range(2)]
                    t2 = T("t2", [128, 640])
                    t1s = T("t1s", [128, 640])
                    t1p = t1s[:]
                    b_t12 = kb.buf("t12")
                    b_t34 = [kb.buf("t34_%d" % i_) for i_ in range(2)]
                    for i_ in range(2):
                        kb.op("dve", lambda e, i_=i_: e.memset(car_r[i_][:], 0.0), [], [b_car[i_]])
                        kb.op("dve", lambda e, i_=i_: e.memset(car_i[i_][:], 0.0), [], [b_car[i_]])
                    kb.op("dve", lambda e: e.memset(F1r[:, :, 0:1], 1.0), [], [b_F])
                    kb.op("dve", lambda e: e.memset(F1i[:, :, 0:1], 0.0), [], [b_F])
                    kb.op("dve", lambda e: e.memset(mrow[:], 1.0), [], [b_mrow])
                    for b_ in range(4):
                        kb.op("dve", lambda e, b_=b_: e.memset(mrow[:, 512 + 32 * b_:513 + 32 * b_], 0.0), [b_mrow], [b_mrow])
                    selT = T("selT", [128, 64, 128], BF16)
                    b_selT = kb.buf("selT")
                    kb.dma("sp", selT[:], selT_d.rearrange("a j r c -> r (a j) c"), [], [b_selT], b_selT)
                    zT = [T("zT%d" % i_, [128, NT], BF16) for i_ in range(2)]
                    b_zT = [kb.buf("zT%d" % i_) for i_ in range(2)]
                    o3_q = []

                    def o3_step(ft, j):
                        zi_ = ft % 2
                        bk = big[3]
                        pbs = [psb[6], psb[7]]
                        gs = [ft * 8 + a_ for a_ in range(8)]
                        kb.mm(bk[:, 0:512], [(selT[:, a_ * 8 + j, :], u8[:, ft * 8 + a_, 0:512]) for a_ in range(8)],
                              [b_selT] + [b_u8[g_] for g_ in gs], pbs)
                        kb.mm(bk[:, 512:640], [(selT[:, a_ * 8 + j, :], u8[:, ft * 8 + a_, 512:640]) for a_ in range(8)],
                              [b_selT] + [b_u8[g_] for g_ in gs], pbs)
                        kb.op("act", lambda e: e.activation(out=zT[zi_][:, 1024 + j:5120:8], in_=bk[:, 0:512], func=AF.Copy), pbs, [b_zT[zi_]])
                        kb.op("act", lambda e: e.activation(out=zT[zi_][:, j:1024:8], in_=bk[:, 512:640], func=AF.Copy), pbs, [b_zT[zi_]])
                        if j == 7:
                            kb.dma("sp", YCv[:, ft, :], zT[zi_][:], [b_zT[zi_]], [], b_zT[zi_])
                    tblv = TBL[li]
                    SLr, SLi = big[0], big[1]
                    p_slr, p_sli = [psb[0], psb[1]], [psb[2], psb[3]]

                    def cdbl(Xr, Xi, m, n):
                        ar = Xr[:, :, m - 1:m].to_broadcast([128, SB, n])
                        ai = Xi[:, :, m - 1:m].to_broadcast([128, SB, n])
                        br, bi2 = Xr[:, :, 0:n], Xi[:, :, 0:n]
                        qa, qb = fq1[:, :, 0:n], fq2[:, :, 0:n]
                        o = lambda fn: kb.op("dve", fn, [b_F], [b_F])
                        o(lambda e: e.tensor_tensor(out=qa, in0=br, in1=ar, op=ALU.mult))
                        o(lambda e: e.tensor_tensor(out=qb, in0=bi2, in1=ai, op=ALU.mult))
                        o(lambda e: e.tensor_tensor(out=Xr[:, :, m:m + n], in0=qa, in1=qb, op=ALU.subtract))
                        o(lambda e: e.tensor_tensor(out=qa, in0=br, in1=ai, op=ALU.mult))
                        o(lambda e: e.tensor_tensor(out=qb, in0=bi2, in1=ar, op=ALU.mult))
                        o(lambda e: e.tensor_tensor(out=Xi[:, :, m:m + n], in0=qa, in1=qb, op=ALU.add))

                    def b4(a_, kind):
                        if kind == "a":
                            return AP3(a_, [a_.ap[1], a_.ap[2], [0, 32]])
                        return AP3(a_, [a_.ap[1], [0, 16], a_.ap[2]])

                    for g in range(64):
                        if g % SB == 0:
                            kb.op("dve", lambda e, g=g: e.tensor_copy(out=F0r[:, :, 0], in_=e1r[:, li, g:g + SB]), [b_s5p, b_F], [b_F])
                            kb.op("dve", lambda e, g=g: e.tensor_copy(out=F0i[:, :, 0], in_=e1i[:, li, g:g + SB]), [b_s5p, b_F], [b_F])
                            for m in (1, 2, 4, 8, 16):
                                cdbl(F0r, F0i, m, m)
                            kb.op("dve", lambda e: e.tensor_copy(out=F1r[:, :, 1], in_=F0r[:, :, 31]), [b_F], [b_F])
                            kb.op("dve", lambda e: e.tensor_copy(out=F1i[:, :, 1], in_=F0i[:, :, 31]), [b_F], [b_F])
                            H_r, H_i = F1r[:, :, 1:16], F1i[:, :, 1:16]
                            for m, n in ((1, 1), (2, 2), (4, 4), (8, 7)):
                                cdbl(H_r, H_i, m, n)
                        gb, gi = g // GB, g % GB
                        bi_ = gb % 2
                        if gi == 0:
                            gs_ = g % SB
                            Wt, Rt = [b_TW[bi_], b_q], [b_TW[bi_], b_q, b_F]
                            po = lambda fn: kb.op("pool", fn, Rt, Wt)
                            v4_ = lambda X: X.rearrange("p g (a b) -> p g a b", b=32)
                            Or, Oi = v4_(TWr[bi_][:, :, 0:512]), v4_(TWi[bi_][:, :, 0:512])
                            Q1, Q2 = v4_(q1[:]), v4_(q2[:])
                            x0r, x0i = F0r[:, gs_:gs_ + GB, :], F0i[:, gs_:gs_ + GB, :]
                            A_r, A_i = b4(F1r[:, gs_:gs_ + GB, :], "a"), b4(F1i[:, gs_:gs_ + GB, :], "a")
                            B_r, B_i = b4(x0r, "b"), b4(x0i, "b")
                            po(lambda e: e.tensor_tensor(out=Q1, in0=A_r, in1=B_r, op=ALU.mult))
                            po(lambda e: e.tensor_tensor(out=Q2, in0=A_i, in1=B_i, op=ALU.mult))
                            po(lambda e: e.tensor_tensor(out=Or, in0=Q1, in1=Q2, op=ALU.subtract))
                            po(lambda e: e.tensor_tensor(out=Q1, in0=A_r, in1=B_i, op=ALU.mult))
                            po(lambda e: e.tensor_tensor(out=Q2, in0=A_i, in1=B_r, op=ALU.mult))
                            po(lambda e: e.tensor_tensor(out=Oi, in0=Q1, in1=Q2, op=ALU.add))
                            for b_ in range(4):
                                po(lambda e, b_=b_: e.tensor_copy(out=TWr[bi_][:, :, 512 + 32 * b_:544 + 32 * b_], in_=x0r))
                                po(lambda e, b_=b_: e.tensor_copy(out=TWi[bi_][:, :, 512 + 32 * b_:544 + 32 * b_], in_=x0i))
                            r8v = r8t[:, li, g:g + GB]
                            kb.op("dve", lambda e, r8v=r8v, bi_=bi_: e.tensor_tensor(
                                out=dec[bi_][:], in0=AP3(mrow[:], [[0, GB], mrow[:].ap[1]]), in1=AP3(r8v, [r8v.ap[1], [0, 640]]), op=ALU.mult),
                                [b_mrow, b_s5p, b_dec[bi_]], [b_dec[bi_]])
                        si = g % 2
                        Er, Ei = TWr[bi_][:, gi, :], TWi[bi_][:, gi, :]

                        def local_sums(g_):
                            s_ = g_ % 2
                            kb.dma("sp", tb[s_][:], tblv[g_].rearrange("t p n -> p t n"), [], [b_tb[s_]], b_tb[s_])
                            for SL, pb, slot in ((SLr, p_slr, 0), (SLi, p_sli, 1)):
                                kb.mm(SL[0:64, 0:512], [(tb[s_][:, slot, 0:64], u8[:, g_, 0:512])], [b_tb[s_], b_u8[g_]], pb)
                                kb.mm(SL[64:128, 0:512], [(tb[s_][:, slot, 64:128], u8[:, g_, 511::-1])], [b_tb[s_], b_u8[g_]], pb)
                                kb.mm(SL[0:64, 512:640], [(tb[s_][:, slot, 0:64], u8[:, g_, 512:640])], [b_tb[s_], b_u8[g_]], pb)
                                kb.mm(SL[64:128, 512:640], [(tb[s_][:, slot, 64:128], u8[:, g_, 639:511:-1])], [b_tb[s_], b_u8[g_]], pb)
                        if g == 0:
                            local_sums(0)
                        Rd = p_slr + p_sli + [b_TW[bi_], b_t12]
                        dv = lambda fn, Wd: kb.op("dve", fn, Rd + Wd, Wd)
                        dv(lambda e, Er=Er: e.tensor_tensor(out=t1p, in0=SLr[:, 0:640], in1=Er, op=ALU.mult), [b_t12])
                        dv(lambda e, Ei=Ei: e.tensor_tensor(out=t2[:], in0=SLi[:, 0:640], in1=Ei, op=ALU.mult), [b_t12])
                        dv(lambda e: e.tensor_tensor(out=zin_r[:], in0=t1p, in1=t2[:], op=ALU.subtract), [b_zin])
                        dv(lambda e, Er=Er: e.tensor_tensor(out=t1p, in0=SLi[:, 0:640], in1=Er, op=ALU.mult), [b_t12])
                        dv(lambda e, Ei=Ei: e.tensor_tensor(out=t2[:], in0=SLr[:, 0:640], in1=Ei, op=ALU.mult), [b_t12])
                        dv(lambda e: e.tensor_tensor(out=zin_i[:], in0=t1p, in1=t2[:], op=ALU.add), [b_zin])
                        if g + 1 < 64:
                            local_sums(g + 1)
                        for zin, z, h0 in ((zin_r, z_r, h0r), (zin_i, z_i, h0i)):
                            kb.op("dve", lambda e, zin=zin, z=z, h0=h0, g=g, si=si, gi=gi, bi_=bi_: e.tensor_tensor_scan(
                                out=z[si][:], data0=dec[bi_][:, gi, :], data1=zin[:],
                                initial=h0[:, li, g:g + 1], op0=ALU.mult, op1=ALU.add), [b_zin, b_s5p, b_z[si], b_dec[bi_]], [b_z[si]])
                        Rr_ = [b_z[si], b_TW[bi_], b_t12]
                        kb.op("dve", lambda e, Er=Er, si=si: e.tensor_tensor(out=t1p, in0=z_r[si][:], in1=Er, op=ALU.mult), Rr_, [b_t12])
                        kb.op("dve", lambda e, Ei=Ei, si=si: e.tensor_tensor(out=t2[:], in0=z_i[si][:], in1=Ei, op=ALU.mult), Rr_, [b_t12])
                        kb.op("dve", lambda e, si=si: e.tensor_tensor(out=S_r[si][:], in0=t1p, in1=t2[:], op=ALU.add), [b_t12, b_S[si]], [b_S[si]])
                        kb.op("dve", lambda e, Er=Er, si=si: e.tensor_tensor(out=t1p, in0=z_i[si][:], in1=Er, op=ALU.mult), Rr_, [b_t12])
                        kb.op("dve", lambda e, Ei=Ei, si=si: e.tensor_tensor(out=t2[:], in0=z_r[si][:], in1=Ei, op=ALU.mult), Rr_, [b_t12])
                        kb.op("dve", lambda e, si=si: e.tensor_tensor(out=S_i[si][:], in0=t1p, in1=t2[:], op=ALU.subtract), [b_t12, b_S[si]], [b_S[si]])
                        v4 = lambda X: X.rearrange("p (b c) -> p b c", c=32)
                        for part, (S_, car, h0) in enumerate(((S_r, car_r, h0r), (S_i, car_i, h0i))):
                            kb.op("act", lambda e, S_=S_, car=car, si=si: e.activation(out=car[si][:, 1:512], in_=S_[si][:, 0:511], func=AF.Copy),
                                  [b_S[si], b_car[si]], [b_car[si]])
                            kb.op("act", lambda e, S_=S_, car=car, si=si: e.activation(
                                out=v4(car[si][:, 512:640])[:, :, 1:32], in_=v4(S_[si][:, 512:640])[:, :, 0:31], func=AF.Copy),
                                [b_S[si], b_car[si]], [b_car[si]])
                            kb.op("act", lambda e, S_=S_, g=g, part=part, si=si: e.activation(out=finS[:, g, :, part], in_=S_[si][:, 543:640:32], func=AF.Copy),
                                  [b_S[si], b_fin], [b_fin])
                            kb.op("act", lambda e, g=g, si=si, car=car, h0=h0: e.activation(out=car[si][:, 0:1], in_=h0[:, li, g:g + 1], func=AF.Copy),
                                  [b_s5p, b_car[si]], [b_car[si]])
                        Y = big[2]
                        pby = [psb[4], psb[5]]
                        kb.mm(Y[:, 0:512], [(tb[si][:, 2, :], u8[:, g, 0:512]),
                                            (tb[si][:, 3, :], car_r[si][:, 0:512]), (tb[si][:, 4, :], car_i[si][:, 0:512]),
                                            (tb[si][:, 5, :], car_r[si][:, 511::-1]), (tb[si][:, 6, :], car_i[si][:, 511::-1])],
                              [b_tb[si], b_u8[g], b_car[si]], pby)
                        kb.mm(Y[:, 512:640], [(tb[si][:, 2, :], u8[:, g, 512:640]),
                                              (tb[si][:, 3, :], car_r[si][:, 512:640]), (tb[si][:, 4, :], car_i[si][:, 512:640]),
                                              (tb[si][:, 5, :], car_r[si][:, 639:511:-1]), (tb[si][:, 6, :], car_i[si][:, 639:511:-1])],
                              [b_tb[si], b_u8[g], b_car[si]], pby)
                        kb.op("act", lambda e, g=g, Y=Y: e.activation(out=u8[:, g, :], in_=Y[:, 0:640], func=AF.Gelu), pby, [b_u8[g]])
                        if g % 8 == 7:
                            for j_ in range(8):
                                o3_q.append((g // 8, j_))
                        if o3_q:
                            o3_step(*o3_q.pop(0))
                    while o3_q:
                        o3_step(*o3_q.pop(0))
                    kb.barrier()
                with ExitStack() as ph:
                    outS = ph.enter_context(nc.sbuf_tensor(uq("outS"), [128, 4, 64, 2], F32))
                    b_out = kb.buf("outS")
                    n_ = 0
                    for s_ in range(4):
                        for c_ in range(2):
                            pi = n_ % 8
                            n_ += 1
                            for d_ in range(2):
                                sl = slice(d_ * 64, (d_ + 1) * 64)
                                b_ = s_ if d_ == 0 else 3 - s_
                                kb.mm(pst[pi][sl, 0:64], [(finS[sl, :, b_, c_], ident[sl, sl])], [b_fin, b_const], [psb[pi]])
                            kb.op("dve", lambda e, pi=pi, s_=s_, c_=c_: e.tensor_copy(out=outS[:, s_, :, c_], in_=pst[pi][:, 0:64]),
                                  [psb[pi]], [b_out])
                        kb.dma("sp", ns[s_, li].rearrange("d g p c -> (d g) (p c)"), outS[:, s_].rearrange("q p c -> q (p c)"), [b_out], [], b_out)
                    kb.barrier()
            out_proj(l, w_glu[li], glu=True)

        if mode == "s5setup":
            s5_setup(0)
            b_dbg = kb.buf("dbg")
            kb.dma("sp", dbg_tbl, TBL[0], [], [], b_dbg)
            for i_, t_ in enumerate((r8t, e1r, e1i, h0r, h0i)):
                kb.dma("sp", dbg_small[:, i_], t_[:], [b_s5p], [], b_dbg)
            kb.barrier()
            return nc
        if with_s5:
            for l in range(1, depth, 2):
                s5_setup(l // 2)
        for l in range(depth):
            if l % 2 == 0:
                even_mixer(l)
            else:
                if with_s5:
                    odd_mixer(l)
            last = (l == depth - 1)
            ffn(l, None if last else l + 1, 0)

        kb.barrier()
    return nc


_CONST = {}


def _consts():
    if _CONST:
        return _CONST
    bf = ml_dtypes.bfloat16
    _CONST["c_ident"] = np.eye(128, dtype=np.float32)
    _CONST["c_ones"] = np.ones((128, 128), dtype=bf)
    c = np.arange(128)[:, None] * np.arange(128)[None, :]
    ang = 2 * np.pi * (c % 128) / 128.0
    _CONST["c_cs128"] = np.concatenate([np.cos(ang), np.sin(ang)], axis=1).astype(bf)

    def dft(L):
        lk = (np.arange(L, dtype=np.int64)[:, None] * np.arange(L, dtype=np.int64)[None, :]) % L
        a = 2 * np.pi * lk / float(L)
        return np.stack([np.cos(a), -np.sin(a)]).astype(bf)
    _CONST["c_dftp"] = dft(256)
    t = dft(4096).reshape(2, 32, 128, 16, 256)
    _CONST["c_dfts"] = np.ascontiguousarray(t.transpose(3, 2, 0, 1, 4)).reshape(16, 128, 2 * 32 * 256)
    sel = np.zeros((8, 8, 128, 128), np.float32)
    for a in range(8):
        for j in range(8):
            for h in range(16):
                sel[a, j, a * 16 + h, j * 16 + h] = 1.0
    _CONST["c_sel"] = sel.astype(bf)
    _CONST["c_selT"] = np.ascontiguousarray(sel.transpose(0, 1, 3, 2)).astype(bf)
    jj = np.arange(128) // 16
    mf = (jj[None, :] >= jj[:, None]).astype(np.float32)
    mb = (jj[None, :] <= jj[:, None]).astype(np.float32)
    _CONST["c_mask"] = np.stack([mf, mb])
    return _CONST


_NC_CACHE = {}


def kernel(x_prompt, x_sample, state_ssm, c, c_ctx, w_ada, b_ada, norm1_g, norm2_g, final_g,
           w_in_mix, w_conv, w_out_mix, ssm_lambda_re, ssm_lambda_im, ssm_log_step,
           ssm_b_re, ssm_b_im, ssm_c_re, ssm_c_im, ssm_d, w_glu, w_ffn_in, w_ffn_out,
           _depth=4, _with_s5=True, _mode=None, _ncores=8):
    f = lambda a: np.ascontiguousarray(np.asarray(a, dtype=np.float32))
    key = (_depth, _with_s5, _mode)
    if key not in _NC_CACHE:
        _NC_CACHE[key] = build(_depth, _with_s5, _mode)
    nc = _NC_CACHE[key]
    shared = dict(w_ada=f(w_ada), b_ada=f(b_ada), norm1_g=f(norm1_g), norm2_g=f(norm2_g), final_g=f(final_g),
                  w_conv=f(w_conv), w_out_mix=f(w_out_mix),
                  ssm_lambda_re=f(ssm_lambda_re), ssm_lambda_im=f(ssm_lambda_im), ssm_log_step=f(ssm_log_step),
                  ssm_b_re=f(ssm_b_re), ssm_b_im=f(ssm_b_im), ssm_c_re=f(ssm_c_re), ssm_c_im=f(ssm_c_im),
                  ssm_d=f(ssm_d), w_glu=f(w_glu), w_ffn_out=f(w_ffn_out))
    wfi = f(w_ffn_in).reshape(4, 8, 128, 2, 22, 128)
    shared["w_ffn_in"] = np.ascontiguousarray(wfi.transpose(0, 4, 2, 1, 3, 5)).reshape(4, 22, 128, 8 * 256)
    wim = f(w_in_mix).reshape(2, 8, 128, 4, 4, 128)
    cv = wim[:, :, :, [0, 2, 1]]
    cv = np.ascontiguousarray(cv.transpose(0, 4, 2, 1, 3, 5)).reshape(2, 4, 128, 8 * 384)
    fv = np.zeros((2, 4, 128, 8, 384), np.float32)
    fv[:, :, :, :, 0:128] = wim[:, :, :, 3].transpose(0, 3, 2, 1, 4)
    shared["w_in_mix"] = np.ascontiguousarray(np.concatenate([cv, fv.reshape(2, 4, 128, 8 * 384)], axis=1))
    shared.update(_consts())
    xpn = f(x_prompt)
    xsn = f(x_sample)
    stn = f(state_ssm)
    cn = f(c)
    cc = f(c_ctx)
    in_maps = []
    for i in range(8):
        m = dict(shared)
        m["xp"] = xpn[4 * i:4 * i + 4].reshape(1024, D)
        m["xs"] = xsn[i]
        m["st"] = stn[i]
        m["cond"] = np.stack([cc, cn[i]])
        in_maps.append(m)
    if _mode is not None:
        res = run_bass_kernel_spmd(nc, in_maps[:_ncores], core_ids=list(range(_ncores)))
        return res.results
    res = run_bass_kernel_spmd(nc, in_maps, core_ids=list(range(8)))
    r = res.results
    y_prompt = np.concatenate([r[i]["yp"].reshape(4, 256, D) for i in range(8)], axis=0)
    y_sample = np.stack([r[i]["ys"] for i in range(8)], axis=0)
    new_state = np.concatenate([r[i]["ns"] for i in range(8)], axis=0)
    return (y_prompt.astype(np.float32), y_sample.astype(np.float32), new_state.astype(np.float32))
```

```python
import math
from contextlib import ExitStack
import numpy as np
import ml_dtypes
import concourse.bass as bass
import concourse.mybir as mybir
from concourse.bass_utils import run_bass_kernel_spmd

F32 = mybir.dt.float32
BF16 = mybir.dt.bfloat16
AF = mybir.ActivationFunctionType
ALU = mybir.AluOpType

D = 1024
NT = 5120
TT = 512
NTT = NT // TT
NPT = 2
DFF = 2816
KT = 8
EPS = 1e-6


class Buf:
    __slots__ = ("name", "w", "r", "dsem", "dcnt")

    def __init__(self, name):
        self.name = name
        self.w = None
        self.r = {}
        self.dsem = None
        self.dcnt = 0


class KB:
    def __init__(self, nc, es):
        self.nc = nc
        self.es = es
        self.eng = {"pe": nc.tensor, "act": nc.scalar, "dve": nc.vector,
                    "pool": nc.gpsimd, "sp": nc.sync}
        self.sem = {e: es.enter_context(nc.semaphore("s_" + e)) for e in self.eng}
        self.cnt = {e: 0 for e in self.eng}
        self.seen = {e: {} for e in self.eng}
        self.bar = es.enter_context(nc.semaphore("s_bar"))
        self.barcnt = 0
        self.dsems = []
        self.free_ds = []
        self.bufs = []

    def buf(self, name):
        b = Buf("%s_%d" % (name, len(self.bufs)))
        self.bufs.append(b)
        return b

    def _deps(self, reads, writes):
        deps = {}

        def add(t):
            k, sem, v = t
            if k not in deps or deps[k][1] < v:
                deps[k] = (sem, v)
        for b in reads:
            if b.w is not None:
                add(b.w)
        for b in writes:
            if b.w is not None:
                add(b.w)
            for k, (sem, v) in b.r.items():
                add((k, sem, v))
        return deps

    def _waits(self, e, reads, writes):
        eng = self.eng[e]
        for k, (sem, v) in self._deps(reads, writes).items():
            if e == "pe" and k == "pe":
                continue
            if self.seen[e].get(k, 0) < v:
                eng.wait_ge(sem, v)
                self.seen[e][k] = v

    def _commit(self, tok, reads, writes):
        k, sem, v = tok
        for b in writes:
            b.w = tok
            b.r = {}
        for b in reads:
            if b.r.get(k, (None, 0))[1] < v:
                b.r[k] = (sem, v)

    def op(self, e, fn, reads=(), writes=()):
        self._waits(e, reads, writes)
        ins = fn(self.eng[e])
        self.cnt[e] += 1
        ins.then_inc(self.sem[e], 1)
        self._commit((e, self.sem[e], self.cnt[e]), reads, writes)

    def mm(self, out_ap, pairs, reads, writes, start=True, stop=True):
        self._waits("pe", reads, writes)
        n = len(pairs)
        ins = None
        for i, (l, r) in enumerate(pairs):
            ins = self.nc.tensor.matmul(out_ap, lhsT=l, rhs=r, start=(start and i == 0), stop=(stop and i == n - 1))
        self.cnt["pe"] += 1
        ins.then_inc(self.sem["pe"], 1)
        self._commit(("pe", self.sem["pe"], self.cnt["pe"]), reads, writes)

    def transpose(self, out_ap, in_ap, ident_ap, reads, writes):
        self._waits("pe", reads, writes)
        ins = self.nc.tensor.transpose(out_ap, in_ap, ident_ap)
        self.cnt["pe"] += 1
        ins.then_inc(self.sem["pe"], 1)
        self._commit(("pe", self.sem["pe"], self.cnt["pe"]), reads, writes)

    def dma(self, q, out_ap, in_ap, reads, writes, anchor, **kw):
        self._waits(q, reads, writes)
        if anchor.dsem is None:
            if self.free_ds:
                anchor.dsem = self.free_ds.pop()
            else:
                ds = [self.es.enter_context(self.nc.semaphore("dsem%d" % len(self.dsems))), 0, len(self.dsems)]
                self.dsems.append(ds)
                anchor.dsem = ds
        ds = anchor.dsem
        ins = self.eng[q].dma_start(out=out_ap, in_=in_ap, **kw)
        ds[1] += 16
        ins.then_inc(ds[0], 16)
        self._commit(("d%d" % ds[2], ds[0], ds[1]), reads, writes)

    def barrier(self):
        sp = self.nc.sync
        for e in self.eng:
            if e != "sp" and self.cnt[e] > 0:
                sp.wait_ge(self.sem[e], self.cnt[e])
        for ds in self.dsems:
            if ds[1] > 0:
                sp.wait_ge(ds[0], ds[1])
        self.barcnt += 1
        sp.sem_inc(self.bar, 1)
        for e in self.eng:
            if e != "sp":
                self.eng[e].wait_ge(self.bar, self.barcnt)
        for b in self.bufs:
            b.w = None
            b.r = {}
            b.dsem = None
        self.free_ds = list(self.dsems)
        for e in self.eng:
            for e2 in self.eng:
                self.seen[e][e2] = self.cnt[e2]
            for ds in self.dsems:
                self.seen[e]["d%d" % ds[2]] = ds[1]


def build(depth=4, with_s5=True, mode=None):
    nc = bass.Bass("TRN2", target_bir_lowering=False)
    dt = nc.dram_tensor

    def din(name, shape, dtype=F32):
        return dt(name, list(shape), dtype, kind="ExternalInput").ap()

    xp = din("xp", [1024, D])
    xs = din("xs", [4096, D])
    st_in = din("st", [2, 2, 64, 64, 2])
    cond = din("cond", [2, D])
    w_ada = din("w_ada", [4, D, 6 * D])
    b_ada = din("b_ada", [4, 6 * D])
    norm1_g = din("norm1_g", [4, D])
    norm2_g = din("norm2_g", [4, D])
    final_g = din("final_g", [D])
    w_in_mix = din("w_in_mix", [2, 8, 128, 8 * 384])
    w_conv = din("w_conv", [2, 3, 512])
    w_out_mix = din("w_out_mix", [2, D, D])
    lam_re = din("ssm_lambda_re", [2, 2, 64, 64])
    lam_im = din("ssm_lambda_im", [2, 2, 64, 64])
    log_step = din("ssm_log_step", [2, 2, 64])
    b_re = din("ssm_b_re", [2, 2, 64, 64, 16])
    b_im = din("ssm_b_im", [2, 2, 64, 64, 16])
    c_re = din("ssm_c_re", [2, 2, 64, 16, 64])
    c_im = din("ssm_c_im", [2, 2, 64, 16, 64])
    ssm_d = din("ssm_d", [2, D])
    w_glu = din("w_glu", [2, D, 2 * D])
    w_ffn_in = din("w_ffn_in", [4, 22, 128, 8 * 256])
    w_ffn_out = din("w_ffn_out", [4, DFF, D])
    ident_d = din("c_ident", [128, 128])
    ones_d = din("c_ones", [128, 128], BF16)
    cs128_d = din("c_cs128", [128, 256], BF16)
    dftp_d = din("c_dftp", [2, 256, 256], BF16)
    dfts_d = din("c_dfts", [8, 2, 128, 32 * 512], BF16)
    sel_d = din("c_sel", [8, 8, 128, 128], BF16)
    selT_d = din("c_selT", [8, 8, 128, 128], BF16)
    mask_d = din("c_mask", [2, 128, 128])
    TBL = dt("TBL", [2, 64, 7, 128, 128], BF16, kind="Internal").ap()

    yp = dt("yp", [1024, D], F32, kind="ExternalOutput").ap()
    ys = dt("ys", [4096, D], F32, kind="ExternalOutput").ap()
    ns = dt("ns", [4, 2, 2, 64, 64, 2], F32, kind="ExternalOutput").ap()
    if mode is not None:
        dbg_tbl = dt("dbg_tbl", [64, 7, 128, 128], BF16, kind="ExternalOutput").ap()
        dbg_small = dt("dbg_small", [128, 5, 2, 64], F32, kind="ExternalOutput").ap()

    XT = dt("XT", [D, NT], F32, kind="Internal").ap()
    HT = dt("HT", [D, NT], BF16, kind="Internal").ap()
    YC = dt("YC", [D, NT], BF16, kind="Internal").ap()
    HID = dt("HID", [DFF, NT], BF16, kind="Internal").ap()

    XTv = XT.rearrange("(k p) n -> p k n", p=128)
    HTv = HT.rearrange("(k p) n -> p k n", p=128)
    YCv = YC.rearrange("(k p) n -> p k n", p=128)
    HIDv = HID.rearrange("(k p) n -> p k n", p=128)

    _uq = [0]

    def uq(name):
        _uq[0] += 1
        return "%s_%d" % (name, _uq[0])

    with ExitStack() as es:
        kb = KB(nc, es)
        sb = lambda name, shape, dtype: es.enter_context(nc.sbuf_tensor(uq(name), list(shape), dtype))

        ident = sb("ident", [128, 128], F32)
        ones = sb("ones", [128, 128], BF16)
        modT = sb("modT", [128, 4, 48, 2], F32)
        g1t = sb("g1t", [128, 4, 8], F32)
        g2t = sb("g2t", [128, 4, 8], F32)
        gft = sb("gft", [128, 8], F32)
        acoef = sb("acoef", [128, 4, 2, 8, 2], F32)
        epsb = sb("epsb", [128, 1], F32)
        b_const = kb.buf("const")
        b_mod = kb.buf("mod")
        psb = [kb.buf("ps%d" % i) for i in range(8)]
        big = [es.enter_context(nc.psum_tensor("psbig%d" % i, [128, 1024], F32)) for i in range(4)]

        class _V:
            def __init__(self, t, lo):
                self.t, self.lo = t, lo

            def __getitem__(self, idx):
                if not isinstance(idx, tuple):
                    idx = (idx, slice(None))
                p, c = idx[0], idx[1]
                c0 = 0 if c.start is None else c.start
                c1 = 512 if c.stop is None else c.stop
                assert c.step is None
                return self.t[p, self.lo + c0:self.lo + c1]
        pst = [_V(big[i // 2], 512 * (i % 2)) for i in range(8)]

        kb.dma("sp", ident[:], ident_d, [], [b_const], b_const)
        kb.dma("sp", ones[:], ones_d, [], [b_const], b_const)
        kb.op("dve", lambda e: e.memset(epsb[:], EPS), [], [b_const])

        _ADA_PH = []
        if True:
            ph = es.enter_context(ExitStack())
            psb_ = lambda name, shape, dtype: ph.enter_context(nc.sbuf_tensor(uq(name), list(shape), dtype))
            cT = psb_("cT", [128, 2, 8], F32)
            cTb = psb_("cTb", [128, 2, 8], BF16)
            bT = psb_("bT", [128, 4, 48], F32)
            gT = psb_("gT", [128, 2, 4, 8], F32)
            wsl = [psb_("wada%d" % i, [128, 6144], BF16) for i in range(3)]
            b_wsl = [kb.buf("wada%d" % i) for i in range(3)]
            b_c = kb.buf("cT")
            stage = psb_("vstage", [128, 128], F32)
            b_stage = kb.buf("vstage")

            def load_T(dst_ap, src_rows, n):
                kb.dma("sp", stage[0:n, :], src_rows, [], [b_stage], b_stage)
                kb.transpose(pst[1][:, 0:n], stage[0:n, :], ident[0:n, 0:n], [b_stage, b_const], [psb[1]])
                kb.op("dve", lambda e: e.tensor_copy(out=dst_ap, in_=pst[1][:, 0:n]), [psb[1]], [b_c])
            load_T(cT[:].rearrange("p j k -> p (j k)"), cond.rearrange("j (k p) -> (j k) p", p=128), 16)
            b2 = b_ada.rearrange("l (m p) -> (l m) p", p=128)
            load_T(bT[:, 0:2, :].rearrange("p l m -> p (l m)"), b2[0:96, :], 96)
            load_T(bT[:, 2:4, :].rearrange("p l m -> p (l m)"), b2[96:192, :], 96)
            load_T(gT[:, 0].rearrange("p l k -> p (l k)"), norm1_g.rearrange("l (k p) -> (l k) p", p=128), 32)
            load_T(gT[:, 1].rearrange("p l k -> p (l k)"), norm2_g.rearrange("l (k p) -> (l k) p", p=128), 32)
            load_T(gft[:], final_g.rearrange("(k p) -> k p", p=128), 8)
            kb.op("act", lambda e: e.activation(out=cTb[:], in_=cT[:], func=AF.Silu), [b_c], [b_c])
            idx_ = [0]

            def ada_step(l, kt):
                s = idx_[0] % 3
                idx_[0] += 1
                kb.dma("pool", wsl[s][:], w_ada[l, kt * 128:(kt + 1) * 128, :], [], [b_wsl[s]], b_wsl[s], max_dma_last_dim=4096)
                kb._waits("pe", [b_wsl[s], b_c], [psb[0]])
                ins = None
                for m in range(48):
                    ins = nc.tensor.matmul(pst[0][:, 2 * m:2 * m + 2], lhsT=wsl[s][:, m * 128:(m + 1) * 128],
                                           rhs=cTb[:, :, kt], start=(kt == 0 and m == 0), stop=(kt == KT - 1 and m == 47),
                                           skip_group_check=True)
                kb.cnt["pe"] += 1
                ins.then_inc(kb.sem["pe"], 1)
                kb._commit(("pe", kb.sem["pe"], kb.cnt["pe"]), [b_wsl[s], b_c], [psb[0]])

            def ada_epi(l):
                for j in range(2):
                    kb.op("dve", lambda e, l=l, j=j: e.tensor_tensor(
                        out=modT[:, l, :, j], in0=pst[0][:, 0:96].rearrange("p (m j) -> p m j", j=2)[:, :, j],
                        in1=bT[:, l, :], op=ALU.add), [psb[0], b_c], [b_mod])
                for sub in range(2):
                    base = 24 * sub
                    for j in range(2):
                        kb.op("dve", lambda e, l=l, sub=sub, j=j, base=base: e.scalar_tensor_tensor(
                            out=acoef[:, l, sub, :, j], in0=modT[:, l, base + 8:base + 16, j], scalar=1.0,
                            in1=gT[:, sub, l, :], op0=ALU.add, op1=ALU.mult), [b_mod, b_c], [b_mod])

            def ada_layer(l):
                for kt in range(KT):
                    ada_step(l, kt)
                ada_epi(l)
            _ada_q = []
            for l_ in range(1, depth):
                for kt_ in range(KT):
                    _ada_q.append(lambda l_=l_, kt_=kt_: ada_step(l_, kt_))
                _ada_q.append(lambda l_=l_: ada_epi(l_))
            ada_layer(0)
            _ADA_PH.append((ph, _ada_q))

        def mod_ap(l, chunk, ft, j):
            return modT[:, l, chunk * 8 + ft, j:j + 1]

        class Fin:
            pass

        def make_fin(ph, final=False):
            f = Fin()
            psb_ = lambda name, shape, dtype: ph.enter_context(nc.sbuf_tensor(uq(name), list(shape), dtype))
            f.sq = psb_("f_sq", [128, 8, TT], BF16)
            f.rstd = psb_("f_rstd", [128, TT], F32)
            if not final:
                f.tmp = [psb_("f_tmp%d" % i, [128, TT], F32) for i in range(2)]
                f.h = [psb_("f_h%d" % i, [128, 8, TT], BF16) for i in range(2)]
            f.b_sq = kb.buf("f_sq")
            f.b_rstd = kb.buf("f_rstd")
            f.b_tmp = [kb.buf("f_tmp%d" % i) for i in range(2)]
            f.b_h = [kb.buf("f_h%d" % i) for i in range(2)]
            f.n = 0
            return f

        b_XT = kb.buf("XT")
        b_HT = kb.buf("HT")

        def finish(f, tt, xn, b_xn, l_next, sub_next, ps_i, store_x=True, q="sp"):
            j = 0 if tt < NPT else 1
            cols = slice(tt * TT, (tt + 1) * TT)
            if store_x and l_next is not None:
                kb.dma(q, XTv[:, :, cols], xn[:], [b_xn], [], b_xn)
            kb.op("act", lambda e: e.activation(out=f.sq[:], in_=xn[:], func=AF.Square), [b_xn], [f.b_sq])
            kb.mm(pst[ps_i][:], [(ones[:], f.sq[:, k, :]) for k in range(KT)], [f.b_sq, b_const], [psb[ps_i]])
            kb.op("act", lambda e: e.activation(out=f.rstd[:], in_=pst[ps_i][:], func=AF.Sqrt,
                                                bias=epsb[:], scale=1.0 / D), [psb[ps_i], b_const], [f.b_rstd])
            kb.op("dve", lambda e: e.reciprocal(out=f.rstd[:], in_=f.rstd[:]), [f.b_rstd], [f.b_rstd])
            if l_next is None:
                return
            hi = f.n % 2
            f.n += 1
            for k in range(KT):
                ti = k % 2
                kb.op("dve", lambda e, k=k, ti=ti: e.scalar_tensor_tensor(
                    out=f.tmp[ti][:], in0=xn[:, k, :], scalar=acoef[:, l_next, sub_next, k, j:j + 1],
                    in1=f.rstd[:], op0=ALU.mult, op1=ALU.mult), [b_xn, f.b_rstd, b_mod], [f.b_tmp[ti]])
                kb.op("act", lambda e, k=k, ti=ti, hi=hi: e.activation(
                    out=f.h[hi][:, k, :], in_=f.tmp[ti][:], func=AF.Identity,
                    bias=mod_ap(l_next, 3 * sub_next, k, j), scale=1.0), [f.b_tmp[ti], b_mod], [f.b_h[hi]])
            kb.dma(q, HTv[:, :, cols], f.h[hi][:], [f.b_h[hi]], [], f.b_h[hi])

        with ExitStack() as ph:
            psb_ = lambda name, shape, dtype: ph.enter_context(nc.sbuf_tensor(uq(name), list(shape), dtype))
            fin = make_fin(ph)
            xtok = [psb_("xtok%d" % i, [128, 4, D], F32) for i in range(2)]
            b_xtok = [kb.buf("xtok%d" % i) for i in range(2)]
            xn = [psb_("xn%d" % i, [128, 8, TT], F32) for i in range(2)]
            b_xn = [kb.buf("xn%d" % i) for i in range(2)]
            for tt in range(NTT):
                i = tt % 2
                src = xp if tt < NPT else xs
                r0 = tt * TT if tt < NPT else (tt - NPT) * TT
                kb.dma("sp", xtok[i][:], src[r0:r0 + TT, :].rearrange("(a p) d -> p a d", p=128),
                       [], [b_xtok[i]], b_xtok[i])
                for k in range(KT):
                    pi = 1 + (k % 3)
                    for a in range(4):
                        kb.transpose(pst[pi][:, a * 128:(a + 1) * 128], xtok[i][:, a, k * 128:(k + 1) * 128],
                                     ident[:], [b_xtok[i], b_const], [psb[pi]])
                    if k % 2 == 0:
                        kb.op("act", lambda e, k=k, pi=pi, i=i: e.activation(out=xn[i][:, k, :], in_=pst[pi][:], func=AF.Copy),
                              [psb[pi]], [b_xn[i]])
                    else:
                        kb.op("dve", lambda e, k=k, pi=pi, i=i: e.tensor_copy(out=xn[i][:, k, :], in_=pst[pi][:]),
                              [psb[pi]], [b_xn[i]])
                finish(fin, tt, xn[i], b_xn[i], 0, 0, 4 + (tt % 2))
                for _ in range(3):
                    if _ADA_PH[0][1]:
                        _ADA_PH[0][1].pop(0)()
            while _ADA_PH[0][1]:
                _ADA_PH[0][1].pop(0)()
            kb.barrier()
        _ADA_PH[0][0].close()

        def load_h_all(ph, name="hall"):
            hall = ph.enter_context(nc.sbuf_tensor(uq(name), [128, 8, NT], BF16))
            b_hall = [kb.buf("%s%d" % (name, t)) for t in range(NTT)]
            for tt in range(NTT):
                cols = slice(tt * TT, (tt + 1) * TT)
                kb.dma("sp", hall[:, :, cols], HTv[:, :, cols], [], [b_hall[tt]], b_hall[tt])
            return hall, b_hall

        def ffn(l, l_next, sub_next):
            with ExitStack() as pho:
                wo = pho.enter_context(nc.sbuf_tensor(uq("wo"), [128, 22, D], BF16))
                b_wo = kb.buf("wo")
                wov = w_ffn_out[l].rearrange("(k p) n -> p k n", p=128)
                with ExitStack() as ph:
                    psb_ = lambda name, shape, dtype: ph.enter_context(nc.sbuf_tensor(uq(name), list(shape), dtype))
                    hall, b_hall = load_h_all(ph)
                    NS = 3
                    wr = [psb_("wf%d" % i, [128, 8, 256], BF16) for i in range(NS)]
                    b_wr = [kb.buf("wf%d" % i) for i in range(NS)]
                    gs = [psb_("gs%d" % i, [128, TT], F32) for i in range(2)]
                    b_gs = [kb.buf("gs%d" % i) for i in range(2)]
                    ho = [psb_("ho%d" % i, [128, NT], BF16) for i in range(2)]
                    b_ho = [kb.buf("ho%d" % i) for i in range(2)]
                    for m in range(22):
                        s = m % NS
                        kb.dma("pool", wr[s][:].rearrange("p k n -> p (k n)"), w_ffn_in[l, m], [], [b_wr[s]], b_wr[s], max_dma_last_dim=4096)
                        if 3 <= m < 14:
                            k0 = 2 * (m - 3)
                            kb.dma("pool", wo[:, k0:k0 + 2, :], wov[:, k0:k0 + 2, :], [], [b_wo], b_wo)
                        oi = m % 2
                        for tt in range(NTT):
                            cols = slice(tt * TT, (tt + 1) * TT)
                            pg = (2 * tt) % 8
                            pu = pg + 1
                            kb.mm(pst[pg][:], [(wr[s][:, k, 0:128], hall[:, k, cols]) for k in range(KT)],
                                  [b_wr[s], b_hall[tt]], [psb[pg]])
                            kb.mm(pst[pu][:], [(wr[s][:, k, 128:256], hall[:, k, cols]) for k in range(KT)],
                                  [b_wr[s], b_hall[tt]], [psb[pu]])
                            gi = tt % 2
                            kb.op("act", lambda e, pg=pg, gi=gi: e.activation(out=gs[gi][:], in_=pst[pg][:], func=AF.Silu),
                                  [psb[pg]], [b_gs[gi]])
                            kb.op("dve", lambda e, pu=pu, gi=gi, oi=oi, cols=cols: e.tensor_tensor(
                                out=ho[oi][:, cols], in0=pst[pu][:], in1=gs[gi][:], op=ALU.mult),
                                [psb[pu], b_gs[gi]], [b_ho[oi]])
                        kb.dma("sp", HIDv[:, m, :], ho[oi][:], [b_ho[oi]], [], b_ho[oi])
                    kb.barrier()
                with ExitStack() as ph:
                    psb_ = lambda name, shape, dtype: ph.enter_context(nc.sbuf_tensor(uq(name), list(shape), dtype))
                    fin = make_fin(ph, final=(l_next is None))
                    hid = [psb_("hid%d" % i, [128, 22, TT], BF16) for i in range(2)]
                    b_hid = [kb.buf("hid%d" % i) for i in range(2)]
                    NBX = 3
                    xt_ = [psb_("xt%d" % i, [128, 8, TT], F32) for i in range(NBX)]
                    b_xt = [kb.buf("xt%d" % i) for i in range(NBX)]

                    def load(tt):
                        cols = slice(tt * TT, (tt + 1) * TT)
                        kb.dma("sp", hid[tt % 2][:], HIDv[:, :, cols], [], [b_hid[tt % 2]], b_hid[tt % 2])
                        kb.dma("sp", xt_[tt % NBX][:], XTv[:, :, cols], [], [b_xt[tt % NBX]], b_xt[tt % NBX])
                    load(0)
                    for tt in range(NTT):
                        i = tt % 2
                        ix = tt % NBX
                        j = 0 if tt < NPT else 1
                        if tt + 1 < NTT:
                            load(tt + 1)
                        for m in range(KT):
                            pi = m % 6
                            kb.mm(pst[pi][:], [(wo[:, k, m * 128:(m + 1) * 128], hid[i][:, k, :]) for k in range(22)],
                                  [b_wo, b_hid[i]], [psb[pi]])
                            kb.op("dve", lambda e, m=m, pi=pi, ix=ix, j=j: e.scalar_tensor_tensor(
                                out=xt_[ix][:, m, :], in0=pst[pi][:], scalar=mod_ap(l, 5, m, j), in1=xt_[ix][:, m, :],
                                op0=ALU.mult, op1=ALU.add), [psb[pi], b_xt[ix], b_mod], [b_xt[ix]])
                        if l_next is None:
                            final_out(fin, tt, xt_[ix], b_xt[ix], ph)
                        else:
                            finish(fin, tt, xt_[ix], b_xt[ix], l_next, sub_next, 6 + (tt % 2))
                    kb.barrier()

        fo = {}

        def final_out(fin, tt, xn, b_xn, ph):
            if "y" not in fo:
                fo["y"] = [ph.enter_context(nc.sbuf_tensor(uq("fo_y%d" % i), [128, 8, TT], F32)) for i in range(1)]
                fo["b_y"] = [kb.buf("fo_y%d" % i) for i in range(1)]
                fo["o"] = [ph.enter_context(nc.sbuf_tensor(uq("fo_o%d" % i), [128, 4, D], F32)) for i in range(1)]
                fo["b_o"] = [kb.buf("fo_o%d" % i) for i in range(1)]
            finish(fin, tt, xn, b_xn, None, None, 6 + (tt % 2))
            y = fo["y"][0]
            b_y = fo["b_y"][0]
            o = fo["o"][0]
            b_o = fo["b_o"][0]
            for k in range(KT):
                kb.op("dve", lambda e, k=k: e.scalar_tensor_tensor(
                    out=y[:, k, :], in0=xn[:, k, :], scalar=gft[:, k:k + 1], in1=fin.rstd[:],
                    op0=ALU.mult, op1=ALU.mult), [b_xn, fin.b_rstd, b_const], [b_y])
            for a in range(4):
                for k0 in range(0, KT, 4):
                    pi = (a * 2 + k0 // 4) % 6
                    for kk in range(4):
                        k = k0 + kk
                        kb.transpose(pst[pi][:, kk * 128:(kk + 1) * 128], y[:, k, a * 128:(a + 1) * 128], ident[:],
                                     [b_y, b_const], [psb[pi]])
                    if (a + k0 // 4) % 2 == 0:
                        kb.op("act", lambda e, a=a, k0=k0, pi=pi: e.activation(
                            out=o[:, a, k0 * 128:(k0 + 4) * 128], in_=pst[pi][:], func=AF.Copy), [psb[pi]], [b_o])
                    else:
                        kb.op("dve", lambda e, a=a, k0=k0, pi=pi: e.tensor_copy(
                            out=o[:, a, k0 * 128:(k0 + 4) * 128], in_=pst[pi][:]), [psb[pi]], [b_o])
            dst = yp if tt < NPT else ys
            r0 = tt * TT if tt < NPT else (tt - NPT) * TT
            kb.dma("sp", dst[r0:r0 + TT, :].rearrange("(a p) d -> p a d", p=128), o[:], [b_o], [], b_o)

        def even_mixer(l):
            i2 = l // 2
            with ExitStack() as ph:
                psb_ = lambda name, shape, dtype: ph.enter_context(nc.sbuf_tensor(uq(name), list(shape), dtype))
                fall = psb_("fall", [128, 4, NT], BF16)
                b_fall = [kb.buf("fall%d" % g) for g in range(4)]
                with ExitStack() as ph1:
                    p1 = lambda name, shape, dtype: ph1.enter_context(nc.sbuf_tensor(uq(name), list(shape), dtype))
                    hall, b_hall = load_h_all(ph1)
                    NS = 3
                    wr = [p1("wm%d" % i, [128, 8, 384], BF16) for i in range(NS)]
                    b_wr = [kb.buf("wm%d" % i) for i in range(NS)]
                    wc = p1("wc", [128, 4, 3], F32)
                    b_wc = kb.buf("wc")
                    with nc.allow_non_contiguous_dma(reason="tiny conv weight load"):
                        for t_ in range(3):
                            for c_ in range(4):
                                kb.dma("sp", wc[:, c_, t_:t_ + 1], w_conv[i2, t_, c_ * 128:(c_ + 1) * 128].rearrange("(p o) -> p o", o=1),
                                       [], [b_wc], b_wc)
                    vS = [p1("vS%d" % i, [128, TT], F32) for i in range(2)]
                    b_vS = [kb.buf("vS%d" % i) for i in range(2)]
                    tS = [p1("tS%d" % i, [128, TT], F32) for i in range(2)]
                    b_tS = [kb.buf("tS%d" % i) for i in range(2)]
                    cS = [p1("cS%d" % i, [128, TT], F32) for i in range(2)]
                    b_cS = [kb.buf("cS%d" % i) for i in range(2)]
                    yo = [p1("yo%d" % i, [128, NT], BF16) for i in range(2)]
                    b_yo = [kb.buf("yo%d" % i) for i in range(2)]
                    cnt = 0
                    for c in range(4):
                        s = c % NS
                        kb.dma("pool", wr[s][:].rearrange("p k n -> p (k n)"), w_in_mix[i2, c], [], [b_wr[s]], b_wr[s], max_dma_last_dim=4096)
                        oi = c % 2
                        for tt in range(NTT):
                            cols = slice(tt * TT, (tt + 1) * TT)
                            rl = 256 if tt < NPT else 64
                            nr = TT // rl
                            pa = (3 * cnt) % 6
                            cnt += 1
                            for part in range(3):
                                kb.mm(pst[pa + part][:], [(wr[s][:, k, part * 128:(part + 1) * 128], hall[:, k, cols]) for k in range(KT)],
                                      [b_wr[s], b_hall[tt]], [psb[pa + part]])
                            bi = tt % 2
                            kb.op("act", lambda e, pa=pa, bi=bi: e.activation(out=vS[bi][:], in_=pst[pa + 1][:], func=AF.Copy),
                                  [psb[pa + 1]], [b_vS[bi]])
                            kb.op("dve", lambda e, pa=pa, bi=bi: e.tensor_tensor(out=tS[bi][:], in0=pst[pa][:], in1=vS[bi][:], op=ALU.mult),
                                  [psb[pa], b_vS[bi]], [b_tS[bi]])
                            kb.op("act", lambda e, bi=bi, c=c: e.activation(out=cS[bi][:], in_=tS[bi][:], func=AF.Copy,
                                                                          scale=wc[:, c, 1:2]), [b_tS[bi], b_wc], [b_cS[bi]])
                            t3 = tS[bi][:].rearrange("p (r w) -> p r w", w=rl)
                            c3 = cS[bi][:].rearrange("p (r w) -> p r w", w=rl)
                            kb.op("dve", lambda e, t3=t3, c3=c3, c=c, rl=rl: e.scalar_tensor_tensor(
                                out=c3[:, :, 1:rl], in0=t3[:, :, 0:rl - 1], scalar=wc[:, c, 0:1], in1=c3[:, :, 1:rl],
                                op0=ALU.mult, op1=ALU.add), [b_tS[bi], b_cS[bi], b_wc], [b_cS[bi]])
                            kb.op("dve", lambda e, t3=t3, c3=c3, c=c, rl=rl: e.scalar_tensor_tensor(
                                out=c3[:, :, 0:rl - 1], in0=t3[:, :, 1:rl], scalar=wc[:, c, 2:3], in1=c3[:, :, 0:rl - 1],
                                op0=ALU.mult, op1=ALU.add), [b_tS[bi], b_cS[bi], b_wc], [b_cS[bi]])
                            kb.op("dve", lambda e, pa=pa, bi=bi, oi=oi, cols=cols: e.tensor_tensor(
                                out=yo[oi][:, cols], in0=pst[pa + 2][:], in1=cS[bi][:], op=ALU.mult),
                                [psb[pa + 2], b_cS[bi]], [b_yo[oi]])
                        kb.dma("sp", YCv[:, c, :], yo[oi][:], [b_yo[oi]], [], b_yo[oi])
                    for g in range(4):
                        s = (4 + g) % NS
                        kb.dma("pool", wr[s][:].rearrange("p k n -> p (k n)"), w_in_mix[i2, 4 + g], [], [b_wr[s]], b_wr[s], max_dma_last_dim=4096)
                        for tt in range(NTT):
                            cols = slice(tt * TT, (tt + 1) * TT)
                            pa = 6 + (tt % 2)
                            kb.mm(pst[pa][:], [(wr[s][:, k, 0:128], hall[:, k, cols]) for k in range(KT)],
                                  [b_wr[s], b_hall[tt]], [psb[pa]])
                            kb.op("act", lambda e, pa=pa, g=g, cols=cols: e.activation(out=fall[:, g, cols], in_=pst[pa][:], func=AF.Copy),
                                  [psb[pa]], [b_fall[g]])
                    kb.barrier()
                with ExitStack() as ph2:
                    p2 = lambda name, shape, dtype: ph2.enter_context(nc.sbuf_tensor(uq(name), list(shape), dtype))
                    cs128 = p2("cs128", [128, 256], BF16)
                    dftp = p2("dftp", [128, 2, 2, 256], BF16)
                    b_tab = kb.buf("ftab")
                    kb.dma("sp", cs128[:], cs128_d, [], [b_tab], b_tab)
                    for cs_ in range(2):
                        kb.dma("sp", dftp[:, cs_], dftp_d[cs_].rearrange("(a p) k -> p a k", p=128), [], [b_tab], b_tab)
                    ftok = p2("ftok", [128, 32, 4, 256], BF16)
                    b_ftok = [kb.buf("ftok%d" % t) for t in range(32)]
                    def chan_dft(lt0, n):
                        for sl in range(n):
                            lt = lt0 + sl
                            for gp in range(2):
                                pa = (lt * 2 + gp) % 4
                                for gg in range(2):
                                    g = gp * 2 + gg
                                    kb.mm(pst[pa][:, gg * 256:(gg + 1) * 256], [(fall[:, g, lt * 128:(lt + 1) * 128], cs128[:])],
                                          [b_fall[g], b_tab], [psb[pa]])
                                if gp == 0:
                                    kb.op("act", lambda e, pa=pa, sl=sl, gp=gp: e.activation(
                                        out=ftok[:, sl, 2 * gp:2 * gp + 2, :].rearrange("p g c -> p (g c)"), in_=pst[pa][:], func=AF.Copy),
                                        [psb[pa]], [b_ftok[sl]])
                                else:
                                    kb.op("dve", lambda e, pa=pa, sl=sl, gp=gp: e.tensor_copy(
                                        out=ftok[:, sl, 2 * gp:2 * gp + 2, :].rearrange("p g c -> p (g c)"), in_=pst[pa][:]),
                                        [psb[pa]], [b_ftok[sl]])
                    yf = [p2("yf%d" % i, [128, 4, 256], BF16) for i in range(2)]
                    b_yf = [kb.buf("yf%d" % i) for i in range(2)]
                    nyf = 0
                    sc_p = 1.0 / math.sqrt(256 * 128)
                    chan_dft(0, 8)
                    for sq in range(4):
                        yi = nyf % 2
                        nyf += 1
                        for g in range(4):
                            pa = 4 + (g % 4)
                            pairs = []
                            for a in range(2):
                                lt = sq * 2 + a
                                pairs.append((ftok[:, lt, g, 0:128], dftp[:, 0, a, :]))
                                pairs.append((ftok[:, lt, g, 128:256], dftp[:, 1, a, :]))
                            kb.mm(pst[pa][:, 0:256], pairs, [b_ftok[sq * 2], b_ftok[sq * 2 + 1], b_tab], [psb[pa]])
                            kb.op("act", lambda e, pa=pa, g=g, yi=yi: e.activation(out=yf[yi][:, g, :], in_=pst[pa][:, 0:256],
                                                                                func=AF.Copy, scale=sc_p), [psb[pa]], [b_yf[yi]])
                        kb.dma("sp", YCv[:, 4:8, sq * 256:(sq + 1) * 256], yf[yi][:], [b_yf[yi]], [], b_yf[yi])
                    sc_s = 1.0 / math.sqrt(4096 * 128)
                    chan_dft(8, 32)
                    tb = [p2("dft%d" % i, [128, 32, 512], BF16) for i in range(2)]
                    tbv = [t_[:] for t_ in tb] + [fall[:].rearrange("p g n -> p (g n)")[:, 0:16384].rearrange("p (a k) -> p a k", a=32)]
                    b_tb = [kb.buf("dft%d" % i) for i in range(3)]
                    yfs = [p2("yfs%d" % i, [128, 4, 512], BF16) for i in range(2)]
                    b_yfs = [kb.buf("yfs%d" % i) for i in range(2)]
                    nld = 0
                    for kbk in range(8):
                        pbase = 4 * (kbk % 2)
                        for cs_ in range(2):
                            ti = nld % 3
                            extra = b_fall if nld == 2 else []
                            nld += 1
                            tfl = tbv[ti].rearrange("p a k -> p (a k)")
                            for q_ in range(4):
                                kb.dma("sp", tfl[:, q_ * 4096:(q_ + 1) * 4096], dfts_d[kbk, cs_, :, q_ * 4096:(q_ + 1) * 4096],
                                       [], [b_tb[ti]] + extra, b_tb[ti])
                            for g in range(4):
                                pa = pbase + g
                                pairs = [(ftok[:, a, g, cs_ * 128:(cs_ + 1) * 128], tbv[ti][:, a, :]) for a in range(32)]
                                kb.mm(pst[pa][:], pairs, b_ftok[0:32] + [b_tb[ti]], [psb[pa]], start=(cs_ == 0), stop=(cs_ == 1))
                        yi = kbk % 2
                        for g in range(4):
                            pa = pbase + g
                            kb.op("act", lambda e, pa=pa, g=g, yi=yi: e.activation(out=yfs[yi][:, g, :], in_=pst[pa][:],
                                                                                func=AF.Copy, scale=sc_s), [psb[pa]], [b_yfs[yi]])
                        kb.dma("sp", YCv[:, 4:8, 1024 + kbk * 512:1024 + (kbk + 1) * 512], yfs[yi][:], [b_yfs[yi]], [], b_yfs[yi])
                    kb.barrier()
            out_proj(l, w_out_mix[i2], glu=False)

        def out_proj(l, w_ap, glu):
            ncol = 2 * D if glu else D
            with ExitStack() as ph:
                psb_ = lambda name, shape, dtype: ph.enter_context(nc.sbuf_tensor(uq(name), list(shape), dtype))
                fin = make_fin(ph)
                wo = psb_("wom", [128, 8, ncol], BF16)
                b_wo = kb.buf("wom")
                wov = w_ap.rearrange("(k p) n -> p k n", p=128)
                for k0 in range(0, 8, 2):
                    kb.dma("pool", wo[:, k0:k0 + 2, :], wov[:, k0:k0 + 2, :], [], [b_wo], b_wo)
                NB = 3
                yc = [psb_("yc%d" % i, [128, 8, TT], BF16) for i in range(NB)]
                b_yc = [kb.buf("yc%d" % i) for i in range(NB)]
                xt_ = [psb_("xt%d" % i, [128, 8, TT], F32) for i in range(NB)]
                b_xt = [kb.buf("xt%d" % i) for i in range(NB)]
                sg = [psb_("sg%d" % i, [128, TT], F32) for i in range(2)]
                b_sg = [kb.buf("sg%d" % i) for i in range(2)]

                def load(tt):
                    i = tt % NB
                    cols = slice(tt * TT, (tt + 1) * TT)
                    kb.dma("sp", yc[i][:], YCv[:, :, cols], [], [b_yc[i]], b_yc[i])
                    kb.dma("sp", xt_[i][:], XTv[:, :, cols], [], [b_xt[i]], b_xt[i])
                load(0)
                for tt in range(NTT):
                    i = tt % NB
                    cols = slice(tt * TT, (tt + 1) * TT)
                    j = 0 if tt < NPT else 1
                    if tt + 1 < NTT:
                        load(tt + 1)
                    for m in range(KT):
                        pi = (2 * m) % 6
                        kb.mm(pst[pi][:], [(wo[:, k, m * 128:(m + 1) * 128], yc[i][:, k, :]) for k in range(KT)],
                              [b_wo, b_yc[i]], [psb[pi]])
                        if glu:
                            kb.mm(pst[pi + 1][:], [(wo[:, k, D + m * 128:D + (m + 1) * 128], yc[i][:, k, :]) for k in range(KT)],
                                  [b_wo, b_yc[i]], [psb[pi + 1]])
                            si = m % 2
                            kb.op("act", lambda e, pi=pi, si=si: e.activation(out=sg[si][:], in_=pst[pi + 1][:], func=AF.Sigmoid),
                                  [psb[pi + 1]], [b_sg[si]])
                            kb.op("dve", lambda e, pi=pi, si=si: e.tensor_tensor(out=sg[si][:], in0=pst[pi][:], in1=sg[si][:], op=ALU.mult),
                                  [psb[pi], b_sg[si]], [b_sg[si]])
                            kb.op("dve", lambda e, m=m, si=si, i=i, j=j: e.scalar_tensor_tensor(
                                out=xt_[i][:, m, :], in0=sg[si][:], scalar=mod_ap(l, 2, m, j), in1=xt_[i][:, m, :],
                                op0=ALU.mult, op1=ALU.add), [b_sg[si], b_xt[i], b_mod], [b_xt[i]])
                        else:
                            kb.op("dve", lambda e, m=m, pi=pi, i=i, j=j: e.scalar_tensor_tensor(
                                out=xt_[i][:, m, :], in0=pst[pi][:], scalar=mod_ap(l, 2, m, j), in1=xt_[i][:, m, :],
                                op0=ALU.mult, op1=ALU.add), [psb[pi], b_xt[i], b_mod], [b_xt[i]])
                    finish(fin, tt, xt_[i], b_xt[i], l, 1, 6 + (tt % 2))
                kb.barrier()

        I32 = mybir.dt.int32
        r8t = sb("r8t", [128, 2, 64], F32)
        e1r = sb("e1r", [128, 2, 64], F32)
        e1i = sb("e1i", [128, 2, 64], F32)
        h0r = sb("h0r", [128, 2, 64], F32)
        h0i = sb("h0i", [128, 2, 64], F32)
        identb = sb("identb", [128, 128], BF16)
        b_s5p = kb.buf("s5p")
        kb.op("dve", lambda e: e.tensor_copy(out=identb[:], in_=ident[:]), [b_const], [b_s5p])

        def AP3(a, dims):
            return bass.AP(tensor=a.tensor, offset=a.offset, ap=[list(a.ap[0])] + [list(d_) for d_ in dims])

        def bc_last(a, n):
            return AP3(a, [a.ap[1], [0, n]])

        def bc_mid(a, n):
            return AP3(a, [[0, n], a.ap[1]])

        PI = float(np.pi)
        import os as _os
        _STOP = int(_os.environ.get('S5STOP', '0'))

        def s5_setup(li):
            with ExitStack() as ph:
                def T(name, shape, dtype=F32):
                    return ph.enter_context(nc.sbuf_tensor(uq(name), list(shape), dtype))
                B = kb.buf("s5set")
                R, W = [B, b_const, b_s5p], [B]

                def dve(fn):
                    kb.op("dve", fn, R, W)

                def act(fn):
                    kb.op("act", fn, R, W)
                ps1 = pst[1]

                Bpr = T("Bpr", [128, 64, 16]); Bpi = T("Bpi", [128, 64, 16])
                b_in = kb.buf("s5in")
                stgA = T("stgA", [128, 64]); stgB = T("stgB", [128, 64]); stgH = T("stgH", [128, 128])
                cstA = [T("cst%d" % i_, [128, 8, 64]) for i_ in range(4)]
                stD = T("stD", [64, 16])
                msk = T("msk", [128, 2, 128])
                LS = T("LS", [128, 64])
                kb.dma("sp", stgA[:], lam_re[li].rearrange("d g p -> (d g) p"), [], [b_in], b_in)
                kb.dma("sp", stgB[:], lam_im[li].rearrange("d g p -> (d g) p"), [], [b_in], b_in)
                for d_ in range(2):
                    kb.dma("sp", LS[d_ * 64:(d_ + 1) * 64, :], log_step[li, d_, :].partition_broadcast(64), [], [b_in], b_in)
                kb.dma("sp", stgH[:], st_in[li].rearrange("d g p c -> (d g) (p c)"), [], [b_in], b_in)
                kb.dma("sp", stD[:], ssm_d[li].rearrange("(g h) -> g h", h=16), [], [b_in], b_in)
                for m_ in range(2):
                    kb.dma("sp", msk[:, m_, :], mask_d[m_], [], [b_in], b_in)
                for d_ in range(2):
                    sl = slice(d_ * 64, (d_ + 1) * 64)
                    for dst, src in ((Bpr, b_re), (Bpi, b_im)):
                        for g0 in range(0, 64, 16):
                            kb.dma("sp", dst[sl, g0:g0 + 16, :], src[li, d_, g0:g0 + 16].rearrange("g p h -> p g h"), [], [b_in], b_in)
                for ci_, (src, d_) in enumerate(((c_re, 0), (c_re, 1), (c_im, 0), (c_im, 1))):
                    kb.dma("sp", cstA[ci_][:], src[li, d_].rearrange("(t g) h p -> (g h) t p", g=8), [], [b_in], b_in)
                R = R + [b_in]

                def tr_halves(dst, stg_):
                    for d_ in range(2):
                        sl = slice(d_ * 64, (d_ + 1) * 64)
                        kb.mm(ps1[sl, 0:64], [(stg_[sl, 0:64], ident[sl, sl])], R, [psb[1]])
                    kb.op("dve", lambda e: e.tensor_copy(out=dst, in_=ps1[:, 0:64]), R + [psb[1]], W)
                LR = T("LR", [128, 64]); LI = T("LI", [128, 64])
                tr_halves(LR[:], stgA)
                tr_halves(LI[:], stgB)
                if _STOP == 1:
                    kb.barrier()
                    return

                stp = T("stp", [128, 64]); lr = T("lr", [128, 64]); Are = T("Are", [128, 64]); Aim = T("Aim", [128, 64])
                mag = T("mag", [128, 64]); kf = T("kf", [128, 64]); ki = T("ki", [128, 64], I32); red = T("red", [128, 64])
                sn = T("sn", [128, 64]); cs = T("cs", [128, 64]); P1r = T("P1r", [128, 64]); P1i = T("P1i", [128, 64])
                u1 = T("u1", [128, 64]); u2 = T("u2", [128, 64])
                act(lambda e: e.activation(out=stp[:], in_=LS[:], func=AF.Exp))
                dve(lambda e: e.tensor_scalar(out=lr[:], in0=LR[:], scalar1=-1e-4, scalar2=None, op0=ALU.min))
                dve(lambda e: e.tensor_tensor(out=Are[:], in0=lr[:], in1=stp[:], op=ALU.mult))
                dve(lambda e: e.tensor_tensor(out=Aim[:], in0=LI[:], in1=stp[:], op=ALU.mult))
                act(lambda e: e.activation(out=mag[:], in_=Are[:], func=AF.Exp))

                def wrap(x):
                    dve(lambda e: e.tensor_scalar(out=kf[:], in0=x, scalar1=PI, scalar2=None, op0=ALU.is_gt))
                    dve(lambda e: e.scalar_tensor_tensor(out=x, in0=kf[:], scalar=-2 * PI, in1=x, op0=ALU.mult, op1=ALU.add))
                    dve(lambda e: e.tensor_scalar(out=kf[:], in0=x, scalar1=-PI, scalar2=None, op0=ALU.is_lt))
                    dve(lambda e: e.scalar_tensor_tensor(out=x, in0=kf[:], scalar=2 * PI, in1=x, op0=ALU.mult, op1=ALU.add))
                dve(lambda e: e.tensor_scalar(out=kf[:], in0=Aim[:], scalar1=1.0 / (2 * PI), scalar2=None, op0=ALU.mult))
                dve(lambda e: e.tensor_copy(out=ki[:], in_=kf[:]))
                dve(lambda e: e.tensor_copy(out=u1[:], in_=ki[:]))
                dve(lambda e: e.scalar_tensor_tensor(out=red[:], in0=u1[:], scalar=-2 * PI, in1=Aim[:], op0=ALU.mult, op1=ALU.add))
                wrap(red[:])
                wrap(red[:])
                act(lambda e: e.activation(out=sn[:], in_=red[:], func=AF.Sin))
                dve(lambda e: e.tensor_scalar(out=red[:], in0=red[:], scalar1=PI / 2, scalar2=None, op0=ALU.add))
                wrap(red[:])
                act(lambda e: e.activation(out=cs[:], in_=red[:], func=AF.Sin))
                dve(lambda e: e.tensor_tensor(out=P1r[:], in0=mag[:], in1=cs[:], op=ALU.mult))
                dve(lambda e: e.tensor_tensor(out=P1i[:], in0=mag[:], in1=sn[:], op=ALU.mult))
                if _STOP == 2:
                    kb.barrier()
                    return


                def cmul(eng, outr, outi, ar, ai, br, bi, t1, t2, neg_i=False, R_=None, W_=None):
                    o = lambda fn: kb.op(eng, fn, R if R_ is None else R_, W if W_ is None else W_)
                    o(lambda e: e.tensor_tensor(out=t1, in0=ar, in1=br, op=ALU.mult))
                    o(lambda e: e.tensor_tensor(out=t2, in0=ai, in1=bi, op=ALU.mult))
                    o(lambda e: e.tensor_tensor(out=outr, in0=t1, in1=t2, op=ALU.subtract))
                    o(lambda e: e.tensor_tensor(out=t1, in0=ar, in1=bi, op=ALU.mult))
                    o(lambda e: e.tensor_tensor(out=t2, in0=ai, in1=br, op=ALU.mult))
                    if neg_i:
                        o(lambda e: e.scalar_tensor_tensor(out=outi, in0=t1, scalar=-1.0, in1=t2, op0=ALU.mult, op1=ALU.subtract))
                    else:
                        o(lambda e: e.tensor_tensor(out=outi, in0=t1, in1=t2, op=ALU.add))
                den = T("den", [128, 64]); cr = T("cr", [128, 64]); ci = T("ci", [128, 64]); pm1 = T("pm1", [128, 64])
                dve(lambda e: e.tensor_tensor(out=u1[:], in0=lr[:], in1=lr[:], op=ALU.mult))
                dve(lambda e: e.tensor_tensor(out=u2[:], in0=LI[:], in1=LI[:], op=ALU.mult))
                dve(lambda e: e.tensor_tensor(out=den[:], in0=u1[:], in1=u2[:], op=ALU.add))
                dve(lambda e: e.reciprocal(out=den[:], in_=den[:]))
                dve(lambda e: e.tensor_scalar(out=pm1[:], in0=P1r[:], scalar1=-1.0, scalar2=None, op0=ALU.add))
                dve(lambda e: e.tensor_tensor(out=u1[:], in0=pm1[:], in1=lr[:], op=ALU.mult))
                dve(lambda e: e.tensor_tensor(out=u2[:], in0=P1i[:], in1=LI[:], op=ALU.mult))
                dve(lambda e: e.tensor_tensor(out=cr[:], in0=u1[:], in1=u2[:], op=ALU.add))
                dve(lambda e: e.tensor_tensor(out=cr[:], in0=cr[:], in1=den[:], op=ALU.mult))
                dve(lambda e: e.tensor_tensor(out=u1[:], in0=P1i[:], in1=lr[:], op=ALU.mult))
                dve(lambda e: e.tensor_tensor(out=u2[:], in0=pm1[:], in1=LI[:], op=ALU.mult))
                dve(lambda e: e.tensor_tensor(out=ci[:], in0=u1[:], in1=u2[:], op=ALU.subtract))
                dve(lambda e: e.tensor_tensor(out=ci[:], in0=ci[:], in1=den[:], op=ALU.mult))
                ivr = T("ivr", [128, 64]); ivi = T("ivi", [128, 64]); im2 = T("im2", [128, 64])
                act(lambda e: e.activation(out=im2[:], in_=Are[:], func=AF.Exp, scale=-2.0))
                dve(lambda e: e.tensor_tensor(out=ivr[:], in0=P1r[:], in1=im2[:], op=ALU.mult))
                dve(lambda e: e.scalar_tensor_tensor(out=ivi[:], in0=P1i[:], scalar=-1.0, in1=im2[:], op0=ALU.mult, op1=ALU.mult))
                M1r = T("M1r", [128, 64]); M1i = T("M1i", [128, 64]); Mvr = T("Mvr", [128, 64]); Mvi = T("Mvi", [128, 64])
                fh, bh = slice(0, 64), slice(64, 128)
                for dst, sf, sb_ in ((M1r, ivr, P1r), (M1i, ivi, P1i), (Mvr, P1r, ivr), (Mvi, P1i, ivi)):
                    dve(lambda e, dst=dst, sf=sf: e.tensor_copy(out=dst[fh, :], in_=sf[fh, :]))
                    dve(lambda e, dst=dst, sb_=sb_: e.tensor_copy(out=dst[bh, :], in_=sb_[bh, :]))
                Rr = T("Rr", [128, 9, 64]); Ri = T("Ri", [128, 9, 64]); Tr = T("Tr", [128, 9, 64]); Ti = T("Ti", [128, 9, 64])
                for X, v in ((Rr, 1.0), (Ri, 0.0), (Tr, 1.0), (Ti, 0.0)):
                    dve(lambda e, X=X, v=v: e.memset(X[:, 0, :], v))
                for j in range(8):
                    cmul("dve", Rr[:, j + 1, :], Ri[:, j + 1, :], Rr[:, j, :], Ri[:, j, :], M1r[:], M1i[:], u1[:], u2[:])
                    cmul("dve", Tr[:, j + 1, :], Ti[:, j + 1, :], Tr[:, j, :], Ti[:, j, :], Mvr[:], Mvi[:], u1[:], u2[:])
                A7r = T("A7r", [128, 64]); A7i = T("A7i", [128, 64]); C7r = T("C7r", [128, 64]); C7i = T("C7i", [128, 64])
                Cpr = T("Cpr", [128, 64]); Cpi = T("Cpi", [128, 64]); L8r = T("L8r", [128, 64]); L8i = T("L8i", [128, 64])
                for X, v in ((A7r, 1.0), (A7i, 0.0), (C7r, 1.0), (C7i, 0.0)):
                    dve(lambda e, X=X, v=v: e.memset(X[:], v))
                for dst, src, hs in ((A7r, Tr[:, 7, :], fh), (A7i, Ti[:, 7, :], fh), (C7r, Rr[:, 7, :], fh), (C7i, Ri[:, 7, :], fh),
                                     (Cpr, Tr[:, 1, :], fh), (Cpi, Ti[:, 1, :], fh), (Cpr, Rr[:, 8, :], bh), (Cpi, Ri[:, 8, :], bh),
                                     (L8r, Tr[:, 8, :], fh), (L8i, Ti[:, 8, :], fh), (L8r, Rr[:, 8, :], bh), (L8i, Ri[:, 8, :], bh)):
                    dve(lambda e, dst=dst, src=src, hs=hs: e.tensor_copy(out=dst[hs, :], in_=src[hs, :]))
                act(lambda e: e.activation(out=r8t[:, li, :], in_=Are[:], func=AF.Exp, scale=8.0))
                act(lambda e: e.activation(out=im2[:], in_=Are[:], func=AF.Exp, scale=-8.0))
                dve(lambda e: e.tensor_tensor(out=e1r[:, li, :], in0=L8r[:], in1=im2[:], op=ALU.mult))
                dve(lambda e: e.scalar_tensor_tensor(out=e1i[:, li, :], in0=L8i[:], scalar=-1.0, in1=im2[:], op0=ALU.mult, op1=ALU.mult))
                for d_ in range(2):
                    sl = slice(d_ * 64, (d_ + 1) * 64)
                    for c_ in range(2):
                        kb.mm(ps1[sl, c_ * 64:(c_ + 1) * 64], [(stgH[sl, c_:128:2], ident[sl, sl])], R, [psb[1]])
                dve(lambda e: e.tensor_copy(out=h0r[:, li, :], in_=ps1[:, 0:64]))
                dve(lambda e: e.tensor_copy(out=h0i[:, li, :], in_=ps1[:, 64:128]))
                if _STOP == 3:
                    kb.barrier()
                    return

                stD2 = T("stD2", [64, 8, 16]); D8 = T("D8", [128, 64])
                dve(lambda e: e.tensor_copy(out=stD2[:], in_=bc_mid(stD[:], 8)))
                kb.mm(ps1[:, 0:64], [(stD2[:].rearrange("p j h -> p (j h)"), ident[0:64, 0:64])], R, [psb[1]])
                dve(lambda e: e.tensor_copy(out=D8[:], in_=ps1[:, 0:64]))
                if _STOP == 4:
                    kb.barrier()
                    return

                Cpr_ = T("Cpr_", [128, 64, 16]); Cpi_ = T("Cpi_", [128, 64, 16])
                w1 = T("w1", [128, 64, 16]); w2 = T("w2", [128, 64, 16]); Xr = T("Xr", [128, 64, 16]); Xi = T("Xi", [128, 64, 16])
                Yr = T("Yr", [128, 64, 16]); Yi = T("Yi", [128, 64, 16])
                for ri_, dst in enumerate((Cpr_, Cpi_)):
                    for d_ in range(2):
                        sl = slice(d_ * 64, (d_ + 1) * 64)
                        cst = cstA[ri_ * 2 + d_]
                        for t0 in range(0, 8, 4):
                            for tt_ in range(4):
                                kb.mm(big[1][sl, tt_ * 128:(tt_ + 1) * 128], [(cst[:, t0 + tt_, :], ident[:])], R, [psb[2]])
                            dve(lambda e, dst=dst, sl=sl, t0=t0: e.tensor_copy(
                                out=dst[sl, t0 * 8:(t0 + 4) * 8, :].rearrange("p g h -> p (g h)"), in_=big[1][sl, 0:512]))
                cmul("dve", Xr[:], Xi[:], bc_last(cr[:], 16), bc_last(ci[:], 16), Bpr[:], Bpi[:], w1[:], w2[:])
                cmul("dve", Bpr[:], Bpi[:], bc_last(A7r[:], 16), bc_last(A7i[:], 16), Xr[:], Xi[:], w1[:], w2[:])
                cmul("dve", Xr[:], Xi[:], bc_last(C7r[:], 16), bc_last(C7i[:], 16), Cpr_[:], Cpi_[:], w1[:], w2[:])
                cmul("dve", Yr[:], Yi[:], bc_last(Cpr[:], 16), bc_last(Cpi[:], 16), Cpr_[:], Cpi_[:], w1[:], w2[:])
                if _STOP == 5:
                    kb.barrier()
                    return

                BFr = T("BFr", [128, 64, 8, 16], BF16); BFi = T("BFi", [128, 64, 8, 16], BF16)
                CMr = T("CMr", [128, 64, 8, 16], BF16); CMn = T("CMn", [128, 64, 8, 16], BF16)
                CFr = T("CFr", [128, 64, 8, 16], BF16); CFn = T("CFn", [128, 64, 8, 16], BF16)
                w3 = T("w3", [128, 64, 16]); w4 = T("w4", [128, 64, 16])
                b_bf = kb.buf("s5bf")
                b_cm = kb.buf("s5cm")
                for j in range(8):
                    cmul("pool", BFr[:, :, j, :], BFi[:, :, j, :], bc_last(Rr[:, j, :], 16), bc_last(Ri[:, j, :], 16), Bpr[:], Bpi[:], w3[:], w4[:],
                         R_=R + [b_bf], W_=[b_bf])
                    cmul("dve", CMr[:, :, j, :], CMn[:, :, j, :], bc_last(Tr[:, j, :], 16), bc_last(Ti[:, j, :], 16), Xr[:], Xi[:], w1[:], w2[:], neg_i=True,
                         R_=R + [b_cm], W_=[b_cm])
                    cmul("dve", CFr[:, :, j, :], CFn[:, :, j, :], bc_last(Tr[:, j, :], 16), bc_last(Ti[:, j, :], 16), Yr[:], Yi[:], w1[:], w2[:], neg_i=True,
                         R_=R + [b_cm], W_=[b_cm])
                R = R + [b_bf, b_cm]
                tblv = TBL[li]
                zt = T("zt", [128, 16, 128], BF16)
                dve(lambda e: e.memset(zt[:], 0.0))
                for slot, X, hs, ho_ in ((3, CFr, slice(0, 64), slice(64, 128)), (4, CFn, slice(0, 64), slice(64, 128)),
                                         (5, CFr, slice(64, 128), slice(0, 64)), (6, CFn, slice(64, 128), slice(0, 64))):
                    tv = tblv[:, slot].rearrange("g p n -> p g n")
                    kb.dma("sp", tv[hs], X[hs].rearrange("p g j h -> p g (j h)"), R, [], B)
                    for g0 in range(0, 64, 16):
                        kb.dma("sp", tv[ho_, g0:g0 + 16, :], zt[ho_], R, [], B)
                if _STOP == 6:
                    kb.barrier()
                    return

                st3 = [T("st3_%d" % i_, [128, 3, 8, 128], BF16) for i_ in range(2)]
                b_st3 = [kb.buf("st3_%d" % i_) for i_ in range(2)]
                k1 = [T("k1_%d" % i_, [128, 128]) for i_ in range(2)]
                k2 = [T("k2_%d" % i_, [128, 128]) for i_ in range(2)]
                b_k = [kb.buf("k_%d" % i_) for i_ in range(2)]
                cmz = [T("cmz%d" % i_, [128, 4, 128], BF16) for i_ in range(2)]
                b_cmz = [kb.buf("cmz%d" % i_) for i_ in range(2)]
                for i_ in range(2):
                    kb.op("dve", lambda e, i_=i_: e.memset(cmz[i_][:], 0.0), [], [b_cmz[i_]])
                RO = [B, b_const, b_s5p, b_in, b_bf, b_cm]
                for g in range(64):
                    s_ = g % 2
                    x_ = (g // 8) % 2
                    sti = st3[x_]
                    gi = g % 8
                    fl = lambda X: X[:, g].rearrange("p j h -> p (j h)")
                    pT, pK = psb[4 + s_], psb[6 + s_]
                    cT, cK = 512 * s_, 512 * s_
                    kb.mm(big[2][:, cT:cT + 128], [(fl(BFr), identb[:])], RO, [pT])
                    kb.mm(big[2][:, cT + 128:cT + 256], [(fl(BFi), identb[:])], RO, [pT])
                    kb.op("act", lambda e, sti=sti, gi=gi, cT=cT: e.activation(
                        out=sti[:, 0:2, gi, :], in_=big[2][:, cT:cT + 256].rearrange("p (t n) -> p t n", t=2), func=AF.Copy), [pT], [b_st3[x_]])
                    for d_ in range(2):
                        sl = slice(d_ * 64, (d_ + 1) * 64)
                        kb.op("act", lambda e, sl=sl, d_=d_, g=g, s_=s_: e.activation(
                            out=cmz[s_][sl, 2 * d_, :], in_=CMr[sl, g].rearrange("p j h -> p (j h)"), func=AF.Copy), RO + [b_cmz[s_]], [b_cmz[s_]])
                        kb.op("act", lambda e, sl=sl, d_=d_, g=g, s_=s_: e.activation(
                            out=cmz[s_][sl, 2 * d_ + 1, :], in_=CMn[sl, g].rearrange("p j h -> p (j h)"), func=AF.Copy), RO + [b_cmz[s_]], [b_cmz[s_]])
                    for d_ in range(2):
                        kb.mm(big[3][:, cK + d_ * 128:cK + (d_ + 1) * 128],
                              [(fl(BFr), cmz[s_][:, 2 * d_, :]), (fl(BFi), cmz[s_][:, 2 * d_ + 1, :])], RO + [b_cmz[s_]], [pK])
                    kb.op("dve", lambda e, s_=s_, cK=cK: e.tensor_tensor(out=k1[s_][:], in0=big[3][:, cK:cK + 128], in1=msk[:, 0, :], op=ALU.mult),
                          RO + [pK, b_k[s_]], [b_k[s_]])
                    kb.op("dve", lambda e, s_=s_, cK=cK: e.tensor_tensor(out=k2[s_][:], in0=big[3][:, cK + 128:cK + 256], in1=msk[:, 1, :], op=ALU.mult),
                          RO + [pK, b_k[s_]], [b_k[s_]])
                    kb.op("dve", lambda e, s_=s_: e.tensor_tensor(out=k1[s_][:], in0=k1[s_][:], in1=k2[s_][:], op=ALU.add), [b_k[s_]], [b_k[s_]])
                    kb.op("dve", lambda e, g=g, sti=sti, gi=gi, s_=s_: e.scalar_tensor_tensor(
                        out=sti[:, 2, gi, :], in0=ident[:], scalar=D8[:, g:g + 1], in1=k1[s_][:], op0=ALU.mult, op1=ALU.add),
                        RO + [b_k[s_], b_st3[x_]], [b_st3[x_]])
                    if gi == 7:
                        g0 = g - 7
                        for t_ in range(3):
                            kb.dma("sp", tblv[g0:g0 + 8, t_].rearrange("g p n -> p g n"), sti[:, t_], [b_st3[x_]], [], b_st3[x_])
                kb.barrier()

        def odd_mixer(l):
            li = l // 2
            with ExitStack() as phO:
                u8 = phO.enter_context(nc.sbuf_tensor(uq("u8all"), [128, 64, 640], BF16))
                b_u8 = [kb.buf("u8_%d" % g) for g in range(64)]
                finS = phO.enter_context(nc.sbuf_tensor(uq("finS"), [128, 64, 4, 2], F32))
                b_fin = kb.buf("finS")
                with ExitStack() as ph:
                    hall, b_hall = load_h_all(ph)
                    selS = ph.enter_context(nc.sbuf_tensor(uq("selS"), [128, 64, 128], BF16))
                    b_sel = kb.buf("selS")
                    kb.dma("sp", selS[:], sel_d.rearrange("a j r c -> r (a j) c"), [], [b_sel], b_sel)
                    for g in range(64):
                        ft, a = g // 8, g % 8
                        bk = big[g % 4]
                        pbs = [psb[2 * (g % 4)], psb[2 * (g % 4) + 1]]
                        kb.mm(bk[:, 0:512], [(selS[:, a * 8 + j, :], hall[:, ft, 1024 + j:5120:8]) for j in range(8)],
                              [b_sel] + b_hall[2:], pbs)
                        kb.mm(bk[:, 512:640], [(selS[:, a * 8 + j, :], hall[:, ft, j:1024:8]) for j in range(8)],
                              [b_sel] + b_hall[0:2], pbs)
                        if g % 2 == 0:
                            kb.op("act", lambda e, g=g, bk=bk: e.activation(out=u8[:, g, :], in_=bk[:, 0:640], func=AF.Copy), pbs, [b_u8[g]])
                        else:
                            kb.op("dve", lambda e, g=g, bk=bk: e.tensor_copy(out=u8[:, g, :], in_=bk[:, 0:640]), pbs, [b_u8[g]])
                    kb.barrier()
                with ExitStack() as ph:
                    def T(name, shape, dtype=F32):
                        return ph.enter_context(nc.sbuf_tensor(uq(name), list(shape), dtype))
                    GB = 2
                    SB = 16
                    TWr = [T("TWr%d" % i_, [128, GB, 640]) for i_ in range(2)]
                    TWi = [T("TWi%d" % i_, [128, GB, 640]) for i_ in range(2)]
                    dec = [T("dec%d" % i_, [128, GB, 640]) for i_ in range(2)]
                    b_TW = [kb.buf("TW%d" % i_) for i_ in range(2)]
                    b_dec = [kb.buf("dec%d" % i_) for i_ in range(2)]
                    F0r = T("F0r", [128, SB, 32]); F0i = T("F0i", [128, SB, 32])
                    F1r = T("F1r", [128, SB, 16]); F1i = T("F1i", [128, SB, 16])
                    b_F = kb.buf("F")
                    fq1 = T("fq1", [128, SB, 16]); fq2 = T("fq2", [128, SB, 16])
                    q1 = T("q1", [128, GB, 512]); q2 = T("q2", [128, GB, 512])
                    b_q = kb.buf("q")
                    mrow = T("mrow", [128, 640])
                    b_mrow = kb.buf("mrow")
                    tb = [T("tb%d" % i_, [128, 7, 128], BF16) for i_ in range(2)]
                    b_tb = [kb.buf("tb%d" % i_) for i_ in range(2)]
                    zin_r = T("zinr", [128, 640]); zin_i = T("zini", [128, 640])
                    b_zin = kb.buf("zin")
                    _zr = T("zr", [128, 640]); _zi = T("zi", [128, 640])
                    z_r = [_zr, _zr]
                    z_i = [_zi, _zi]
                    _bz = kb.buf("z")
                    b_z = [_bz, _bz]
                    S_r = [T("Sr%d" % i_, [128, 640]) for i_ in range(2)]
                    S_i = [T("Si%d" % i_, [128, 640]) for i_ in range(2)]
                    b_S = [kb.buf("S%d" % i_) for i_ in range(2)]
                    car_r = [T("carr%d" % i_, [128, 640], BF16) for i_ in range(2)]
                    car_i = [T("cari%d" % i_, [128, 640], BF16) for i_ in range(2)]
                    b_car = [kb.buf("car%d" % i_) for i_ in range(2)]
                    t2 = T("t2", [128, 640])
                    t1s = T("t1s", [128, 640])
                    t1p = t1s[:]
                    b_t12 = kb.buf("t12")
                    b_t34 = [kb.buf("t34_%d" % i_) for i_ in range(2)]
                    for i_ in range(2):
                        kb.op("dve", lambda e, i_=i_: e.memset(car_r[i_][:], 0.0), [], [b_car[i_]])
                        kb.op("dve", lambda e, i_=i_: e.memset(car_i[i_][:], 0.0), [], [b_car[i_]])
                    kb.op("dve", lambda e: e.memset(F1r[:, :, 0:1], 1.0), [], [b_F])
                    kb.op("dve", lambda e: e.memset(F1i[:, :, 0:1], 0.0), [], [b_F])
                    kb.op("dve", lambda e: e.memset(mrow[:], 1.0), [], [b_mrow])
                    for b_ in range(4):
                        kb.op("dve", lambda e, b_=b_: e.memset(mrow[:, 512 + 32 * b_:513 + 32 * b_], 0.0), [b_mrow], [b_mrow])
                    selT = T("selT", [128, 64, 128], BF16)
                    b_selT = kb.buf("selT")
                    kb.dma("sp", selT[:], selT_d.rearrange("a j r c -> r (a j) c"), [], [b_selT], b_selT)
                    zT = [T("zT%d" % i_, [128, NT], BF16) for i_ in range(2)]
                    b_zT = [kb.buf("zT%d" % i_) for i_ in range(2)]
                    o3_q = []

                    def o3_step(ft, j):
                        zi_ = ft % 2
                        bk = big[3]
                        pbs = [psb[6], psb[7]]
                        gs = [ft * 8 + a_ for a_ in range(8)]
                        kb.mm(bk[:, 0:512], [(selT[:, a_ * 8 + j, :], u8[:, ft * 8 + a_, 0:512]) for a_ in range(8)],
                              [b_selT] + [b_u8[g_] for g_ in gs], pbs)
                        kb.mm(bk[:, 512:640], [(selT[:, a_ * 8 + j, :], u8[:, ft * 8 + a_, 512:640]) for a_ in range(8)],
                              [b_selT] + [b_u8[g_] for g_ in gs], pbs)
                        kb.op("act", lambda e: e.activation(out=zT[zi_][:, 1024 + j:5120:8], in_=bk[:, 0:512], func=AF.Copy), pbs, [b_zT[zi_]])
                        kb.op("act", lambda e: e.activation(out=zT[zi_][:, j:1024:8], in_=bk[:, 512:640], func=AF.Copy), pbs, [b_zT[zi_]])
                        if j == 7:
                            kb.dma("sp", YCv[:, ft, :], zT[zi_][:], [b_zT[zi_]], [], b_zT[zi_])
                    tblv = TBL[li]
                    SLr, SLi = big[0], big[1]
                    p_slr, p_sli = [psb[0], psb[1]], [psb[2], psb[3]]

                    def cdbl(Xr, Xi, m, n):
                        ar = Xr[:, :, m - 1:m].to_broadcast([128, SB, n])
                        ai = Xi[:, :, m - 1:m].to_broadcast([128, SB, n])
                        br, bi2 = Xr[:, :, 0:n], Xi[:, :, 0:n]
                        qa, qb = fq1[:, :, 0:n], fq2[:, :, 0:n]
                        o = lambda fn: kb.op("dve", fn, [b_F], [b_F])
                        o(lambda e: e.tensor_tensor(out=qa, in0=br, in1=ar, op=ALU.mult))
                        o(lambda e: e.tensor_tensor(out=qb, in0=bi2, in1=ai, op=ALU.mult))
                        o(lambda e: e.tensor_tensor(out=Xr[:, :, m:m + n], in0=qa, in1=qb, op=ALU.subtract))
                        o(lambda e: e.tensor_tensor(out=qa, in0=br, in1=ai, op=ALU.mult))
                        o(lambda e: e.tensor_tensor(out=qb, in0=bi2, in1=ar, op=ALU.mult))
                        o(lambda e: e.tensor_tensor(out=Xi[:, :, m:m + n], in0=qa, in1=qb, op=ALU.add))

                    def b4(a_, kind):
                        if kind == "a":
                            return AP3(a_, [a_.ap[1], a_.ap[2], [0, 32]])
                        return AP3(a_, [a_.ap[1], [0, 16], a_.ap[2]])

                    for g in range(64):
                        if g % SB == 0:
                            kb.op("dve", lambda e, g=g: e.tensor_copy(out=F0r[:, :, 0], in_=e1r[:, li, g:g + SB]), [b_s5p, b_F], [b_F])
                            kb.op("dve", lambda e, g=g: e.tensor_copy(out=F0i[:, :, 0], in_=e1i[:, li, g:g + SB]), [b_s5p, b_F], [b_F])
                            for m in (1, 2, 4, 8, 16):
                                cdbl(F0r, F0i, m, m)
                            kb.op("dve", lambda e: e.tensor_copy(out=F1r[:, :, 1], in_=F0r[:, :, 31]), [b_F], [b_F])
                            kb.op("dve", lambda e: e.tensor_copy(out=F1i[:, :, 1], in_=F0i[:, :, 31]), [b_F], [b_F])
                            H_r, H_i = F1r[:, :, 1:16], F1i[:, :, 1:16]
                            for m, n in ((1, 1), (2, 2), (4, 4), (8, 7)):
                                cdbl(H_r, H_i, m, n)
                        gb, gi = g // GB, g % GB
                        bi_ = gb % 2
                        if gi == 0:
                            gs_ = g % SB
                            Wt, Rt = [b_TW[bi_], b_q], [b_TW[bi_], b_q, b_F]
                            po = lambda fn: kb.op("pool", fn, Rt, Wt)
                            v4_ = lambda X: X.rearrange("p g (a b) -> p g a b", b=32)
                            Or, Oi = v4_(TWr[bi_][:, :, 0:512]), v4_(TWi[bi_][:, :, 0:512])
                            Q1, Q2 = v4_(q1[:]), v4_(q2[:])
                            x0r, x0i = F0r[:, gs_:gs_ + GB, :], F0i[:, gs_:gs_ + GB, :]
                            A_r, A_i = b4(F1r[:, gs_:gs_ + GB, :], "a"), b4(F1i[:, gs_:gs_ + GB, :], "a")
                            B_r, B_i = b4(x0r, "b"), b4(x0i, "b")
                            po(lambda e: e.tensor_tensor(out=Q1, in0=A_r, in1=B_r, op=ALU.mult))
                            po(lambda e: e.tensor_tensor(out=Q2, in0=A_i, in1=B_i, op=ALU.mult))
                            po(lambda e: e.tensor_tensor(out=Or, in0=Q1, in1=Q2, op=ALU.subtract))
                            po(lambda e: e.tensor_tensor(out=Q1, in0=A_r, in1=B_i, op=ALU.mult))
                            po(lambda e: e.tensor_tensor(out=Q2, in0=A_i, in1=B_r, op=ALU.mult))
                            po(lambda e: e.tensor_tensor(out=Oi, in0=Q1, in1=Q2, op=ALU.add))
                            for b_ in range(4):
                                po(lambda e, b_=b_: e.tensor_copy(out=TWr[bi_][:, :, 512 + 32 * b_:544 + 32 * b_], in_=x0r))
                                po(lambda e, b_=b_: e.tensor_copy(out=TWi[bi_][:, :, 512 + 32 * b_:544 + 32 * b_], in_=x0i))
                            r8v = r8t[:, li, g:g + GB]
                            kb.op("dve", lambda e, r8v=r8v, bi_=bi_: e.tensor_tensor(
                                out=dec[bi_][:], in0=AP3(mrow[:], [[0, GB], mrow[:].ap[1]]), in1=AP3(r8v, [r8v.ap[1], [0, 640]]), op=ALU.mult),
                                [b_mrow, b_s5p, b_dec[bi_]], [b_dec[bi_]])
                        si = g % 2
                        Er, Ei = TWr[bi_][:, gi, :], TWi[bi_][:, gi, :]

                        def local_sums(g_):
                            s_ = g_ % 2
                            kb.dma("sp", tb[s_][:], tblv[g_].rearrange("t p n -> p t n"), [], [b_tb[s_]], b_tb[s_])
                            for SL, pb, slot in ((SLr, p_slr, 0), (SLi, p_sli, 1)):
                                kb.mm(SL[0:64, 0:512], [(tb[s_][:, slot, 0:64], u8[:, g_, 0:512])], [b_tb[s_], b_u8[g_]], pb)
                                kb.mm(SL[64:128, 0:512], [(tb[s_][:, slot, 64:128], u8[:, g_, 511::-1])], [b_tb[s_], b_u8[g_]], pb)
                                kb.mm(SL[0:64, 512:640], [(tb[s_][:, slot, 0:64], u8[:, g_, 512:640])], [b_tb[s_], b_u8[g_]], pb)
                                kb.mm(SL[64:128, 512:640], [(tb[s_][:, slot, 64:128], u8[:, g_, 639:511:-1])], [b_tb[s_], b_u8[g_]], pb)
                        if g == 0:
                            local_sums(0)
                        Rd = p_slr + p_sli + [b_TW[bi_], b_t12]
                        dv = lambda fn, Wd: kb.op("dve", fn, Rd + Wd, Wd)
                        dv(lambda e, Er=Er: e.tensor_tensor(out=t1p, in0=SLr[:, 0:640], in1=Er, op=ALU.mult), [b_t12])
                        dv(lambda e, Ei=Ei: e.tensor_tensor(out=t2[:], in0=SLi[:, 0:640], in1=Ei, op=ALU.mult), [b_t12])
                        dv(lambda e: e.tensor_tensor(out=zin_r[:], in0=t1p, in1=t2[:], op=ALU.subtract), [b_zin])
                        dv(lambda e, Er=Er: e.tensor_tensor(out=t1p, in0=SLi[:, 0:640], in1=Er, op=ALU.mult), [b_t12])
                        dv(lambda e, Ei=Ei: e.tensor_tensor(out=t2[:], in0=SLr[:, 0:640], in1=Ei, op=ALU.mult), [b_t12])
                        dv(lambda e: e.tensor_tensor(out=zin_i[:], in0=t1p, in1=t2[:], op=ALU.add), [b_zin])
                        if g + 1 < 64:
                            local_sums(g + 1)
                        for zin, z, h0 in ((zin_r, z_r, h0r), (zin_i, z_i, h0i)):
                            kb.op("dve", lambda e, zin=zin, z=z, h0=h0, g=g, si=si, gi=gi, bi_=bi_: e.tensor_tensor_scan(
                                out=z[si][:], data0=dec[bi_][:, gi, :], data1=zin[:],
                                initial=h0[:, li, g:g + 1], op0=ALU.mult, op1=ALU.add), [b_zin, b_s5p, b_z[si], b_dec[bi_]], [b_z[si]])
                        Rr_ = [b_z[si], b_TW[bi_], b_t12]
                        kb.op("dve", lambda e, Er=Er, si=si: e.tensor_tensor(out=t1p, in0=z_r[si][:], in1=Er, op=ALU.mult), Rr_, [b_t12])
                        kb.op("dve", lambda e, Ei=Ei, si=si: e.tensor_tensor(out=t2[:], in0=z_i[si][:], in1=Ei, op=ALU.mult), Rr_, [b_t12])
                        kb.op("dve", lambda e, si=si: e.tensor_tensor(out=S_r[si][:], in0=t1p, in1=t2[:], op=ALU.add), [b_t12, b_S[si]], [b_S[si]])
                        kb.op("dve", lambda e, Er=Er, si=si: e.tensor_tensor(out=t1p, in0=z_i[si][:], in1=Er, op=ALU.mult), Rr_, [b_t12])
                        kb.op("dve", lambda e, Ei=Ei, si=si: e.tensor_tensor(out=t2[:], in0=z_r[si][:], in1=Ei, op=ALU.mult), Rr_, [b_t12])
                        kb.op("dve", lambda e, si=si: e.tensor_tensor(out=S_i[si][:], in0=t1p, in1=t2[:], op=ALU.subtract), [b_t12, b_S[si]], [b_S[si]])
                        v4 = lambda X: X.rearrange("p (b c) -> p b c", c=32)
                        for part, (S_, car, h0) in enumerate(((S_r, car_r, h0r), (S_i, car_i, h0i))):
                            kb.op("act", lambda e, S_=S_, car=car, si=si: e.activation(out=car[si][:, 1:512], in_=S_[si][:, 0:511], func=AF.Copy),
                                  [b_S[si], b_car[si]], [b_car[si]])
                            kb.op("act", lambda e, S_=S_, car=car, si=si: e.activation(
                                out=v4(car[si][:, 512:640])[:, :, 1:32], in_=v4(S_[si][:, 512:640])[:, :, 0:31], func=AF.Copy),
                                [b_S[si], b_car[si]], [b_car[si]])
                            kb.op("act", lambda e, S_=S_, g=g, part=part, si=si: e.activation(out=finS[:, g, :, part], in_=S_[si][:, 543:640:32], func=AF.Copy),
                                  [b_S[si], b_fin], [b_fin])
                            kb.op("act", lambda e, g=g, si=si, car=car, h0=h0: e.activation(out=car[si][:, 0:1], in_=h0[:, li, g:g + 1], func=AF.Copy),
                                  [b_s5p, b_car[si]], [b_car[si]])
                        Y = big[2]
                        pby = [psb[4], psb[5]]
                        kb.mm(Y[:, 0:512], [(tb[si][:, 2, :], u8[:, g, 0:512]),
                                            (tb[si][:, 3, :], car_r[si][:, 0:512]), (tb[si][:, 4, :], car_i[si][:, 0:512]),
                                            (tb[si][:, 5, :], car_r[si][:, 511::-1]), (tb[si][:, 6, :], car_i[si][:, 511::-1])],
                              [b_tb[si], b_u8[g], b_car[si]], pby)
                        kb.mm(Y[:, 512:640], [(tb[si][:, 2, :], u8[:, g, 512:640]),
                                              (tb[si][:, 3, :], car_r[si][:, 512:640]), (tb[si][:, 4, :], car_i[si][:, 512:640]),
                                              (tb[si][:, 5, :], car_r[si][:, 639:511:-1]), (tb[si][:, 6, :], car_i[si][:, 639:511:-1])],
                              [b_tb[si], b_u8[g], b_car[si]], pby)
                        kb.op("act", lambda e, g=g, Y=Y: e.activation(out=u8[:, g, :], in_=Y[:, 0:640], func=AF.Gelu), pby, [b_u8[g]])
                        if g % 8 == 7:
                            for j_ in range(8):
                                o3_q.append((g // 8, j_))
                        if o3_q:
                            o3_step(*o3_q.pop(0))
                    while o3_q:
                        o3_step(*o3_q.pop(0))
                    kb.barrier()
                with ExitStack() as ph:
                    outS = ph.enter_context(nc.sbuf_tensor(uq("outS"), [128, 4, 64, 2], F32))
                    b_out = kb.buf("outS")
                    n_ = 0
                    for s_ in range(4):
                        for c_ in range(2):
                            pi = n_ % 8
                            n_ += 1
                            for d_ in range(2):
                                sl = slice(d_ * 64, (d_ + 1) * 64)
                                b_ = s_ if d_ == 0 else 3 - s_
                                kb.mm(pst[pi][sl, 0:64], [(finS[sl, :, b_, c_], ident[sl, sl])], [b_fin, b_const], [psb[pi]])
                            kb.op("dve", lambda e, pi=pi, s_=s_, c_=c_: e.tensor_copy(out=outS[:, s_, :, c_], in_=pst[pi][:, 0:64]),
                                  [psb[pi]], [b_out])
                        kb.dma("sp", ns[s_, li].rearrange("d g p c -> (d g) (p c)"), outS[:, s_].rearrange("q p c -> q (p c)"), [b_out], [], b_out)
                    kb.barrier()
            out_proj(l, w_glu[li], glu=True)

        if mode == "s5setup":
            s5_setup(0)
            b_dbg = kb.buf("dbg")
            kb.dma("sp", dbg_tbl, TBL[0], [], [], b_dbg)
            for i_, t_ in enumerate((r8t, e1r, e1i, h0r, h0i)):
                kb.dma("sp", dbg_small[:, i_], t_[:], [b_s5p], [], b_dbg)
            kb.barrier()
            return nc
        if with_s5:
            for l in range(1, depth, 2):
                s5_setup(l // 2)
        for l in range(depth):
            if l % 2 == 0:
                even_mixer(l)
            else:
                if with_s5:
                    odd_mixer(l)
            last = (l == depth - 1)
            ffn(l, None if last else l + 1, 0)

        kb.barrier()
    return nc


_CONST = {}


def _consts():
    if _CONST:
        return _CONST
    bf = ml_dtypes.bfloat16
    _CONST["c_ident"] = np.eye(128, dtype=np.float32)
    _CONST["c_ones"] = np.ones((128, 128), dtype=bf)
    c = np.arange(128)[:, None] * np.arange(128)[None, :]
    ang = 2 * np.pi * (c % 128) / 128.0
    _CONST["c_cs128"] = np.concatenate([np.cos(ang), np.sin(ang)], axis=1).astype(bf)

    def dft(L):
        lk = (np.arange(L, dtype=np.int64)[:, None] * np.arange(L, dtype=np.int64)[None, :]) % L
        a = 2 * np.pi * lk / float(L)
        return np.stack([np.cos(a), -np.sin(a)]).astype(bf)
    _CONST["c_dftp"] = dft(256)
    t = dft(4096).reshape(2, 32, 128, 8, 512)
    _CONST["c_dfts"] = np.ascontiguousarray(t.transpose(3, 0, 2, 1, 4)).reshape(8, 2, 128, 32 * 512)
    sel = np.zeros((8, 8, 128, 128), np.float32)
    for a in range(8):
        for j in range(8):
            for h in range(16):
                sel[a, j, a * 16 + h, j * 16 + h] = 1.0
    _CONST["c_sel"] = sel.astype(bf)
    _CONST["c_selT"] = np.ascontiguousarray(sel.transpose(0, 1, 3, 2)).astype(bf)
    jj = np.arange(128) // 16
    mf = (jj[None, :] >= jj[:, None]).astype(np.float32)
    mb = (jj[None, :] <= jj[:, None]).astype(np.float32)
    _CONST["c_mask"] = np.stack([mf, mb])
    return _CONST


_NC_CACHE = {}


def kernel(x_prompt, x_sample, state_ssm, c, c_ctx, w_ada, b_ada, norm1_g, norm2_g, final_g,
           w_in_mix, w_conv, w_out_mix, ssm_lambda_re, ssm_lambda_im, ssm_log_step,
           ssm_b_re, ssm_b_im, ssm_c_re, ssm_c_im, ssm_d, w_glu, w_ffn_in, w_ffn_out,
           _depth=4, _with_s5=True, _mode=None, _ncores=8):
    f = lambda a: np.ascontiguousarray(np.asarray(a, dtype=np.float32))
    key = (_depth, _with_s5, _mode)
    if key not in _NC_CACHE:
        _NC_CACHE[key] = build(_depth, _with_s5, _mode)
    nc = _NC_CACHE[key]
    shared = dict(w_ada=f(w_ada), b_ada=f(b_ada), norm1_g=f(norm1_g), norm2_g=f(norm2_g), final_g=f(final_g),
                  w_conv=f(w_conv), w_out_mix=f(w_out_mix),
                  ssm_lambda_re=f(ssm_lambda_re), ssm_lambda_im=f(ssm_lambda_im), ssm_log_step=f(ssm_log_step),
                  ssm_b_re=f(ssm_b_re), ssm_b_im=f(ssm_b_im), ssm_c_re=f(ssm_c_re), ssm_c_im=f(ssm_c_im),
                  ssm_d=f(ssm_d), w_glu=f(w_glu), w_ffn_out=f(w_ffn_out))
    wfi = f(w_ffn_in).reshape(4, 8, 128, 2, 22, 128)
    shared["w_ffn_in"] = np.ascontiguousarray(wfi.transpose(0, 4, 2, 1, 3, 5)).reshape(4, 22, 128, 8 * 256)
    wim = f(w_in_mix).reshape(2, 8, 128, 4, 4, 128)
    cv = wim[:, :, :, [0, 2, 1]]
    cv = np.ascontiguousarray(cv.transpose(0, 4, 2, 1, 3, 5)).reshape(2, 4, 128, 8 * 384)
    fv = np.zeros((2, 4, 128, 8, 384), np.float32)
    fv[:, :, :, :, 0:128] = wim[:, :, :, 3].transpose(0, 3, 2, 1, 4)
    shared["w_in_mix"] = np.ascontiguousarray(np.concatenate([cv, fv.reshape(2, 4, 128, 8 * 384)], axis=1))
    shared.update(_consts())
    xpn = f(x_prompt)
    xsn = f(x_sample)
    stn = f(state_ssm)
    cn = f(c)
    cc = f(c_ctx)
    in_maps = []
    for i in range(8):
        m = dict(shared)
        m["xp"] = xpn[4 * i:4 * i + 4].reshape(1024, D)
        m["xs"] = xsn[i]
        m["st"] = stn[i]
        m["cond"] = np.stack([cc, cn[i]])
        in_maps.append(m)
    if _mode is not None:
        res = run_bass_kernel_spmd(nc, in_maps[:_ncores], core_ids=list(range(_ncores)))
        return res.results
    res = run_bass_kernel_spmd(nc, in_maps, core_ids=list(range(8)))
    r = res.results
    y_prompt = np.concatenate([r[i]["yp"].reshape(4, 256, D) for i in range(8)], axis=0)
    y_sample = np.stack([r[i]["ys"] for i in range(8)], axis=0)
    new_state = np.concatenate([r[i]["ns"] for i in range(8)], axis=0)
    return (y_prompt.astype(np.float32), y_sample.astype(np.float32), new_state.astype(np.float32))
```

```python
import math
from contextlib import ExitStack
import numpy as np
import ml_dtypes
import concourse.bass as bass
import concourse.mybir as mybir
from concourse.bass_utils import run_bass_kernel_spmd

F32 = mybir.dt.float32
BF16 = mybir.dt.bfloat16
AF = mybir.ActivationFunctionType
ALU = mybir.AluOpType

D = 1024
NT = 5120
TT = 512
NTT = NT // TT
NPT = 2
DFF = 2816
KT = 8
EPS = 1e-6


class Buf:
    __slots__ = ("name", "w", "r", "dsem", "dcnt")

    def __init__(self, name):
        self.name = name
        self.w = None
        self.r = {}
        self.dsem = None
        self.dcnt = 0


class KB:
    def __init__(self, nc, es):
        self.nc = nc
        self.es = es
        self.eng = {"pe": nc.tensor, "act": nc.scalar, "dve": nc.vector,
                    "pool": nc.gpsimd, "sp": nc.sync}
        self.sem = {e: es.enter_context(nc.semaphore("s_" + e)) for e in self.eng}
        self.cnt = {e: 0 for e in self.eng}
        self.seen = {e: {} for e in self.eng}
        self.bar = es.enter_context(nc.semaphore("s_bar"))
        self.barcnt = 0
        self.dsems = []
        self.free_ds = []
        self.bufs = []
        self.tape = None

    def buf(self, name):
        b = Buf("%s_%d" % (name, len(self.bufs)))
        self.bufs.append(b)
        return b

    def _deps(self, reads, writes):
        deps = {}

        def add(t):
            k, sem, v = t
            if k not in deps or deps[k][1] < v:
                deps[k] = (sem, v)
        for b in reads:
            if b.w is not None:
                add(b.w)
        for b in writes:
            if b.w is not None:
                add(b.w)
            for k, (sem, v) in b.r.items():
                add((k, sem, v))
        return deps

    def _waits(self, e, reads, writes):
        eng = self.eng[e]
        for k, (sem, v) in self._deps(reads, writes).items():
            if e == "pe" and k == "pe":
                continue
            if self.seen[e].get(k, 0) < v:
                eng.wait_ge(sem, v)
                self.seen[e][k] = v

    def _commit(self, tok, reads, writes):
        k, sem, v = tok
        for b in writes:
            b.w = tok
            b.r = {}
        for b in reads:
            if b.r.get(k, (None, 0))[1] < v:
                b.r[k] = (sem, v)

    def replay(self, tape, n):
        k = 0
        while tape and k < n:
            kind, args = tape.pop(0)
            getattr(self, kind)(*args[:-1], **args[-1])
            k += 1

    def op(self, e, fn, reads=(), writes=()):
        if self.tape is not None:
            self.tape.append(("op", (e, fn, list(reads), list(writes), {})))
            return
        self._waits(e, reads, writes)
        ins = fn(self.eng[e])
        self.cnt[e] += 1
        ins.then_inc(self.sem[e], 1)
        self._commit((e, self.sem[e], self.cnt[e]), reads, writes)

    def mm(self, out_ap, pairs, reads, writes, start=True, stop=True):
        if self.tape is not None:
            self.tape.append(("mm", (out_ap, list(pairs), list(reads), list(writes), dict(start=start, stop=stop))))
            return
        self._waits("pe", reads, writes)
        n = len(pairs)
        ins = None
        for i, (l, r) in enumerate(pairs):
            ins = self.nc.tensor.matmul(out_ap, lhsT=l, rhs=r, start=(start and i == 0), stop=(stop and i == n - 1))
        self.cnt["pe"] += 1
        ins.then_inc(self.sem["pe"], 1)
        self._commit(("pe", self.sem["pe"], self.cnt["pe"]), reads, writes)

    def transpose(self, out_ap, in_ap, ident_ap, reads, writes):
        self._waits("pe", reads, writes)
        ins = self.nc.tensor.transpose(out_ap, in_ap, ident_ap)
        self.cnt["pe"] += 1
        ins.then_inc(self.sem["pe"], 1)
        self._commit(("pe", self.sem["pe"], self.cnt["pe"]), reads, writes)

    def dma(self, q, out_ap, in_ap, reads, writes, anchor, **kw):
        if self.tape is not None:
            self.tape.append(("dma", (q, out_ap, in_ap, list(reads), list(writes), anchor, dict(kw))))
            return
        self._waits(q, reads, writes)
        if anchor.dsem is None:
            if self.free_ds:
                anchor.dsem = self.free_ds.pop()
            else:
                ds = [self.es.enter_context(self.nc.semaphore("dsem%d" % len(self.dsems))), 0, len(self.dsems)]
                self.dsems.append(ds)
                anchor.dsem = ds
        ds = anchor.dsem
        ins = self.eng[q].dma_start(out=out_ap, in_=in_ap, **kw)
        ds[1] += 16
        ins.then_inc(ds[0], 16)
        self._commit(("d%d" % ds[2], ds[0], ds[1]), reads, writes)

    def barrier(self):
        sp = self.nc.sync
        for e in self.eng:
            if e != "sp" and self.cnt[e] > 0:
                sp.wait_ge(self.sem[e], self.cnt[e])
        for ds in self.dsems:
            if ds[1] > 0:
                sp.wait_ge(ds[0], ds[1])
        self.barcnt += 1
        sp.sem_inc(self.bar, 1)
        for e in self.eng:
            if e != "sp":
                self.eng[e].wait_ge(self.bar, self.barcnt)
        for b in self.bufs:
            b.w = None
            b.r = {}
            b.dsem = None
        self.free_ds = list(self.dsems)
        for e in self.eng:
            for e2 in self.eng:
                self.seen[e][e2] = self.cnt[e2]
            for ds in self.dsems:
                self.seen[e]["d%d" % ds[2]] = ds[1]


def build(depth=4, with_s5=True, mode=None):
    nc = bass.Bass("TRN2", target_bir_lowering=False)
    dt = nc.dram_tensor

    def din(name, shape, dtype=F32):
        return dt(name, list(shape), dtype, kind="ExternalInput").ap()

    xp = din("xp", [1024, D])
    xs = din("xs", [4096, D])
    st_in = din("st", [2, 2, 64, 64, 2])
    cond = din("cond", [2, D])
    w_ada = din("w_ada", [4, D, 6 * D])
    b_ada = din("b_ada", [4, 6 * D])
    norm1_g = din("norm1_g", [4, D])
    norm2_g = din("norm2_g", [4, D])
    final_g = din("final_g", [D])
    w_in_mix = din("w_in_mix", [2, 8, 128, 8 * 384])
    w_conv = din("w_conv", [2, 3, 512])
    w_out_mix = din("w_out_mix", [2, D, D])
    lam_re = din("ssm_lambda_re", [2, 2, 64, 64])
    lam_im = din("ssm_lambda_im", [2, 2, 64, 64])
    log_step = din("ssm_log_step", [2, 2, 64])
    b_re = din("ssm_b_re", [2, 2, 64, 64, 16])
    b_im = din("ssm_b_im", [2, 2, 64, 64, 16])
    c_re = din("ssm_c_re", [2, 2, 64, 16, 64])
    c_im = din("ssm_c_im", [2, 2, 64, 16, 64])
    ssm_d = din("ssm_d", [2, D])
    w_glu = din("w_glu", [2, D, 2 * D])
    w_ffn_in = din("w_ffn_in", [4, 22, 128, 8 * 256])
    w_ffn_out = din("w_ffn_out", [4, DFF, D])
    ident_d = din("c_ident", [128, 128])
    ones_d = din("c_ones", [128, 128], BF16)
    cs128_d = din("c_cs128", [128, 256], BF16)
    dftp_d = din("c_dftp", [2, 256, 256], BF16)
    dfts_d = din("c_dfts", [8, 2, 128, 32 * 512], BF16)
    sel_d = din("c_sel", [8, 8, 128, 128], BF16)
    selT_d = din("c_selT", [8, 8, 128, 128], BF16)
    mask_d = din("c_mask", [2, 128, 128])
    TBL = dt("TBL", [2, 64, 7, 128, 128], BF16, kind="Internal").ap()

    yp = dt("yp", [1024, D], F32, kind="ExternalOutput").ap()
    ys = dt("ys", [4096, D], F32, kind="ExternalOutput").ap()
    ns = dt("ns", [4, 2, 2, 64, 64, 2], F32, kind="ExternalOutput").ap()
    if mode is not None:
        dbg_tbl = dt("dbg_tbl", [64, 7, 128, 128], BF16, kind="ExternalOutput").ap()
        dbg_small = dt("dbg_small", [128, 5, 2, 64], F32, kind="ExternalOutput").ap()

    XT = dt("XT", [D, NT], F32, kind="Internal").ap()
    HT = dt("HT", [D, NT], BF16, kind="Internal").ap()
    YC = dt("YC", [D, NT], BF16, kind="Internal").ap()
    HID = dt("HID", [DFF, NT], BF16, kind="Internal").ap()

    XTv = XT.rearrange("(k p) n -> p k n", p=128)
    HTv = HT.rearrange("(k p) n -> p k n", p=128)
    YCv = YC.rearrange("(k p) n -> p k n", p=128)
    HIDv = HID.rearrange("(k p) n -> p k n", p=128)

    _uq = [0]

    def uq(name):
        _uq[0] += 1
        return "%s_%d" % (name, _uq[0])

    with ExitStack() as es:
        kb = KB(nc, es)
        sb = lambda name, shape, dtype: es.enter_context(nc.sbuf_tensor(uq(name), list(shape), dtype))

        ident = sb("ident", [128, 128], F32)
        ones = sb("ones", [128, 128], BF16)
        modT = sb("modT", [128, 4, 48, 2], F32)
        g1t = sb("g1t", [128, 4, 8], F32)
        g2t = sb("g2t", [128, 4, 8], F32)
        gft = sb("gft", [128, 8], F32)
        acoef = sb("acoef", [128, 4, 2, 8, 2], F32)
        epsb = sb("epsb", [128, 1], F32)
        b_const = kb.buf("const")
        b_mod = kb.buf("mod")
        psb = [kb.buf("ps%d" % i) for i in range(8)]
        big = [es.enter_context(nc.psum_tensor("psbig%d" % i, [128, 1024], F32)) for i in range(4)]

        class _V:
            def __init__(self, t, lo):
                self.t, self.lo = t, lo

            def __getitem__(self, idx):
                if not isinstance(idx, tuple):
                    idx = (idx, slice(None))
                p, c = idx[0], idx[1]
                c0 = 0 if c.start is None else c.start
                c1 = 512 if c.stop is None else c.stop
                assert c.step is None
                return self.t[p, self.lo + c0:self.lo + c1]
        pst = [_V(big[i // 2], 512 * (i % 2)) for i in range(8)]

        kb.dma("sp", ident[:], ident_d, [], [b_const], b_const)
        kb.dma("sp", ones[:], ones_d, [], [b_const], b_const)
        kb.op("dve", lambda e: e.memset(epsb[:], EPS), [], [b_const])

        _ADA_PH = []
        if True:
            ph = es.enter_context(ExitStack())
            psb_ = lambda name, shape, dtype: ph.enter_context(nc.sbuf_tensor(uq(name), list(shape), dtype))
            cT = psb_("cT", [128, 2, 8], F32)
            cTb = psb_("cTb", [128, 2, 8], BF16)
            bT = psb_("bT", [128, 4, 48], F32)
            gT = psb_("gT", [128, 2, 4, 8], F32)
            wsl = [psb_("wada%d" % i, [128, 6144], BF16) for i in range(3)]
            b_wsl = [kb.buf("wada%d" % i) for i in range(3)]
            b_c = kb.buf("cT")
            stage = psb_("vstage", [128, 128], F32)
            b_stage = kb.buf("vstage")

            def load_T(dst_ap, src_rows, n):
                kb.dma("sp", stage[0:n, :], src_rows, [], [b_stage], b_stage)
                kb.transpose(pst[1][:, 0:n], stage[0:n, :], ident[0:n, 0:n], [b_stage, b_const], [psb[1]])
                kb.op("dve", lambda e: e.tensor_copy(out=dst_ap, in_=pst[1][:, 0:n]), [psb[1]], [b_c])
            load_T(cT[:].rearrange("p j k -> p (j k)"), cond.rearrange("j (k p) -> (j k) p", p=128), 16)
            b2 = b_ada.rearrange("l (m p) -> (l m) p", p=128)
            load_T(bT[:, 0:2, :].rearrange("p l m -> p (l m)"), b2[0:96, :], 96)
            load_T(bT[:, 2:4, :].rearrange("p l m -> p (l m)"), b2[96:192, :], 96)
            load_T(gT[:, 0].rearrange("p l k -> p (l k)"), norm1_g.rearrange("l (k p) -> (l k) p", p=128), 32)
            load_T(gT[:, 1].rearrange("p l k -> p (l k)"), norm2_g.rearrange("l (k p) -> (l k) p", p=128), 32)
            load_T(gft[:], final_g.rearrange("(k p) -> k p", p=128), 8)
            kb.op("act", lambda e: e.activation(out=cTb[:], in_=cT[:], func=AF.Silu), [b_c], [b_c])
            idx_ = [0]

            def ada_step(l, kt):
                s = idx_[0] % 3
                idx_[0] += 1
                kb.dma("pool", wsl[s][:], w_ada[l, kt * 128:(kt + 1) * 128, :], [], [b_wsl[s]], b_wsl[s], max_dma_last_dim=4096)
                kb._waits("pe", [b_wsl[s], b_c], [psb[0]])
                ins = None
                for m in range(48):
                    ins = nc.tensor.matmul(pst[0][:, 2 * m:2 * m + 2], lhsT=wsl[s][:, m * 128:(m + 1) * 128],
                                           rhs=cTb[:, :, kt], start=(kt == 0 and m == 0), stop=(kt == KT - 1 and m == 47),
                                           skip_group_check=True)
                kb.cnt["pe"] += 1
                ins.then_inc(kb.sem["pe"], 1)
                kb._commit(("pe", kb.sem["pe"], kb.cnt["pe"]), [b_wsl[s], b_c], [psb[0]])

            def ada_epi(l):
                for j in range(2):
                    kb.op("dve", lambda e, l=l, j=j: e.tensor_tensor(
                        out=modT[:, l, :, j], in0=pst[0][:, 0:96].rearrange("p (m j) -> p m j", j=2)[:, :, j],
                        in1=bT[:, l, :], op=ALU.add), [psb[0], b_c], [b_mod])
                for sub in range(2):
                    base = 24 * sub
                    for j in range(2):
                        kb.op("dve", lambda e, l=l, sub=sub, j=j, base=base: e.scalar_tensor_tensor(
                            out=acoef[:, l, sub, :, j], in0=modT[:, l, base + 8:base + 16, j], scalar=1.0,
                            in1=gT[:, sub, l, :], op0=ALU.add, op1=ALU.mult), [b_mod, b_c], [b_mod])

            def ada_layer(l):
                for kt in range(KT):
                    ada_step(l, kt)
                ada_epi(l)
            _ada_q = []
            for l_ in range(1, depth):
                for kt_ in range(KT):
                    _ada_q.append(lambda l_=l_, kt_=kt_: ada_step(l_, kt_))
                _ada_q.append(lambda l_=l_: ada_epi(l_))
            ada_layer(0)
            _ADA_PH.append((ph, _ada_q))

        def mod_ap(l, chunk, ft, j):
            return modT[:, l, chunk * 8 + ft, j:j + 1]

        class Fin:
            pass

        def make_fin(ph, final=False):
            f = Fin()
            psb_ = lambda name, shape, dtype: ph.enter_context(nc.sbuf_tensor(uq(name), list(shape), dtype))
            f.sq = psb_("f_sq", [128, 8, TT], BF16)
            f.rstd = psb_("f_rstd", [128, TT], F32)
            if not final:
                f.tmp = [psb_("f_tmp%d" % i, [128, TT], F32) for i in range(2)]
                f.h = [psb_("f_h%d" % i, [128, 8, TT], BF16) for i in range(2)]
            f.b_sq = kb.buf("f_sq")
            f.b_rstd = kb.buf("f_rstd")
            f.b_tmp = [kb.buf("f_tmp%d" % i) for i in range(2)]
            f.b_h = [kb.buf("f_h%d" % i) for i in range(2)]
            f.n = 0
            return f

        b_XT = kb.buf("XT")
        b_HT = kb.buf("HT")

        def finish(f, tt, xn, b_xn, l_next, sub_next, ps_i, store_x=True, q="sp"):
            j = 0 if tt < NPT else 1
            cols = slice(tt * TT, (tt + 1) * TT)
            if store_x and l_next is not None:
                kb.dma(q, XTv[:, :, cols], xn[:], [b_xn], [], b_xn)
            kb.op("act", lambda e: e.activation(out=f.sq[:], in_=xn[:], func=AF.Square), [b_xn], [f.b_sq])
            kb.mm(pst[ps_i][:], [(ones[:], f.sq[:, k, :]) for k in range(KT)], [f.b_sq, b_const], [psb[ps_i]])
            kb.op("act", lambda e: e.activation(out=f.rstd[:], in_=pst[ps_i][:], func=AF.Sqrt,
                                                bias=epsb[:], scale=1.0 / D), [psb[ps_i], b_const], [f.b_rstd])
            kb.op("dve", lambda e: e.reciprocal(out=f.rstd[:], in_=f.rstd[:]), [f.b_rstd], [f.b_rstd])
            if l_next is None:
                return
            hi = f.n % 2
            f.n += 1
            for k in range(KT):
                ti = k % 2
                kb.op("dve", lambda e, k=k, ti=ti: e.scalar_tensor_tensor(
                    out=f.tmp[ti][:], in0=xn[:, k, :], scalar=acoef[:, l_next, sub_next, k, j:j + 1],
                    in1=f.rstd[:], op0=ALU.mult, op1=ALU.mult), [b_xn, f.b_rstd, b_mod], [f.b_tmp[ti]])
                kb.op("act", lambda e, k=k, ti=ti, hi=hi: e.activation(
                    out=f.h[hi][:, k, :], in_=f.tmp[ti][:], func=AF.Identity,
                    bias=mod_ap(l_next, 3 * sub_next, k, j), scale=1.0), [f.b_tmp[ti], b_mod], [f.b_h[hi]])
            kb.dma(q, HTv[:, :, cols], f.h[hi][:], [f.b_h[hi]], [], f.b_h[hi])

        with ExitStack() as ph:
            psb_ = lambda name, shape, dtype: ph.enter_context(nc.sbuf_tensor(uq(name), list(shape), dtype))
            fin = make_fin(ph)
            xtok = [psb_("xtok%d" % i, [128, 4, D], F32) for i in range(2)]
            b_xtok = [kb.buf("xtok%d" % i) for i in range(2)]
            xn = [psb_("xn%d" % i, [128, 8, TT], F32) for i in range(2)]
            b_xn = [kb.buf("xn%d" % i) for i in range(2)]
            for tt in range(NTT):
                i = tt % 2
                src = xp if tt < NPT else xs
                r0 = tt * TT if tt < NPT else (tt - NPT) * TT
                kb.dma("sp", xtok[i][:], src[r0:r0 + TT, :].rearrange("(a p) d -> p a d", p=128),
                       [], [b_xtok[i]], b_xtok[i])
                for k in range(KT):
                    pi = 1 + (k % 3)
                    for a in range(4):
                        kb.transpose(pst[pi][:, a * 128:(a + 1) * 128], xtok[i][:, a, k * 128:(k + 1) * 128],
                                     ident[:], [b_xtok[i], b_const], [psb[pi]])
                    if k % 2 == 0:
                        kb.op("act", lambda e, k=k, pi=pi, i=i: e.activation(out=xn[i][:, k, :], in_=pst[pi][:], func=AF.Copy),
                              [psb[pi]], [b_xn[i]])
                    else:
                        kb.op("dve", lambda e, k=k, pi=pi, i=i: e.tensor_copy(out=xn[i][:, k, :], in_=pst[pi][:]),
                              [psb[pi]], [b_xn[i]])
                finish(fin, tt, xn[i], b_xn[i], 0, 0, 4 + (tt % 2))
                for _ in range(3):
                    if _ADA_PH[0][1]:
                        _ADA_PH[0][1].pop(0)()
            while _ADA_PH[0][1]:
                _ADA_PH[0][1].pop(0)()
            kb.barrier()
        _ADA_PH[0][0].close()

        def load_h_all(ph, name="hall"):
            hall = ph.enter_context(nc.sbuf_tensor(uq(name), [128, 8, NT], BF16))
            b_hall = [kb.buf("%s%d" % (name, t)) for t in range(NTT)]
            for tt in range(NTT):
                cols = slice(tt * TT, (tt + 1) * TT)
                kb.dma("sp", hall[:, :, cols], HTv[:, :, cols], [], [b_hall[tt]], b_hall[tt])
            return hall, b_hall

        def ffn(l, l_next, sub_next):
            host_li = (l // 2) if (with_s5 and l % 2 == 0 and l + 1 < depth) else None
            with ExitStack() as pho:
                b_wo = kb.buf("wo")
                wov = w_ffn_out[l].rearrange("(k p) n -> p k n", p=128)
                wo = None
                if host_li is None:
                    wo = pho.enter_context(nc.sbuf_tensor(uq("wo"), [128, 22, D], BF16))
                with ExitStack() as ph:
                    psb_ = lambda name, shape, dtype: ph.enter_context(nc.sbuf_tensor(uq(name), list(shape), dtype))
                    hall, b_hall = load_h_all(ph)
                    NS = 3
                    wr = [psb_("wf%d" % i, [128, 8, 256], BF16) for i in range(NS)]
                    b_wr = [kb.buf("wf%d" % i) for i in range(NS)]
                    gs = [psb_("gs%d" % i, [128, TT], F32) for i in range(2)]
                    b_gs = [kb.buf("gs%d" % i) for i in range(2)]
                    ho = [psb_("ho%d" % i, [128, NT], BF16) for i in range(2)]
                    b_ho = [kb.buf("ho%d" % i) for i in range(2)]
                    tape = None
                    per_it = 0
                    if host_li is not None:
                        ph2 = ph.enter_context(ExitStack())
                        kb.tape = []
                        s5_setup(host_li, host=ph2)
                        tape = kb.tape
                        kb.tape = None
                        per_it = len(tape) // (22 * NTT - 40) + 1
                    nbank = 8 if host_li is None else 4
                    for m in range(22):
                        s = m % NS
                        kb.dma("pool", wr[s][:].rearrange("p k n -> p (k n)"), w_ffn_in[l, m], [], [b_wr[s]], b_wr[s], max_dma_last_dim=4096)
                        if host_li is None and 3 <= m < 14:
                            k0 = 2 * (m - 3)
                            kb.dma("pool", wo[:, k0:k0 + 2, :], wov[:, k0:k0 + 2, :], [], [b_wo], b_wo)
                        oi = m % 2
                        for tt in range(NTT):
                            if tape:
                                kb.replay(tape, per_it)
                            cols = slice(tt * TT, (tt + 1) * TT)
                            pg = (2 * tt) % nbank
                            pu = pg + 1
                            kb.mm(pst[pg][:], [(wr[s][:, k, 0:128], hall[:, k, cols]) for k in range(KT)],
                                  [b_wr[s], b_hall[tt]], [psb[pg]])
                            kb.mm(pst[pu][:], [(wr[s][:, k, 128:256], hall[:, k, cols]) for k in range(KT)],
                                  [b_wr[s], b_hall[tt]], [psb[pu]])
                            gi = tt % 2
                            kb.op("act", lambda e, pg=pg, gi=gi: e.activation(out=gs[gi][:], in_=pst[pg][:], func=AF.Silu),
                                  [psb[pg]], [b_gs[gi]])
                            kb.op("dve", lambda e, pu=pu, gi=gi, oi=oi, cols=cols: e.tensor_tensor(
                                out=ho[oi][:, cols], in0=pst[pu][:], in1=gs[gi][:], op=ALU.mult),
                                [psb[pu], b_gs[gi]], [b_ho[oi]])
                        kb.dma("sp", HIDv[:, m, :], ho[oi][:], [b_ho[oi]], [], b_ho[oi])
                    if tape:
                        kb.replay(tape, len(tape))
                    kb.barrier()
                with ExitStack() as ph:
                    psb_ = lambda name, shape, dtype: ph.enter_context(nc.sbuf_tensor(uq(name), list(shape), dtype))
                    fin = make_fin(ph, final=(l_next is None))
                    if wo is None:
                        wo = psb_("wo", [128, 22, D], BF16)
                        for k0 in range(0, 22, 2):
                            kb.dma("pool", wo[:, k0:k0 + 2, :], wov[:, k0:k0 + 2, :], [], [b_wo], b_wo)
                    hid = [psb_("hid%d" % i, [128, 22, TT], BF16) for i in range(2)]
                    b_hid = [kb.buf("hid%d" % i) for i in range(2)]
                    NBX = 3
                    xt_ = [psb_("xt%d" % i, [128, 8, TT], F32) for i in range(NBX)]
                    b_xt = [kb.buf("xt%d" % i) for i in range(NBX)]

                    def load(tt):
                        cols = slice(tt * TT, (tt + 1) * TT)
                        kb.dma("sp", hid[tt % 2][:], HIDv[:, :, cols], [], [b_hid[tt % 2]], b_hid[tt % 2])
                        kb.dma("sp", xt_[tt % NBX][:], XTv[:, :, cols], [], [b_xt[tt % NBX]], b_xt[tt % NBX])
                    load(0)
                    for tt in range(NTT):
                        i = tt % 2
                        ix = tt % NBX
                        j = 0 if tt < NPT else 1
                        if tt + 1 < NTT:
                            load(tt + 1)
                        for m in range(KT):
                            pi = m % 6
                            kb.mm(pst[pi][:], [(wo[:, k, m * 128:(m + 1) * 128], hid[i][:, k, :]) for k in range(22)],
                                  [b_wo, b_hid[i]], [psb[pi]])
                            kb.op("dve", lambda e, m=m, pi=pi, ix=ix, j=j: e.scalar_tensor_tensor(
                                out=xt_[ix][:, m, :], in0=pst[pi][:], scalar=mod_ap(l, 5, m, j), in1=xt_[ix][:, m, :],
                                op0=ALU.mult, op1=ALU.add), [psb[pi], b_xt[ix], b_mod], [b_xt[ix]])
                        if l_next is None:
                            final_out(fin, tt, xt_[ix], b_xt[ix], ph)
                        else:
                            finish(fin, tt, xt_[ix], b_xt[ix], l_next, sub_next, 6 + (tt % 2))
                    kb.barrier()

        fo = {}

        def final_out(fin, tt, xn, b_xn, ph):
            if "y" not in fo:
                fo["y"] = [ph.enter_context(nc.sbuf_tensor(uq("fo_y%d" % i), [128, 8, TT], F32)) for i in range(1)]
                fo["b_y"] = [kb.buf("fo_y%d" % i) for i in range(1)]
                fo["o"] = [ph.enter_context(nc.sbuf_tensor(uq("fo_o%d" % i), [128, 4, D], F32)) for i in range(1)]
                fo["b_o"] = [kb.buf("fo_o%d" % i) for i in range(1)]
            finish(fin, tt, xn, b_xn, None, None, 6 + (tt % 2))
            y = fo["y"][0]
            b_y = fo["b_y"][0]
            o = fo["o"][0]
            b_o = fo["b_o"][0]
            for k in range(KT):
                kb.op("dve", lambda e, k=k: e.scalar_tensor_tensor(
                    out=y[:, k, :], in0=xn[:, k, :], scalar=gft[:, k:k + 1], in1=fin.rstd[:],
                    op0=ALU.mult, op1=ALU.mult), [b_xn, fin.b_rstd, b_const], [b_y])
            for a in range(4):
                for k0 in range(0, KT, 4):
                    pi = (a * 2 + k0 // 4) % 6
                    for kk in range(4):
                        k = k0 + kk
                        kb.transpose(pst[pi][:, kk * 128:(kk + 1) * 128], y[:, k, a * 128:(a + 1) * 128], ident[:],
                                     [b_y, b_const], [psb[pi]])
                    if (a + k0 // 4) % 2 == 0:
                        kb.op("act", lambda e, a=a, k0=k0, pi=pi: e.activation(
                            out=o[:, a, k0 * 128:(k0 + 4) * 128], in_=pst[pi][:], func=AF.Copy), [psb[pi]], [b_o])
                    else:
                        kb.op("dve", lambda e, a=a, k0=k0, pi=pi: e.tensor_copy(
                            out=o[:, a, k0 * 128:(k0 + 4) * 128], in_=pst[pi][:]), [psb[pi]], [b_o])
            dst = yp if tt < NPT else ys
            r0 = tt * TT if tt < NPT else (tt - NPT) * TT
            kb.dma("sp", dst[r0:r0 + TT, :].rearrange("(a p) d -> p a d", p=128), o[:], [b_o], [], b_o)

        def even_mixer(l):
            i2 = l // 2
            with ExitStack() as ph:
                psb_ = lambda name, shape, dtype: ph.enter_context(nc.sbuf_tensor(uq(name), list(shape), dtype))
                fall = psb_("fall", [128, 4, NT], BF16)
                b_fall = [kb.buf("fall%d" % g) for g in range(4)]
                with ExitStack() as ph1:
                    p1 = lambda name, shape, dtype: ph1.enter_context(nc.sbuf_tensor(uq(name), list(shape), dtype))
                    hall, b_hall = load_h_all(ph1)
                    NS = 3
                    wr = [p1("wm%d" % i, [128, 8, 384], BF16) for i in range(NS)]
                    b_wr = [kb.buf("wm%d" % i) for i in range(NS)]
                    wc = p1("wc", [128, 4, 3], F32)
                    b_wc = kb.buf("wc")
                    with nc.allow_non_contiguous_dma(reason="tiny conv weight load"):
                        for t_ in range(3):
                            for c_ in range(4):
                                kb.dma("sp", wc[:, c_, t_:t_ + 1], w_conv[i2, t_, c_ * 128:(c_ + 1) * 128].rearrange("(p o) -> p o", o=1),
                                       [], [b_wc], b_wc)
                    vS = [p1("vS%d" % i, [128, TT], F32) for i in range(2)]
                    b_vS = [kb.buf("vS%d" % i) for i in range(2)]
                    tS = [p1("tS%d" % i, [128, TT], F32) for i in range(2)]
                    b_tS = [kb.buf("tS%d" % i) for i in range(2)]
                    cS = [p1("cS%d" % i, [128, TT], F32) for i in range(2)]
                    b_cS = [kb.buf("cS%d" % i) for i in range(2)]
                    yo = [p1("yo%d" % i, [128, NT], BF16) for i in range(2)]
                    b_yo = [kb.buf("yo%d" % i) for i in range(2)]
                    cnt = 0
                    for c in range(4):
                        s = c % NS
                        kb.dma("pool", wr[s][:].rearrange("p k n -> p (k n)"), w_in_mix[i2, c], [], [b_wr[s]], b_wr[s], max_dma_last_dim=4096)
                        oi = c % 2
                        for tt in range(NTT):
                            cols = slice(tt * TT, (tt + 1) * TT)
                            rl = 256 if tt < NPT else 64
                            nr = TT // rl
                            pa = (3 * cnt) % 6
                            cnt += 1
                            for part in range(3):
                                kb.mm(pst[pa + part][:], [(wr[s][:, k, part * 128:(part + 1) * 128], hall[:, k, cols]) for k in range(KT)],
                                      [b_wr[s], b_hall[tt]], [psb[pa + part]])
                            bi = tt % 2
                            kb.op("act", lambda e, pa=pa, bi=bi: e.activation(out=vS[bi][:], in_=pst[pa + 1][:], func=AF.Copy),
                                  [psb[pa + 1]], [b_vS[bi]])
                            kb.op("dve", lambda e, pa=pa, bi=bi: e.tensor_tensor(out=tS[bi][:], in0=pst[pa][:], in1=vS[bi][:], op=ALU.mult),
                                  [psb[pa], b_vS[bi]], [b_tS[bi]])
                            kb.op("act", lambda e, bi=bi, c=c: e.activation(out=cS[bi][:], in_=tS[bi][:], func=AF.Copy,
                                                                          scale=wc[:, c, 1:2]), [b_tS[bi], b_wc], [b_cS[bi]])
                            t3 = tS[bi][:].rearrange("p (r w) -> p r w", w=rl)
                            c3 = cS[bi][:].rearrange("p (r w) -> p r w", w=rl)
                            kb.op("dve", lambda e, t3=t3, c3=c3, c=c, rl=rl: e.scalar_tensor_tensor(
                                out=c3[:, :, 1:rl], in0=t3[:, :, 0:rl - 1], scalar=wc[:, c, 0:1], in1=c3[:, :, 1:rl],
                                op0=ALU.mult, op1=ALU.add), [b_tS[bi], b_cS[bi], b_wc], [b_cS[bi]])
                            kb.op("dve", lambda e, t3=t3, c3=c3, c=c, rl=rl: e.scalar_tensor_tensor(
                                out=c3[:, :, 0:rl - 1], in0=t3[:, :, 1:rl], scalar=wc[:, c, 2:3], in1=c3[:, :, 0:rl - 1],
                                op0=ALU.mult, op1=ALU.add), [b_tS[bi], b_cS[bi], b_wc], [b_cS[bi]])
                            kb.op("dve", lambda e, pa=pa, bi=bi, oi=oi, cols=cols: e.tensor_tensor(
                                out=yo[oi][:, cols], in0=pst[pa + 2][:], in1=cS[bi][:], op=ALU.mult),
                                [psb[pa + 2], b_cS[bi]], [b_yo[oi]])
                        kb.dma("sp", YCv[:, c, :], yo[oi][:], [b_yo[oi]], [], b_yo[oi])
                    for g in range(4):
                        s = (4 + g) % NS
                        kb.dma("pool", wr[s][:].rearrange("p k n -> p (k n)"), w_in_mix[i2, 4 + g], [], [b_wr[s]], b_wr[s], max_dma_last_dim=4096)
                        for tt in range(NTT):
                            cols = slice(tt * TT, (tt + 1) * TT)
                            pa = 6 + (tt % 2)
                            kb.mm(pst[pa][:], [(wr[s][:, k, 0:128], hall[:, k, cols]) for k in range(KT)],
                                  [b_wr[s], b_hall[tt]], [psb[pa]])
                            kb.op("act", lambda e, pa=pa, g=g, cols=cols: e.activation(out=fall[:, g, cols], in_=pst[pa][:], func=AF.Copy),
                                  [psb[pa]], [b_fall[g]])
                    kb.barrier()
                with ExitStack() as ph2:
                    p2 = lambda name, shape, dtype: ph2.enter_context(nc.sbuf_tensor(uq(name), list(shape), dtype))
                    cs128 = p2("cs128", [128, 256], BF16)
                    dftp = p2("dftp", [128, 2, 2, 256], BF16)
                    b_tab = kb.buf("ftab")
                    kb.dma("sp", cs128[:], cs128_d, [], [b_tab], b_tab)
                    for cs_ in range(2):
                        kb.dma("sp", dftp[:, cs_], dftp_d[cs_].rearrange("(a p) k -> p a k", p=128), [], [b_tab], b_tab)
                    ftok = p2("ftok", [128, 32, 4, 256], BF16)
                    b_ftok = [kb.buf("ftok%d" % t) for t in range(32)]
                    def chan_dft(lt0, n):
                        for sl in range(n):
                            lt = lt0 + sl
                            for gp in range(2):
                                pa = (lt * 2 + gp) % 4
                                for gg in range(2):
                                    g = gp * 2 + gg
                                    kb.mm(pst[pa][:, gg * 256:(gg + 1) * 256], [(fall[:, g, lt * 128:(lt + 1) * 128], cs128[:])],
                                          [b_fall[g], b_tab], [psb[pa]])
                                if gp == 0:
                                    kb.op("act", lambda e, pa=pa, sl=sl, gp=gp: e.activation(
                                        out=ftok[:, sl, 2 * gp:2 * gp + 2, :].rearrange("p g c -> p (g c)"), in_=pst[pa][:], func=AF.Copy),
                                        [psb[pa]], [b_ftok[sl]])
                                else:
                                    kb.op("dve", lambda e, pa=pa, sl=sl, gp=gp: e.tensor_copy(
                                        out=ftok[:, sl, 2 * gp:2 * gp + 2, :].rearrange("p g c -> p (g c)"), in_=pst[pa][:]),
                                        [psb[pa]], [b_ftok[sl]])
                    yf = [p2("yf%d" % i, [128, 4, 256], BF16) for i in range(2)]
                    b_yf = [kb.buf("yf%d" % i) for i in range(2)]
                    nyf = 0
                    sc_p = 1.0 / math.sqrt(256 * 128)
                    chan_dft(0, 8)
                    for sq in range(4):
                        yi = nyf % 2
                        nyf += 1
                        for g in range(4):
                            pa = 4 + (g % 4)
                            pairs = []
                            for a in range(2):
                                lt = sq * 2 + a
                                pairs.append((ftok[:, lt, g, 0:128], dftp[:, 0, a, :]))
                                pairs.append((ftok[:, lt, g, 128:256], dftp[:, 1, a, :]))
                            kb.mm(pst[pa][:, 0:256], pairs, [b_ftok[sq * 2], b_ftok[sq * 2 + 1], b_tab], [psb[pa]])
                            kb.op("act", lambda e, pa=pa, g=g, yi=yi: e.activation(out=yf[yi][:, g, :], in_=pst[pa][:, 0:256],
                                                                                func=AF.Copy, scale=sc_p), [psb[pa]], [b_yf[yi]])
                        kb.dma("sp", YCv[:, 4:8, sq * 256:(sq + 1) * 256], yf[yi][:], [b_yf[yi]], [], b_yf[yi])
                    sc_s = 1.0 / math.sqrt(4096 * 128)
                    chan_dft(8, 32)
                    tb = [p2("dft%d" % i, [128, 32, 512], BF16) for i in range(2)]
                    tbv = [t_[:] for t_ in tb] + [fall[:].rearrange("p g n -> p (g n)")[:, 0:16384].rearrange("p (a k) -> p a k", a=32)]
                    b_tb = [kb.buf("dft%d" % i) for i in range(3)]
                    yfs = [p2("yfs%d" % i, [128, 4, 512], BF16) for i in range(2)]
                    b_yfs = [kb.buf("yfs%d" % i) for i in range(2)]
                    nld = 0
                    for kbk in range(8):
                        pbase = 4 * (kbk % 2)
                        for cs_ in range(2):
                            ti = nld % 3
                            extra = b_fall if nld == 2 else []
                            nld += 1
                            tfl = tbv[ti].rearrange("p a k -> p (a k)")
                            for q_ in range(4):
                                kb.dma("sp", tfl[:, q_ * 4096:(q_ + 1) * 4096], dfts_d[kbk, cs_, :, q_ * 4096:(q_ + 1) * 4096],
                                       [], [b_tb[ti]] + extra, b_tb[ti])
                            for g in range(4):
                                pa = pbase + g
                                pairs = [(ftok[:, a, g, cs_ * 128:(cs_ + 1) * 128], tbv[ti][:, a, :]) for a in range(32)]
                                kb.mm(pst[pa][:], pairs, b_ftok[0:32] + [b_tb[ti]], [psb[pa]], start=(cs_ == 0), stop=(cs_ == 1))
                        yi = kbk % 2
                        for g in range(4):
                            pa = pbase + g
                            kb.op("act", lambda e, pa=pa, g=g, yi=yi: e.activation(out=yfs[yi][:, g, :], in_=pst[pa][:],
                                                                                func=AF.Copy, scale=sc_s), [psb[pa]], [b_yfs[yi]])
                        kb.dma("sp", YCv[:, 4:8, 1024 + kbk * 512:1024 + (kbk + 1) * 512], yfs[yi][:], [b_yfs[yi]], [], b_yfs[yi])
                    kb.barrier()
            out_proj(l, w_out_mix[i2], glu=False)

        def out_proj(l, w_ap, glu):
            ncol = 2 * D if glu else D
            with ExitStack() as ph:
                psb_ = lambda name, shape, dtype: ph.enter_context(nc.sbuf_tensor(uq(name), list(shape), dtype))
                fin = make_fin(ph)
                wo = psb_("wom", [128, 8, ncol], BF16)
                b_wo = kb.buf("wom")
                wov = w_ap.rearrange("(k p) n -> p k n", p=128)
                for k0 in range(0, 8, 2):
                    kb.dma("pool", wo[:, k0:k0 + 2, :], wov[:, k0:k0 + 2, :], [], [b_wo], b_wo)
                NB = 3
                yc = [psb_("yc%d" % i, [128, 8, TT], BF16) for i in range(NB)]
                b_yc = [kb.buf("yc%d" % i) for i in range(NB)]
                xt_ = [psb_("xt%d" % i, [128, 8, TT], F32) for i in range(NB)]
                b_xt = [kb.buf("xt%d" % i) for i in range(NB)]
                sg = [psb_("sg%d" % i, [128, TT], F32) for i in range(2)]
                b_sg = [kb.buf("sg%d" % i) for i in range(2)]

                def load(tt):
                    i = tt % NB
                    cols = slice(tt * TT, (tt + 1) * TT)
                    kb.dma("sp", yc[i][:], YCv[:, :, cols], [], [b_yc[i]], b_yc[i])
                    kb.dma("sp", xt_[i][:], XTv[:, :, cols], [], [b_xt[i]], b_xt[i])
                load(0)
                for tt in range(NTT):
                    i = tt % NB
                    cols = slice(tt * TT, (tt + 1) * TT)
                    j = 0 if tt < NPT else 1
                    if tt + 1 < NTT:
                        load(tt + 1)
                    for m in range(KT):
                        pi = (2 * m) % 6
                        kb.mm(pst[pi][:], [(wo[:, k, m * 128:(m + 1) * 128], yc[i][:, k, :]) for k in range(KT)],
                              [b_wo, b_yc[i]], [psb[pi]])
                        if glu:
                            kb.mm(pst[pi + 1][:], [(wo[:, k, D + m * 128:D + (m + 1) * 128], yc[i][:, k, :]) for k in range(KT)],
                                  [b_wo, b_yc[i]], [psb[pi + 1]])
                            si = m % 2
                            kb.op("act", lambda e, pi=pi, si=si: e.activation(out=sg[si][:], in_=pst[pi + 1][:], func=AF.Sigmoid),
                                  [psb[pi + 1]], [b_sg[si]])
                            kb.op("dve", lambda e, pi=pi, si=si: e.tensor_tensor(out=sg[si][:], in0=pst[pi][:], in1=sg[si][:], op=ALU.mult),
                                  [psb[pi], b_sg[si]], [b_sg[si]])
                            kb.op("dve", lambda e, m=m, si=si, i=i, j=j: e.scalar_tensor_tensor(
                                out=xt_[i][:, m, :], in0=sg[si][:], scalar=mod_ap(l, 2, m, j), in1=xt_[i][:, m, :],
                                op0=ALU.mult, op1=ALU.add), [b_sg[si], b_xt[i], b_mod], [b_xt[i]])
                        else:
                            kb.op("dve", lambda e, m=m, pi=pi, i=i, j=j: e.scalar_tensor_tensor(
                                out=xt_[i][:, m, :], in0=pst[pi][:], scalar=mod_ap(l, 2, m, j), in1=xt_[i][:, m, :],
                                op0=ALU.mult, op1=ALU.add), [psb[pi], b_xt[i], b_mod], [b_xt[i]])
                    finish(fin, tt, xt_[i], b_xt[i], l, 1, 6 + (tt % 2))
                kb.barrier()

        I32 = mybir.dt.int32
        r8t = sb("r8t", [128, 2, 64], F32)
        e1r = sb("e1r", [128, 2, 64], F32)
        e1i = sb("e1i", [128, 2, 64], F32)
        h0r = sb("h0r", [128, 2, 64], F32)
        h0i = sb("h0i", [128, 2, 64], F32)
        identb = sb("identb", [128, 128], BF16)
        b_s5p = kb.buf("s5p")
        kb.op("dve", lambda e: e.tensor_copy(out=identb[:], in_=ident[:]), [b_const], [b_s5p])

        def AP3(a, dims):
            return bass.AP(tensor=a.tensor, offset=a.offset, ap=[list(a.ap[0])] + [list(d_) for d_ in dims])

        def bc_last(a, n):
            return AP3(a, [a.ap[1], [0, n]])

        def bc_mid(a, n):
            return AP3(a, [[0, n], a.ap[1]])

        PI = float(np.pi)
        import os as _os
        _STOP = int(_os.environ.get('S5STOP', '0'))

        def s5_setup(li, host=None):
            own = ExitStack() if host is None else None
            try:
                _s5_body(li, own if host is None else host)
                if host is None:
                    kb.barrier()
            finally:
                if own is not None:
                    own.close()

        def _s5_body(li, ph):
            if True:
                def T(name, shape, dtype=F32):
                    return ph.enter_context(nc.sbuf_tensor(uq(name), list(shape), dtype))
                B = kb.buf("s5set")
                R, W = [B, b_const, b_s5p], [B]

                def dve(fn):
                    kb.op("dve", fn, R, W)

                def act(fn):
                    kb.op("act", fn, R, W)
                ps1 = pst[4]

                Bpr = T("Bpr", [128, 16, 16]); Bpi = T("Bpi", [128, 16, 16])
                b_in = kb.buf("s5in")
                stgA = T("stgA", [128, 64]); stgB = T("stgB", [128, 64]); stgH = T("stgH", [128, 128])
                cstA = [T("cst%d" % i_, [128, 2, 64]) for i_ in range(4)]
                stD = T("stD", [64, 16])
                msk = T("msk", [128, 2, 128])
                LS = T("LS", [128, 64])
                kb.dma("sp", stgA[:], lam_re[li].rearrange("d g p -> (d g) p"), [], [b_in], b_in)
                kb.dma("sp", stgB[:], lam_im[li].rearrange("d g p -> (d g) p"), [], [b_in], b_in)
                for d_ in range(2):
                    kb.dma("sp", LS[d_ * 64:(d_ + 1) * 64, :], log_step[li, d_, :].partition_broadcast(64), [], [b_in], b_in)
                kb.dma("sp", stgH[:], st_in[li].rearrange("d g p c -> (d g) (p c)"), [], [b_in], b_in)
                kb.dma("sp", stD[:], ssm_d[li].rearrange("(g h) -> g h", h=16), [], [b_in], b_in)
                for m_ in range(2):
                    kb.dma("sp", msk[:, m_, :], mask_d[m_], [], [b_in], b_in)
                b_inq = kb.buf("s5inq")

                def load_quarter(q_):
                    g0 = 16 * q_
                    for d_ in range(2):
                        sl = slice(d_ * 64, (d_ + 1) * 64)
                        for dst, src in ((Bpr, b_re), (Bpi, b_im)):
                            kb.dma("sp", dst[sl, :, :], src[li, d_, g0:g0 + 16].rearrange("g p h -> p g h"), [], [b_inq], b_inq)
                    for ci_, (src, d_) in enumerate(((c_re, 0), (c_re, 1), (c_im, 0), (c_im, 1))):
                        kb.dma("sp", cstA[ci_][:], src[li, d_].rearrange("(t g) h p -> (g h) t p", g=8)[:, 2 * q_:2 * q_ + 2, :],
                               [], [b_inq], b_inq)
                load_quarter(0)
                R = R + [b_in]

                def tr_halves(dst, stg_):
                    for d_ in range(2):
                        sl = slice(d_ * 64, (d_ + 1) * 64)
                        kb.mm(ps1[sl, 0:64], [(stg_[sl, 0:64], ident[sl, sl])], R, [psb[4]])
                    kb.op("dve", lambda e: e.tensor_copy(out=dst, in_=ps1[:, 0:64]), R + [psb[4]], W)
                LR = T("LR", [128, 64]); LI = T("LI", [128, 64])
                tr_halves(LR[:], stgA)
                tr_halves(LI[:], stgB)
                if _STOP == 1:
                    kb.barrier()
                    return

                stp = T("stp", [128, 64]); lr = T("lr", [128, 64]); Are = T("Are", [128, 64]); Aim = T("Aim", [128, 64])
                mag = T("mag", [128, 64]); kf = T("kf", [128, 64]); ki = T("ki", [128, 64], I32); red = T("red", [128, 64])
                sn = T("sn", [128, 64]); cs = T("cs", [128, 64]); P1r = T("P1r", [128, 64]); P1i = T("P1i", [128, 64])
                u1 = T("u1", [128, 64]); u2 = T("u2", [128, 64])
                act(lambda e: e.activation(out=stp[:], in_=LS[:], func=AF.Exp))
                dve(lambda e: e.tensor_scalar(out=lr[:], in0=LR[:], scalar1=-1e-4, scalar2=None, op0=ALU.min))
                dve(lambda e: e.tensor_tensor(out=Are[:], in0=lr[:], in1=stp[:], op=ALU.mult))
                dve(lambda e: e.tensor_tensor(out=Aim[:], in0=LI[:], in1=stp[:], op=ALU.mult))
                act(lambda e: e.activation(out=mag[:], in_=Are[:], func=AF.Exp))

                def wrap(x):
                    dve(lambda e: e.tensor_scalar(out=kf[:], in0=x, scalar1=PI, scalar2=None, op0=ALU.is_gt))
                    dve(lambda e: e.scalar_tensor_tensor(out=x, in0=kf[:], scalar=-2 * PI, in1=x, op0=ALU.mult, op1=ALU.add))
                    dve(lambda e: e.tensor_scalar(out=kf[:], in0=x, scalar1=-PI, scalar2=None, op0=ALU.is_lt))
                    dve(lambda e: e.scalar_tensor_tensor(out=x, in0=kf[:], scalar=2 * PI, in1=x, op0=ALU.mult, op1=ALU.add))
                dve(lambda e: e.tensor_scalar(out=kf[:], in0=Aim[:], scalar1=1.0 / (2 * PI), scalar2=None, op0=ALU.mult))
                dve(lambda e: e.tensor_copy(out=ki[:], in_=kf[:]))
                dve(lambda e: e.tensor_copy(out=u1[:], in_=ki[:]))
                dve(lambda e: e.scalar_tensor_tensor(out=red[:], in0=u1[:], scalar=-2 * PI, in1=Aim[:], op0=ALU.mult, op1=ALU.add))
                wrap(red[:])
                wrap(red[:])
                act(lambda e: e.activation(out=sn[:], in_=red[:], func=AF.Sin))
                dve(lambda e: e.tensor_scalar(out=red[:], in0=red[:], scalar1=PI / 2, scalar2=None, op0=ALU.add))
                wrap(red[:])
                act(lambda e: e.activation(out=cs[:], in_=red[:], func=AF.Sin))
                dve(lambda e: e.tensor_tensor(out=P1r[:], in0=mag[:], in1=cs[:], op=ALU.mult))
                dve(lambda e: e.tensor_tensor(out=P1i[:], in0=mag[:], in1=sn[:], op=ALU.mult))
                if _STOP == 2:
                    kb.barrier()
                    return


                def cmul(eng, outr, outi, ar, ai, br, bi, t1, t2, neg_i=False, R_=None, W_=None):
                    o = lambda fn: kb.op(eng, fn, R if R_ is None else R_, W if W_ is None else W_)
                    o(lambda e: e.tensor_tensor(out=t1, in0=ar, in1=br, op=ALU.mult))
                    o(lambda e: e.tensor_tensor(out=t2, in0=ai, in1=bi, op=ALU.mult))
                    o(lambda e: e.tensor_tensor(out=outr, in0=t1, in1=t2, op=ALU.subtract))
                    o(lambda e: e.tensor_tensor(out=t1, in0=ar, in1=bi, op=ALU.mult))
                    o(lambda e: e.tensor_tensor(out=t2, in0=ai, in1=br, op=ALU.mult))
                    if neg_i:
                        o(lambda e: e.scalar_tensor_tensor(out=outi, in0=t1, scalar=-1.0, in1=t2, op0=ALU.mult, op1=ALU.subtract))
                    else:
                        o(lambda e: e.tensor_tensor(out=outi, in0=t1, in1=t2, op=ALU.add))
                den = T("den", [128, 64]); cr = T("cr", [128, 64]); ci = T("ci", [128, 64]); pm1 = T("pm1", [128, 64])
                dve(lambda e: e.tensor_tensor(out=u1[:], in0=lr[:], in1=lr[:], op=ALU.mult))
                dve(lambda e: e.tensor_tensor(out=u2[:], in0=LI[:], in1=LI[:], op=ALU.mult))
                dve(lambda e: e.tensor_tensor(out=den[:], in0=u1[:], in1=u2[:], op=ALU.add))
                dve(lambda e: e.reciprocal(out=den[:], in_=den[:]))
                dve(lambda e: e.tensor_scalar(out=pm1[:], in0=P1r[:], scalar1=-1.0, scalar2=None, op0=ALU.add))
                dve(lambda e: e.tensor_tensor(out=u1[:], in0=pm1[:], in1=lr[:], op=ALU.mult))
                dve(lambda e: e.tensor_tensor(out=u2[:], in0=P1i[:], in1=LI[:], op=ALU.mult))
                dve(lambda e: e.tensor_tensor(out=cr[:], in0=u1[:], in1=u2[:], op=ALU.add))
                dve(lambda e: e.tensor_tensor(out=cr[:], in0=cr[:], in1=den[:], op=ALU.mult))
                dve(lambda e: e.tensor_tensor(out=u1[:], in0=P1i[:], in1=lr[:], op=ALU.mult))
                dve(lambda e: e.tensor_tensor(out=u2[:], in0=pm1[:], in1=LI[:], op=ALU.mult))
                dve(lambda e: e.tensor_tensor(out=ci[:], in0=u1[:], in1=u2[:], op=ALU.subtract))
                dve(lambda e: e.tensor_tensor(out=ci[:], in0=ci[:], in1=den[:], op=ALU.mult))
                ivr = T("ivr", [128, 64]); ivi = T("ivi", [128, 64]); im2 = T("im2", [128, 64])
                act(lambda e: e.activation(out=im2[:], in_=Are[:], func=AF.Exp, scale=-2.0))
                dve(lambda e: e.tensor_tensor(out=ivr[:], in0=P1r[:], in1=im2[:], op=ALU.mult))
                dve(lambda e: e.scalar_tensor_tensor(out=ivi[:], in0=P1i[:], scalar=-1.0, in1=im2[:], op0=ALU.mult, op1=ALU.mult))
                M1r = T("M1r", [128, 64]); M1i = T("M1i", [128, 64]); Mvr = T("Mvr", [128, 64]); Mvi = T("Mvi", [128, 64])
                fh, bh = slice(0, 64), slice(64, 128)
                for dst, sf, sb_ in ((M1r, ivr, P1r), (M1i, ivi, P1i), (Mvr, P1r, ivr), (Mvi, P1i, ivi)):
                    dve(lambda e, dst=dst, sf=sf: e.tensor_copy(out=dst[fh, :], in_=sf[fh, :]))
                    dve(lambda e, dst=dst, sb_=sb_: e.tensor_copy(out=dst[bh, :], in_=sb_[bh, :]))
                Rr = T("Rr", [128, 9, 64]); Ri = T("Ri", [128, 9, 64]); Tr = T("Tr", [128, 9, 64]); Ti = T("Ti", [128, 9, 64])
                for X, v in ((Rr, 1.0), (Ri, 0.0), (Tr, 1.0), (Ti, 0.0)):
                    dve(lambda e, X=X, v=v: e.memset(X[:, 0, :], v))
                for j in range(8):
                    cmul("dve", Rr[:, j + 1, :], Ri[:, j + 1, :], Rr[:, j, :], Ri[:, j, :], M1r[:], M1i[:], u1[:], u2[:])
                    cmul("dve", Tr[:, j + 1, :], Ti[:, j + 1, :], Tr[:, j, :], Ti[:, j, :], Mvr[:], Mvi[:], u1[:], u2[:])
                A7r = T("A7r", [128, 64]); A7i = T("A7i", [128, 64]); C7r = T("C7r", [128, 64]); C7i = T("C7i", [128, 64])
                Cpr = T("Cpr", [128, 64]); Cpi = T("Cpi", [128, 64]); L8r = T("L8r", [128, 64]); L8i = T("L8i", [128, 64])
                for X, v in ((A7r, 1.0), (A7i, 0.0), (C7r, 1.0), (C7i, 0.0)):
                    dve(lambda e, X=X, v=v: e.memset(X[:], v))
                for dst, src, hs in ((A7r, Tr[:, 7, :], fh), (A7i, Ti[:, 7, :], fh), (C7r, Rr[:, 7, :], fh), (C7i, Ri[:, 7, :], fh),
                                     (Cpr, Tr[:, 1, :], fh), (Cpi, Ti[:, 1, :], fh), (Cpr, Rr[:, 8, :], bh), (Cpi, Ri[:, 8, :], bh),
                                     (L8r, Tr[:, 8, :], fh), (L8i, Ti[:, 8, :], fh), (L8r, Rr[:, 8, :], bh), (L8i, Ri[:, 8, :], bh)):
                    dve(lambda e, dst=dst, src=src, hs=hs: e.tensor_copy(out=dst[hs, :], in_=src[hs, :]))
                act(lambda e: e.activation(out=r8t[:, li, :], in_=Are[:], func=AF.Exp, scale=8.0))
                act(lambda e: e.activation(out=im2[:], in_=Are[:], func=AF.Exp, scale=-8.0))
                dve(lambda e: e.tensor_tensor(out=e1r[:, li, :], in0=L8r[:], in1=im2[:], op=ALU.mult))
                dve(lambda e: e.scalar_tensor_tensor(out=e1i[:, li, :], in0=L8i[:], scalar=-1.0, in1=im2[:], op0=ALU.mult, op1=ALU.mult))
                for d_ in range(2):
                    sl = slice(d_ * 64, (d_ + 1) * 64)
                    for c_ in range(2):
                        kb.mm(ps1[sl, c_ * 64:(c_ + 1) * 64], [(stgH[sl, c_:128:2], ident[sl, sl])], R, [psb[4]])
                dve(lambda e: e.tensor_copy(out=h0r[:, li, :], in_=ps1[:, 0:64]))
                dve(lambda e: e.tensor_copy(out=h0i[:, li, :], in_=ps1[:, 64:128]))
                if _STOP == 3:
                    kb.barrier()
                    return

                stD2 = T("stD2", [64, 8, 16]); D8 = T("D8", [128, 64])
                dve(lambda e: e.tensor_copy(out=stD2[:], in_=bc_mid(stD[:], 8)))
                kb.mm(ps1[:, 0:64], [(stD2[:].rearrange("p j h -> p (j h)"), ident[0:64, 0:64])], R, [psb[4]])
                dve(lambda e: e.tensor_copy(out=D8[:], in_=ps1[:, 0:64]))
                if _STOP == 4:
                    kb.barrier()
                    return

                Cpr_ = T("Cpr_", [128, 16, 16]); Cpi_ = T("Cpi_", [128, 16, 16])
                w1 = T("w1", [128, 16, 16]); w2 = T("w2", [128, 16, 16]); Xr = T("Xr", [128, 16, 16]); Xi = T("Xi", [128, 16, 16])
                Yr = T("Yr", [128, 16, 16]); Yi = T("Yi", [128, 16, 16])
                w3 = T("w3", [128, 16, 16]); w4 = T("w4", [128, 16, 16])
                BFr = T("BFr", [128, 16, 8, 16], BF16); BFi = T("BFi", [128, 16, 8, 16], BF16)
                CMr = T("CMr", [128, 16, 8, 16], BF16); CMn = T("CMn", [128, 16, 8, 16], BF16)
                CFr = T("CFr", [128, 16, 8, 16], BF16); CFn = T("CFn", [128, 16, 8, 16], BF16)
                zt = T("zt", [128, 16, 128], BF16)
                dve(lambda e: e.memset(zt[:], 0.0))
                st3 = [T("st3_%d" % i_, [128, 3, 8, 128], BF16) for i_ in range(2)]
                b_st3 = [kb.buf("st3_%d" % i_) for i_ in range(2)]
                k1 = [T("k1_%d" % i_, [128, 128]) for i_ in range(2)]
                k2 = [T("k2_%d" % i_, [128, 128]) for i_ in range(2)]
                b_k = [kb.buf("k_%d" % i_) for i_ in range(2)]
                cmz = [T("cmz%d" % i_, [128, 4, 128], BF16) for i_ in range(2)]
                b_cmz = [kb.buf("cmz%d" % i_) for i_ in range(2)]
                for i_ in range(2):
                    kb.op("dve", lambda e, i_=i_: e.memset(cmz[i_][:], 0.0), [], [b_cmz[i_]])
                b_bf = kb.buf("s5bf")
                b_cm = kb.buf("s5cm")
                tblv = TBL[li]
                Rq = R + [b_inq]
                for q_ in range(4):
                    gs = slice(16 * q_, 16 * q_ + 16)
                    if q_ > 0:
                        load_quarter(q_)
                    for ri_, dst in enumerate((Cpr_, Cpi_)):
                        for d_ in range(2):
                            sl = slice(d_ * 64, (d_ + 1) * 64)
                            cst = cstA[ri_ * 2 + d_]
                            for tt_ in range(2):
                                kb.mm(big[3][sl, tt_ * 128:(tt_ + 1) * 128], [(cst[:, tt_, :], ident[:])], Rq, [psb[6]])
                            kb.op("dve", lambda e, dst=dst, sl=sl: e.tensor_copy(
                                out=dst[sl, :, :].rearrange("p g h -> p (g h)"), in_=big[3][sl, 0:256]), Rq + [psb[6]], W)
                    bq = lambda t_: bc_last(t_[:, gs], 16)
                    bj = lambda t_, j: bc_last(t_[:, j, gs], 16)
                    cmul("dve", Xr[:], Xi[:], bq(cr), bq(ci), Bpr[:], Bpi[:], w1[:], w2[:], R_=Rq + [b_bf, b_cm], W_=W + [b_inq])
                    cmul("dve", Bpr[:], Bpi[:], bq(A7r), bq(A7i), Xr[:], Xi[:], w1[:], w2[:], R_=Rq + [b_bf, b_cm], W_=W + [b_inq])
                    cmul("dve", Xr[:], Xi[:], bq(C7r), bq(C7i), Cpr_[:], Cpi_[:], w1[:], w2[:], R_=Rq + [b_bf, b_cm], W_=W)
                    cmul("dve", Yr[:], Yi[:], bq(Cpr), bq(Cpi), Cpr_[:], Cpi_[:], w1[:], w2[:], R_=Rq + [b_bf, b_cm], W_=W)
                    for j in range(8):
                        cmul("pool", BFr[:, :, j, :], BFi[:, :, j, :], bj(Rr, j), bj(Ri, j), Bpr[:], Bpi[:], w3[:], w4[:],
                             R_=Rq + [b_bf] + b_cmz + b_st3, W_=[b_bf])
                        cmul("dve", CMr[:, :, j, :], CMn[:, :, j, :], bj(Tr, j), bj(Ti, j), Xr[:], Xi[:], w1[:], w2[:], neg_i=True,
                             R_=Rq + [b_cm] + b_cmz, W_=[b_cm])
                        cmul("dve", CFr[:, :, j, :], CFn[:, :, j, :], bj(Tr, j), bj(Ti, j), Yr[:], Yi[:], w1[:], w2[:], neg_i=True,
                             R_=Rq + [b_cm] + b_cmz, W_=[b_cm])
                    RO = Rq + [b_bf, b_cm]
                    for slot, X, hs, ho_ in ((3, CFr, slice(0, 64), slice(64, 128)), (4, CFn, slice(0, 64), slice(64, 128)),
                                             (5, CFr, slice(64, 128), slice(0, 64)), (6, CFn, slice(64, 128), slice(0, 64))):
                        tv = tblv[:, slot].rearrange("g p n -> p g n")
                        kb.dma("sp", tv[hs, gs, :], X[hs].rearrange("p g j h -> p g (j h)"), RO, [], b_cm)
                        kb.dma("sp", tv[ho_, gs, :], zt[ho_], RO, [], b_cm)
                    for gl in range(16):
                        g = 16 * q_ + gl
                        s_ = g % 2
                        x_ = (g // 8) % 2
                        sti = st3[x_]
                        gi = g % 8
                        fl = lambda X: X[:, gl].rearrange("p j h -> p (j h)")
                        pT, pK = psb[4 + s_], psb[6 + s_]
                        cT, cK = 512 * s_, 512 * s_
                        kb.mm(big[2][:, cT:cT + 128], [(fl(BFr), identb[:])], RO, [pT])
                        kb.mm(big[2][:, cT + 128:cT + 256], [(fl(BFi), identb[:])], RO, [pT])
                        kb.op("act", lambda e, sti=sti, gi=gi, cT=cT: e.activation(
                            out=sti[:, 0:2, gi, :], in_=big[2][:, cT:cT + 256].rearrange("p (t n) -> p t n", t=2), func=AF.Copy), [pT], [b_st3[x_]])
                        for d_ in range(2):
                            sl = slice(d_ * 64, (d_ + 1) * 64)
                            kb.op("act", lambda e, sl=sl, d_=d_, gl=gl, s_=s_: e.activation(
                                out=cmz[s_][sl, 2 * d_, :], in_=CMr[sl, gl].rearrange("p j h -> p (j h)"), func=AF.Copy), RO + [b_cmz[s_]], [b_cmz[s_]])
                            kb.op("act", lambda e, sl=sl, d_=d_, gl=gl, s_=s_: e.activation(
                                out=cmz[s_][sl, 2 * d_ + 1, :], in_=CMn[sl, gl].rearrange("p j h -> p (j h)"), func=AF.Copy), RO + [b_cmz[s_]], [b_cmz[s_]])
                        for d_ in range(2):
                            kb.mm(big[3][:, cK + d_ * 128:cK + (d_ + 1) * 128],
                                  [(fl(BFr), cmz[s_][:, 2 * d_, :]), (fl(BFi), cmz[s_][:, 2 * d_ + 1, :])], RO + [b_cmz[s_]], [pK])
                        kb.op("dve", lambda e, s_=s_, cK=cK: e.tensor_tensor(out=k1[s_][:], in0=big[3][:, cK:cK + 128], in1=msk[:, 0, :], op=ALU.mult),
                              RO + [pK, b_k[s_]], [b_k[s_]])
                        kb.op("dve", lambda e, s_=s_, cK=cK: e.tensor_tensor(out=k2[s_][:], in0=big[3][:, cK + 128:cK + 256], in1=msk[:, 1, :], op=ALU.mult),
                              RO + [pK, b_k[s_]], [b_k[s_]])
                        kb.op("dve", lambda e, s_=s_: e.tensor_tensor(out=k1[s_][:], in0=k1[s_][:], in1=k2[s_][:], op=ALU.add), [b_k[s_]], [b_k[s_]])
                        kb.op("dve", lambda e, g=g, sti=sti, gi=gi, s_=s_: e.scalar_tensor_tensor(
                            out=sti[:, 2, gi, :], in0=ident[:], scalar=D8[:, g:g + 1], in1=k1[s_][:], op0=ALU.mult, op1=ALU.add),
                            RO + [b_k[s_], b_st3[x_]], [b_st3[x_]])
                        if gi == 7:
                            g0 = g - 7
                            for t_ in range(3):
                                kb.dma("sp", tblv[g0:g0 + 8, t_].rearrange("g p n -> p g n"), sti[:, t_], [b_st3[x_]], [], b_st3[x_])

        def odd_mixer(l):
            li = l // 2
            with ExitStack() as phO:
                u8 = phO.enter_context(nc.sbuf_tensor(uq("u8all"), [128, 64, 640], BF16))
                b_u8 = [kb.buf("u8_%d" % g) for g in range(64)]
                finS = phO.enter_context(nc.sbuf_tensor(uq("finS"), [128, 64, 4, 2], F32))
                b_fin = kb.buf("finS")
                with ExitStack() as ph:
                    hall, b_hall = load_h_all(ph)
                    selS = ph.enter_context(nc.sbuf_tensor(uq("selS"), [128, 64, 128], BF16))
                    b_sel = kb.buf("selS")
                    kb.dma("sp", selS[:], sel_d.rearrange("a j r c -> r (a j) c"), [], [b_sel], b_sel)
                    for g in range(64):
                        ft, a = g // 8, g % 8
                        bk = big[g % 4]
                        pbs = [psb[2 * (g % 4)], psb[2 * (g % 4) + 1]]
                        kb.mm(bk[:, 0:512], [(selS[:, a * 8 + j, :], hall[:, ft, 1024 + j:5120:8]) for j in range(8)],
                              [b_sel] + b_hall[2:], pbs)
                        kb.mm(bk[:, 512:640], [(selS[:, a * 8 + j, :], hall[:, ft, j:1024:8]) for j in range(8)],
                              [b_sel] + b_hall[0:2], pbs)
                        if g % 2 == 0:
                            kb.op("act", lambda e, g=g, bk=bk: e.activation(out=u8[:, g, :], in_=bk[:, 0:640], func=AF.Copy), pbs, [b_u8[g]])
                        else:
                            kb.op("dve", lambda e, g=g, bk=bk: e.tensor_copy(out=u8[:, g, :], in_=bk[:, 0:640]), pbs, [b_u8[g]])
                    kb.barrier()
                with ExitStack() as ph:
                    def T(name, shape, dtype=F32):
                        return ph.enter_context(nc.sbuf_tensor(uq(name), list(shape), dtype))
                    GB = 2
                    SB = 16
                    TWr = [T("TWr%d" % i_, [128, GB, 640]) for i_ in range(2)]
                    TWi = [T("TWi%d" % i_, [128, GB, 640]) for i_ in range(2)]
                    dec = [T("dec%d" % i_, [128, GB, 640]) for i_ in range(2)]
                    b_TW = [kb.buf("TW%d" % i_) for i_ in range(2)]
                    b_dec = [kb.buf("dec%d" % i_) for i_ in range(2)]
                    F0r = T("F0r", [128, SB, 32]); F0i = T("F0i", [128, SB, 32])
                    F1r = T("F1r", [128, SB, 16]); F1i = T("F1i", [128, SB, 16])
                    b_F = kb.buf("F")
                    fq1 = T("fq1", [128, SB, 16]); fq2 = T("fq2", [128, SB, 16])
                    q1 = T("q1", [128, GB, 512]); q2 = T("q2", [128, GB, 512])
                    b_q = kb.buf("q")
                    mrow = T("mrow", [128, 640])
                    b_mrow = kb.buf("mrow")
                    tb = [T("tb%d" % i_, [128, 7, 128], BF16) for i_ in range(2)]
                    b_tb = [kb.buf("tb%d" % i_) for i_ in range(2)]
                    zin_r = T("zinr", [128, 640]); zin_i = T("zini", [128, 640])
                    b_zin = kb.buf("zin")
                    _zr = T("zr", [128, 640]); _zi = T("zi", [128, 640])
                    z_r = [_zr, _zr]
                    z_i = [_zi, _zi]
                    _bz = kb.buf("z")
                    b_z = [_bz, _bz]
                    S_r = [T("Sr%d" % i_, [128, 640]) for i_ in range(2)]
                    S_i = [T("Si%d" % i_, [128, 640]) for i_ in range(2)]
                    b_S = [kb.buf("S%d" % i_) for i_ in range(2)]
                    car_r = [T("carr%d" % i_, [128, 640], BF16) for i_ in range(2)]
                    car_i = [T("cari%d" % i_, [128, 640], BF16) for i_ in range(2)]
                    b_car = [kb.buf("car%d" % i_) for i_ in range(2)]
                    t2 = T("t2", [128, 640])
                    t1s = T("t1s", [128, 640])
                    t1p = t1s[:]
                    b_t12 = kb.buf("t12")
                    b_t34 = [kb.buf("t34_%d" % i_) for i_ in range(2)]
                    for i_ in range(2):
                        kb.op("dve", lambda e, i_=i_: e.memset(car_r[i_][:], 0.0), [], [b_car[i_]])
                        kb.op("dve", lambda e, i_=i_: e.memset(car_i[i_][:], 0.0), [], [b_car[i_]])
                    kb.op("dve", lambda e: e.memset(F1r[:, :, 0:1], 1.0), [], [b_F])
                    kb.op("dve", lambda e: e.memset(F1i[:, :, 0:1], 0.0), [], [b_F])
                    kb.op("dve", lambda e: e.memset(mrow[:], 1.0), [], [b_mrow])
                    for b_ in range(4):
                        kb.op("dve", lambda e, b_=b_: e.memset(mrow[:, 512 + 32 * b_:513 + 32 * b_], 0.0), [b_mrow], [b_mrow])
                    selT = T("selT", [128, 64, 128], BF16)
                    b_selT = kb.buf("selT")
                    kb.dma("sp", selT[:], selT_d.rearrange("a j r c -> r (a j) c"), [], [b_selT], b_selT)
                    zT = [T("zT%d" % i_, [128, NT], BF16) for i_ in range(2)]
                    b_zT = [kb.buf("zT%d" % i_) for i_ in range(2)]
                    o3_q = []

                    def o3_step(ft, j):
                        zi_ = ft % 2
                        bk = big[3]
                        pbs = [psb[6], psb[7]]
                        gs = [ft * 8 + a_ for a_ in range(8)]
                        kb.mm(bk[:, 0:512], [(selT[:, a_ * 8 + j, :], u8[:, ft * 8 + a_, 0:512]) for a_ in range(8)],
                              [b_selT] + [b_u8[g_] for g_ in gs], pbs)
                        kb.mm(bk[:, 512:640], [(selT[:, a_ * 8 + j, :], u8[:, ft * 8 + a_, 512:640]) for a_ in range(8)],
                              [b_selT] + [b_u8[g_] for g_ in gs], pbs)
                        kb.op("act", lambda e: e.activation(out=zT[zi_][:, 1024 + j:5120:8], in_=bk[:, 0:512], func=AF.Copy), pbs, [b_zT[zi_]])
                        kb.op("act", lambda e: e.activation(out=zT[zi_][:, j:1024:8], in_=bk[:, 512:640], func=AF.Copy), pbs, [b_zT[zi_]])
                        if j == 7:
                            kb.dma("sp", YCv[:, ft, :], zT[zi_][:], [b_zT[zi_]], [], b_zT[zi_])
                    tblv = TBL[li]
                    SLr, SLi = big[0], big[1]
                    p_slr, p_sli = [psb[0], psb[1]], [psb[2], psb[3]]

                    def cdbl(Xr, Xi, m, n):
                        ar = Xr[:, :, m - 1:m].to_broadcast([128, SB, n])
                        ai = Xi[:, :, m - 1:m].to_broadcast([128, SB, n])
                        br, bi2 = Xr[:, :, 0:n], Xi[:, :, 0:n]
                        qa, qb = fq1[:, :, 0:n], fq2[:, :, 0:n]
                        o = lambda fn: kb.op("dve", fn, [b_F], [b_F])
                        o(lambda e: e.tensor_tensor(out=qa, in0=br, in1=ar, op=ALU.mult))
                        o(lambda e: e.tensor_tensor(out=qb, in0=bi2, in1=ai, op=ALU.mult))
                        o(lambda e: e.tensor_tensor(out=Xr[:, :, m:m + n], in0=qa, in1=qb, op=ALU.subtract))
                        o(lambda e: e.tensor_tensor(out=qa, in0=br, in1=ai, op=ALU.mult))
                        o(lambda e: e.tensor_tensor(out=qb, in0=bi2, in1=ar, op=ALU.mult))
                        o(lambda e: e.tensor_tensor(out=Xi[:, :, m:m + n], in0=qa, in1=qb, op=ALU.add))

                    def b4(a_, kind):
                        if kind == "a":
                            return AP3(a_, [a_.ap[1], a_.ap[2], [0, 32]])
                        return AP3(a_, [a_.ap[1], [0, 16], a_.ap[2]])

                    for g in range(64):
                        if g % SB == 0:
                            kb.op("dve", lambda e, g=g: e.tensor_copy(out=F0r[:, :, 0], in_=e1r[:, li, g:g + SB]), [b_s5p, b_F], [b_F])
                            kb.op("dve", lambda e, g=g: e.tensor_copy(out=F0i[:, :, 0], in_=e1i[:, li, g:g + SB]), [b_s5p, b_F], [b_F])
                            for m in (1, 2, 4, 8, 16):
                                cdbl(F0r, F0i, m, m)
                            kb.op("dve", lambda e: e.tensor_copy(out=F1r[:, :, 1], in_=F0r[:, :, 31]), [b_F], [b_F])
                            kb.op("dve", lambda e: e.tensor_copy(out=F1i[:, :, 1], in_=F0i[:, :, 31]), [b_F], [b_F])
                            H_r, H_i = F1r[:, :, 1:16], F1i[:, :, 1:16]
                            for m, n in ((1, 1), (2, 2), (4, 4), (8, 7)):
                                cdbl(H_r, H_i, m, n)
                        gb, gi = g // GB, g % GB
                        bi_ = gb % 2
                        if gi == 0:
                            gs_ = g % SB
                            Wt, Rt = [b_TW[bi_], b_q], [b_TW[bi_], b_q, b_F]
                            po = lambda fn: kb.op("pool", fn, Rt, Wt)
                            v4_ = lambda X: X.rearrange("p g (a b) -> p g a b", b=32)
                            Or, Oi = v4_(TWr[bi_][:, :, 0:512]), v4_(TWi[bi_][:, :, 0:512])
                            Q1, Q2 = v4_(q1[:]), v4_(q2[:])
                            x0r, x0i = F0r[:, gs_:gs_ + GB, :], F0i[:, gs_:gs_ + GB, :]
                            A_r, A_i = b4(F1r[:, gs_:gs_ + GB, :], "a"), b4(F1i[:, gs_:gs_ + GB, :], "a")
                            B_r, B_i = b4(x0r, "b"), b4(x0i, "b")
                            po(lambda e: e.tensor_tensor(out=Q1, in0=A_r, in1=B_r, op=ALU.mult))
                            po(lambda e: e.tensor_tensor(out=Q2, in0=A_i, in1=B_i, op=ALU.mult))
                            po(lambda e: e.tensor_tensor(out=Or, in0=Q1, in1=Q2, op=ALU.subtract))
                            po(lambda e: e.tensor_tensor(out=Q1, in0=A_r, in1=B_i, op=ALU.mult))
                            po(lambda e: e.tensor_tensor(out=Q2, in0=A_i, in1=B_r, op=ALU.mult))
                            po(lambda e: e.tensor_tensor(out=Oi, in0=Q1, in1=Q2, op=ALU.add))
                            for b_ in range(4):
                                po(lambda e, b_=b_: e.tensor_copy(out=TWr[bi_][:, :, 512 + 32 * b_:544 + 32 * b_], in_=x0r))
                                po(lambda e, b_=b_: e.tensor_copy(out=TWi[bi_][:, :, 512 + 32 * b_:544 + 32 * b_], in_=x0i))
                            r8v = r8t[:, li, g:g + GB]
                            kb.op("dve", lambda e, r8v=r8v, bi_=bi_: e.tensor_tensor(
                                out=dec[bi_][:], in0=AP3(mrow[:], [[0, GB], mrow[:].ap[1]]), in1=AP3(r8v, [r8v.ap[1], [0, 640]]), op=ALU.mult),
                                [b_mrow, b_s5p, b_dec[bi_]], [b_dec[bi_]])
                        si = g % 2
                        Er, Ei = TWr[bi_][:, gi, :], TWi[bi_][:, gi, :]

                        def local_sums(g_):
                            s_ = g_ % 2
                            kb.dma("sp", tb[s_][:], tblv[g_].rearrange("t p n -> p t n"), [], [b_tb[s_]], b_tb[s_])
                            for SL, pb, slot in ((SLr, p_slr, 0), (SLi, p_sli, 1)):
                                kb.mm(SL[0:64, 0:512], [(tb[s_][:, slot, 0:64], u8[:, g_, 0:512])], [b_tb[s_], b_u8[g_]], pb)
                                kb.mm(SL[64:128, 0:512], [(tb[s_][:, slot, 64:128], u8[:, g_, 511::-1])], [b_tb[s_], b_u8[g_]], pb)
                                kb.mm(SL[0:64, 512:640], [(tb[s_][:, slot, 0:64], u8[:, g_, 512:640])], [b_tb[s_], b_u8[g_]], pb)
                                kb.mm(SL[64:128, 512:640], [(tb[s_][:, slot, 64:128], u8[:, g_, 639:511:-1])], [b_tb[s_], b_u8[g_]], pb)
                        if g == 0:
                            local_sums(0)
                        Rd = p_slr + p_sli + [b_TW[bi_], b_t12]
                        dv = lambda fn, Wd: kb.op("dve", fn, Rd + Wd, Wd)
                        dv(lambda e, Er=Er: e.tensor_tensor(out=t1p, in0=SLr[:, 0:640], in1=Er, op=ALU.mult), [b_t12])
                        dv(lambda e, Ei=Ei: e.tensor_tensor(out=t2[:], in0=SLi[:, 0:640], in1=Ei, op=ALU.mult), [b_t12])
                        dv(lambda e: e.tensor_tensor(out=zin_r[:], in0=t1p, in1=t2[:], op=ALU.subtract), [b_zin])
                        dv(lambda e, Er=Er: e.tensor_tensor(out=t1p, in0=SLi[:, 0:640], in1=Er, op=ALU.mult), [b_t12])
                        dv(lambda e, Ei=Ei: e.tensor_tensor(out=t2[:], in0=SLr[:, 0:640], in1=Ei, op=ALU.mult), [b_t12])
                        dv(lambda e: e.tensor_tensor(out=zin_i[:], in0=t1p, in1=t2[:], op=ALU.add), [b_zin])
                        if g + 1 < 64:
                            local_sums(g + 1)
                        for zin, z, h0 in ((zin_r, z_r, h0r), (zin_i, z_i, h0i)):
                            kb.op("dve", lambda e, zin=zin, z=z, h0=h0, g=g, si=si, gi=gi, bi_=bi_: e.tensor_tensor_scan(
                                out=z[si][:], data0=dec[bi_][:, gi, :], data1=zin[:],
                                initial=h0[:, li, g:g + 1], op0=ALU.mult, op1=ALU.add), [b_zin, b_s5p, b_z[si], b_dec[bi_]], [b_z[si]])
                        Rr_ = [b_z[si], b_TW[bi_], b_t12]
                        kb.op("dve", lambda e, Er=Er, si=si: e.tensor_tensor(out=t1p, in0=z_r[si][:], in1=Er, op=ALU.mult), Rr_, [b_t12])
                        kb.op("dve", lambda e, Ei=Ei, si=si: e.tensor_tensor(out=t2[:], in0=z_i[si][:], in1=Ei, op=ALU.mult), Rr_, [b_t12])
                        kb.op("dve", lambda e, si=si: e.tensor_tensor(out=S_r[si][:], in0=t1p, in1=t2[:], op=ALU.add), [b_t12, b_S[si]], [b_S[si]])
                        kb.op("dve", lambda e, Er=Er, si=si: e.tensor_tensor(out=t1p, in0=z_i[si][:], in1=Er, op=ALU.mult), Rr_, [b_t12])
                        kb.op("dve", lambda e, Ei=Ei, si=si: e.tensor_tensor(out=t2[:], in0=z_r[si][:], in1=Ei, op=ALU.mult), Rr_, [b_t12])
                        kb.op("dve", lambda e, si=si: e.tensor_tensor(out=S_i[si][:], in0=t1p, in1=t2[:], op=ALU.subtract), [b_t12, b_S[si]], [b_S[si]])
                        v4 = lambda X: X.rearrange("p (b c) -> p b c", c=32)
                        for part, (S_, car, h0) in enumerate(((S_r, car_r, h0r), (S_i, car_i, h0i))):
                            kb.op("act", lambda e, S_=S_, car=car, si=si: e.activation(out=car[si][:, 1:512], in_=S_[si][:, 0:511], func=AF.Copy),
                                  [b_S[si], b_car[si]], [b_car[si]])
                            kb.op("act", lambda e, S_=S_, car=car, si=si: e.activation(
                                out=v4(car[si][:, 512:640])[:, :, 1:32], in_=v4(S_[si][:, 512:640])[:, :, 0:31], func=AF.Copy),
                                [b_S[si], b_car[si]], [b_car[si]])
                            kb.op("act", lambda e, S_=S_, g=g, part=part, si=si: e.activation(out=finS[:, g, :, part], in_=S_[si][:, 543:640:32], func=AF.Copy),
                                  [b_S[si], b_fin], [b_fin])
                            kb.op("act", lambda e, g=g, si=si, car=car, h0=h0: e.activation(out=car[si][:, 0:1], in_=h0[:, li, g:g + 1], func=AF.Copy),
                                  [b_s5p, b_car[si]], [b_car[si]])
                        Y = big[2]
                        pby = [psb[4], psb[5]]
                        kb.mm(Y[:, 0:512], [(tb[si][:, 2, :], u8[:, g, 0:512]),
                                            (tb[si][:, 3, :], car_r[si][:, 0:512]), (tb[si][:, 4, :], car_i[si][:, 0:512]),
                                            (tb[si][:, 5, :], car_r[si][:, 511::-1]), (tb[si][:, 6, :], car_i[si][:, 511::-1])],
                              [b_tb[si], b_u8[g], b_car[si]], pby)
                        kb.mm(Y[:, 512:640], [(tb[si][:, 2, :], u8[:, g, 512:640]),
                                              (tb[si][:, 3, :], car_r[si][:, 512:640]), (tb[si][:, 4, :], car_i[si][:, 512:640]),
                                              (tb[si][:, 5, :], car_r[si][:, 639:511:-1]), (tb[si][:, 6, :], car_i[si][:, 639:511:-1])],
                              [b_tb[si], b_u8[g], b_car[si]], pby)
                        kb.op("act", lambda e, g=g, Y=Y: e.activation(out=u8[:, g, :], in_=Y[:, 0:640], func=AF.Gelu), pby, [b_u8[g]])
                        if g % 8 == 7:
                            for j_ in range(8):
                                o3_q.append((g // 8, j_))
                        if o3_q:
                            o3_step(*o3_q.pop(0))
                    while o3_q:
                        o3_step(*o3_q.pop(0))
                    kb.barrier()
                with ExitStack() as ph:
                    outS = ph.enter_context(nc.sbuf_tensor(uq("outS"), [128, 4, 64, 2], F32))
                    b_out = kb.buf("outS")
                    n_ = 0
                    for s_ in range(4):
                        for c_ in range(2):
                            pi = n_ % 8
                            n_ += 1
                            for d_ in range(2):
                                sl = slice(d_ * 64, (d_ + 1) * 64)
                                b_ = s_ if d_ == 0 else 3 - s_
                                kb.mm(pst[pi][sl, 0:64], [(finS[sl, :, b_, c_], ident[sl, sl])], [b_fin, b_const], [psb[pi]])
                            kb.op("dve", lambda e, pi=pi, s_=s_, c_=c_: e.tensor_copy(out=outS[:, s_, :, c_], in_=pst[pi][:, 0:64]),
                                  [psb[pi]], [b_out])
                        kb.dma("sp", ns[s_, li].rearrange("d g p c -> (d g) (p c)"), outS[:, s_].rearrange("q p c -> q (p c)"), [b_out], [], b_out)
                    kb.barrier()
            out_proj(l, w_glu[li], glu=True)

        if mode == "s5setup":
            s5_setup(0)
            b_dbg = kb.buf("dbg")
            kb.dma("sp", dbg_tbl, TBL[0], [], [], b_dbg)
            for i_, t_ in enumerate((r8t, e1r, e1i, h0r, h0i)):
                kb.dma("sp", dbg_small[:, i_], t_[:], [b_s5p], [], b_dbg)
            kb.barrier()
            return nc
        for l in range(depth):
            if l % 2 == 0:
                even_mixer(l)
            else:
                if with_s5:
                    odd_mixer(l)
            last = (l == depth - 1)
            ffn(l, None if last else l + 1, 0)

        kb.barrier()
    return nc


_CONST = {}


def _consts():
    if _CONST:
        return _CONST
    bf = ml_dtypes.bfloat16
    _CONST["c_ident"] = np.eye(128, dtype=np.float32)
    _CONST["c_ones"] = np.ones((128, 128), dtype=bf)
    c = np.arange(128)[:, None] * np.arange(128)[None, :]
    ang = 2 * np.pi * (c % 128) / 128.0
    _CONST["c_cs128"] = np.concatenate([np.cos(ang), np.sin(ang)], axis=1).astype(bf)

    def dft(L):
        lk = (np.arange(L, dtype=np.int64)[:, None] * np.arange(L, dtype=np.int64)[None, :]) % L
        a = 2 * np.pi * lk / float(L)
        return np.stack([np.cos(a), -np.sin(a)]).astype(bf)
    _CONST["c_dftp"] = dft(256)
    t = dft(4096).reshape(2, 32, 128, 8, 512)
    _CONST["c_dfts"] = np.ascontiguousarray(t.transpose(3, 0, 2, 1, 4)).reshape(8, 2, 128, 32 * 512)
    sel = np.zeros((8, 8, 128, 128), np.float32)
    for a in range(8):
        for j in range(8):
            for h in range(16):
                sel[a, j, a * 16 + h, j * 16 + h] = 1.0
    _CONST["c_sel"] = sel.astype(bf)
    _CONST["c_selT"] = np.ascontiguousarray(sel.transpose(0, 1, 3, 2)).astype(bf)
    jj = np.arange(128) // 16
    mf = (jj[None, :] >= jj[:, None]).astype(np.float32)
    mb = (jj[None, :] <= jj[:, None]).astype(np.float32)
    _CONST["c_mask"] = np.stack([mf, mb])
    return _CONST


_NC_CACHE = {}


def kernel(x_prompt, x_sample, state_ssm, c, c_ctx, w_ada, b_ada, norm1_g, norm2_g, final_g,
           w_in_mix, w_conv, w_out_mix, ssm_lambda_re, ssm_lambda_im, ssm_log_step,
           ssm_b_re, ssm_b_im, ssm_c_re, ssm_c_im, ssm_d, w_glu, w_ffn_in, w_ffn_out,
           _depth=4, _with_s5=True, _mode=None, _ncores=8):
    f = lambda a: np.ascontiguousarray(np.asarray(a, dtype=np.float32))
    key = (_depth, _with_s5, _mode)
    if key not in _NC_CACHE:
        _NC_CACHE[key] = build(_depth, _with_s5, _mode)
    nc = _NC_CACHE[key]
    shared = dict(w_ada=f(w_ada), b_ada=f(b_ada), norm1_g=f(norm1_g), norm2_g=f(norm2_g), final_g=f(final_g),
                  w_conv=f(w_conv), w_out_mix=f(w_out_mix),
                  ssm_lambda_re=f(ssm_lambda_re), ssm_lambda_im=f(ssm_lambda_im), ssm_log_step=f(ssm_log_step),
                  ssm_b_re=f(ssm_b_re), ssm_b_im=f(ssm_b_im), ssm_c_re=f(ssm_c_re), ssm_c_im=f(ssm_c_im),
                  ssm_d=f(ssm_d), w_glu=f(w_glu), w_ffn_out=f(w_ffn_out))
    wfi = f(w_ffn_in).reshape(4, 8, 128, 2, 22, 128)
    shared["w_ffn_in"] = np.ascontiguousarray(wfi.transpose(0, 4, 2, 1, 3, 5)).reshape(4, 22, 128, 8 * 256)
    wim = f(w_in_mix).reshape(2, 8, 128, 4, 4, 128)
    cv = wim[:, :, :, [0, 2, 1]]
    cv = np.ascontiguousarray(cv.transpose(0, 4, 2, 1, 3, 5)).reshape(2, 4, 128, 8 * 384)
    fv = np.zeros((2, 4, 128, 8, 384), np.float32)
    fv[:, :, :, :, 0:128] = wim[:, :, :, 3].transpose(0, 3, 2, 1, 4)
    shared["w_in_mix"] = np.ascontiguousarray(np.concatenate([cv, fv.reshape(2, 4, 128, 8 * 384)], axis=1))
    shared.update(_consts())
    xpn = f(x_prompt)
    xsn = f(x_sample)
    stn = f(state_ssm)
    cn = f(c)
    cc = f(c_ctx)
    in_maps = []
    for i in range(8):
        m = dict(shared)
        m["xp"] = xpn[4 * i:4 * i + 4].reshape(1024, D)
        m["xs"] = xsn[i]
        m["st"] = stn[i]
        m["cond"] = np.stack([cc, cn[i]])
        in_maps.append(m)
    if _mode is not None:
        res = run_bass_kernel_spmd(nc, in_maps[:_ncores], core_ids=list(range(_ncores)))
        return res.results
    res = run_bass_kernel_spmd(nc, in_maps, core_ids=list(range(8)))
    r = res.results
    y_prompt = np.concatenate([r[i]["yp"].reshape(4, 256, D) for i in range(8)], axis=0)
    y_sample = np.stack([r[i]["ys"] for i in range(8)], axis=0)
    new_state = np.concatenate([r[i]["ns"] for i in range(8)], axis=0)
    return (y_prompt.astype(np.float32), y_sample.astype(np.float32), new_state.astype(np.float32))
```
